# Optimizing a Trainium2 kernel written in Bass

```python
import math
import jax
import jax.numpy as jnp
from jax import lax
import numpy as np


D_MODEL = 4096
BATCH = 2
SEQ = 8192
DEPTH = 1

HEAD_DIM = 128
NSA_HEADS = 16
NSA_KV_HEADS = 4
NSA_GROUP = NSA_HEADS // NSA_KV_HEADS
CMP_LEN = 32
CMP_STRIDE = 16
CMP_HIDDEN = 4 * HEAD_DIM
SEL_BLOCK = 64
SEL_TOPK = 16
WINDOW = 512
NSA_Q_BLOCK = 64
GDN_HEADS = 16
GDN_HEAD_DIM = 128
GDN_CONV = 4
GDN_CHUNK = 64
MLP_HIDDEN = 4 * D_MODEL
PLE_DIM = 256
ROPE_THETA = 10000.0
NORM_EPS = 1e-6
NEG_INF = -1e30

NSA_WIDTH = NSA_HEADS * HEAD_DIM
NSA_KV_WIDTH = NSA_KV_HEADS * HEAD_DIM
GDN_WIDTH = GDN_HEADS * GDN_HEAD_DIM
IN_SPLITS = (NSA_WIDTH, 6 * NSA_KV_WIDTH, 3 * NSA_HEADS, 3 * GDN_WIDTH, GDN_WIDTH, GDN_HEADS, GDN_HEADS, 2 * D_MODEL)
IN_WIDTH = sum(IN_SPLITS)

kernel_name = 'hybrid_nsa_gdn_gated_merge_block'


def rms_norm(x, gain):
    xf = x.astype(jnp.float32)
    y = xf * lax.rsqrt(jnp.mean(xf * xf, axis=-1, keepdims=True) + NORM_EPS)
    return (y * gain.astype(jnp.float32)).astype(x.dtype)


def l2_norm(x):
    xf = x.astype(jnp.float32)
    return xf * lax.rsqrt(jnp.sum(xf * xf, axis=-1, keepdims=True) + NORM_EPS)


def rope(x, pos):
    half = x.shape[-1] // 2
    inv_freq = ROPE_THETA ** (-jnp.arange(half, dtype=jnp.float32) / half)
    ang = pos[:, None] * inv_freq[None, :]
    cos, sin = jnp.cos(ang), jnp.sin(ang)
    xf = x.astype(jnp.float32)
    x1, x2 = xf[..., :half], xf[..., half:]
    return jnp.concatenate([x1 * cos - x2 * sin, x2 * cos + x1 * sin], axis=-1).astype(x.dtype)


def masked_softmax(s, mask):
    s = jnp.where(mask, s.astype(jnp.float32), NEG_INF)
    e = jnp.where(mask, jnp.exp(s - jnp.max(s, axis=-1, keepdims=True)), 0.0)
    return e / jnp.maximum(jnp.sum(e, axis=-1, keepdims=True), 1e-30)


def compress_blocks(t, pe, w1, w2):
    b, g, s, d = t.shape
    ch = t.reshape(b, g, s // CMP_STRIDE, CMP_STRIDE, d)
    blocks = jnp.concatenate([ch[:, :, :-1], ch[:, :, 1:]], axis=3) + pe
    hid = jax.nn.gelu(jnp.einsum('bgnld,ldh->bgnh', blocks, w1))
    return jnp.einsum('bgnh,hd->bgnd', hid, w2)


def nsa_attention(q_in, kv_in, gate_in, q_gain, kc_gain, ks_gain, kw_gain, pe_k, pe_v, wk1, wk2, wv1, wv2):
    b, s = q_in.shape[0], q_in.shape[1]
    g, r, d = NSA_KV_HEADS, NSA_GROUP, HEAD_DIM
    qbl = NSA_Q_BLOCK
    n_cmp = s // CMP_STRIDE - 1
    n_sel = s // SEL_BLOCK
    n_blk = s // qbl
    top_k = min(SEL_TOPK, n_sel)
    scale = d ** -0.5
    pos = jnp.arange(s, dtype=jnp.float32)

    q = rms_norm(q_in.reshape(b, s, g, r, d), q_gain).transpose(0, 2, 3, 1, 4)
    q = rope(q, pos)
    kv = kv_in.reshape(b, s, 6, g, d).transpose(2, 0, 3, 1, 4)
    k_c, v_c, k_s, v_s, k_w, v_w = kv[0], kv[1], kv[2], kv[3], kv[4], kv[5]

    cmp_end = jnp.arange(n_cmp) * CMP_STRIDE + (CMP_LEN - 1)
    k_cmp = rope(rms_norm(compress_blocks(k_c, pe_k, wk1, wk2), kc_gain), cmp_end.astype(jnp.float32))
    v_cmp = compress_blocks(v_c, pe_v, wv1, wv2)
    k_slc = rope(rms_norm(k_s, ks_gain), pos).reshape(b, g, n_sel, SEL_BLOCK, d)
    v_slc = v_s.reshape(b, g, n_sel, SEL_BLOCK, d)
    pad = ((0, 0), (0, 0), (WINDOW, 0), (0, 0))
    k_win = jnp.pad(rope(rms_norm(k_w, kw_gain), pos), pad)
    v_win = jnp.pad(v_w, pad)
    gates = jax.nn.sigmoid(gate_in.astype(jnp.float32)).reshape(b, s, g, r, 3).transpose(0, 2, 3, 1, 4)

    c_start = jnp.arange(n_cmp) * CMP_STRIDE
    s_start = jnp.arange(n_sel) * SEL_BLOCK
    overlap = ((c_start[:, None] < s_start[None, :] + SEL_BLOCK) & (c_start[:, None] + CMP_LEN > s_start[None, :])).astype(jnp.float32)
    b_ix = jnp.arange(b)[:, None, None, None]
    g_ix = jnp.arange(g)[None, :, None, None]
    j_blk = jnp.arange(n_sel)

    def block(n):
        t0 = n * qbl
        tq = t0 + jnp.arange(qbl)
        qb = lax.dynamic_slice_in_dim(q, t0, qbl, axis=3)
        s_c = jnp.einsum('bgrqd,bgnd->bgrqn', qb, k_cmp) * scale
        p_c = masked_softmax(s_c, cmp_end[None, :] <= tq[:, None])
        o_c = jnp.einsum('bgrqn,bgnd->bgrqd', p_c, v_cmp.astype(jnp.float32))
        imp = jnp.einsum('bgrqn,nj->bgqj', p_c, overlap)
        cur = tq // SEL_BLOCK
        valid = j_blk[None, :] <= cur[:, None]
        forced = (j_blk[None, :] == 0) | (j_blk[None, :] == cur[:, None]) | (j_blk[None, :] == cur[:, None] - 1)
        score = jnp.where(forced, jnp.inf, jnp.where(valid, imp, -jnp.inf))
        vals, idx = lax.top_k(score, top_k)
        kg = k_slc[b_ix, g_ix, idx]
        vg = v_slc[b_ix, g_ix, idx]
        key_pos = idx[..., None] * SEL_BLOCK + jnp.arange(SEL_BLOCK)
        m_s = (vals > -jnp.inf)[..., None] & (key_pos <= tq[:, None, None])
        s_s = jnp.einsum('bgrqd,bgqkld->bgrqkl', qb, kg) * scale
        n_key = top_k * SEL_BLOCK
        p_s = masked_softmax(s_s.reshape(b, g, r, qbl, n_key), m_s.reshape(b, g, 1, qbl, n_key))
        o_s = jnp.einsum('bgrqm,bgqmd->bgrqd', p_s, vg.reshape(b, g, qbl, n_key, d).astype(jnp.float32))
        kw = lax.dynamic_slice_in_dim(k_win, t0, WINDOW + qbl, axis=2)
        vw = lax.dynamic_slice_in_dim(v_win, t0, WINDOW + qbl, axis=2)
        kpos = t0 - WINDOW + jnp.arange(WINDOW + qbl)
        dist = tq[:, None] - kpos[None, :]
        m_w = (kpos[None, :] >= 0) & (dist >= 0) & (dist < WINDOW)
        s_w = jnp.einsum('bgrqd,bgkd->bgrqk', qb, kw) * scale
        p_w = masked_softmax(s_w, m_w)
        o_w = jnp.einsum('bgrqk,bgkd->bgrqd', p_w, vw.astype(jnp.float32))
        gb = lax.dynamic_slice_in_dim(gates, t0, qbl, axis=3)
        return gb[..., 0:1] * o_c + gb[..., 1:2] * o_s + gb[..., 2:3] * o_w

    out = lax.map(block, jnp.arange(n_blk))
    return out.transpose(1, 0, 4, 2, 3, 5).reshape(b, s, g * r * d)


def gated_deltanet(qkv_in, z_in, beta_in, decay_in, conv_w, a_log, dt_bias, o_gain):
    b, s = qkv_in.shape[0], qkv_in.shape[1]
    h, d, c = GDN_HEADS, GDN_HEAD_DIM, GDN_CHUNK
    n_ch = s // c
    n_feat = 3 * h * d
    conv = lax.conv_general_dilated(qkv_in, conv_w[:, None, :].astype(qkv_in.dtype), window_strides=(1,),
                                    padding=((GDN_CONV - 1, 0),), dimension_numbers=('NWC', 'WIO', 'NWC'),
                                    feature_group_count=n_feat)
    qkv = jax.nn.silu(conv.astype(jnp.float32)).reshape(b, s, 3, h, d).transpose(2, 0, 3, 1, 4)
    q = l2_norm(qkv[0]) * (d ** -0.5)
    k = l2_norm(qkv[1])
    v = qkv[2]
    beta = jax.nn.sigmoid(beta_in.astype(jnp.float32)).transpose(0, 2, 1)
    gdec = -(jnp.exp(a_log.astype(jnp.float32)) * jax.nn.softplus(decay_in.astype(jnp.float32) + dt_bias.astype(jnp.float32)))
    gdec = gdec.transpose(0, 2, 1)

    q = q.reshape(b, h, n_ch, c, d)
    k = k.reshape(b, h, n_ch, c, d)
    v = v.reshape(b, h, n_ch, c, d)
    beta = beta.reshape(b, h, n_ch, c)
    gc = jnp.cumsum(gdec.reshape(b, h, n_ch, c), axis=-1)
    tril = jnp.tril(jnp.ones((c, c), dtype=bool))
    strict = jnp.tril(jnp.ones((c, c), dtype=bool), -1)
    gamma = jnp.exp(jnp.where(tril, gc[..., :, None] - gc[..., None, :], NEG_INF))
    kb = k * beta[..., None]
    a_mat = jnp.where(strict, jnp.einsum('bhncd,bhnsd->bhncs', kb, k) * gamma, 0.0)
    eye = jnp.eye(c, dtype=jnp.float32)
    t_inv = lax.linalg.triangular_solve(eye + a_mat, jnp.broadcast_to(eye, a_mat.shape), left_side=True, lower=True, unit_diagonal=True)
    u = jnp.einsum('bhncs,bhnsd->bhncd', t_inv, v * beta[..., None])
    w = jnp.einsum('bhncs,bhnsd->bhncd', t_inv, kb * jnp.exp(gc)[..., None])
    attn = jnp.einsum('bhncd,bhnsd->bhncs', q, k) * gamma

    def step(state, inp):
        q_c, k_c, u_c, w_c, g_c, a_c = inp
        v_new = u_c - jnp.einsum('bhck,bhkv->bhcv', w_c, state)
        o_c = jnp.einsum('bhck,bhkv->bhcv', q_c * jnp.exp(g_c)[..., None], state) + jnp.einsum('bhcs,bhsv->bhcv', a_c, v_new)
        g_last = g_c[..., -1:]
        state = state * jnp.exp(g_last)[..., None] + jnp.einsum('bhck,bhcv->bhkv', k_c * jnp.exp(g_last - g_c)[..., None], v_new)
        return state, o_c

    xs = (jnp.moveaxis(q, 2, 0), jnp.moveaxis(k, 2, 0), jnp.moveaxis(u, 2, 0), jnp.moveaxis(w, 2, 0),
          jnp.moveaxis(gc, 2, 0), jnp.moveaxis(attn, 2, 0))
    state0 = jnp.zeros((b, h, d, d), dtype=jnp.float32)
    _, o = lax.scan(step, state0, xs)
    o = o.transpose(1, 2, 0, 3, 4).reshape(b, h, s, d).transpose(0, 2, 1, 3)
    z = z_in.reshape(b, s, h, d).astype(jnp.float32)
    o = rms_norm(o, o_gain) * jax.nn.silu(z)
    return o.reshape(b, s, h * d)


def setup_inputs(seed: int = 0) -> dict:
    key = jax.random.key(seed)
    keys = jax.random.split(key, 40)
    counter = [0]
    f32 = jnp.float32
    L = DEPTH

    def nxt():
        k = keys[counter[0]]
        counter[0] += 1
        return k

    def nrm(shape, scale):
        return jax.random.normal(nxt(), shape, f32) * scale

    def gain(shape):
        return 1.0 + 0.02 * jax.random.normal(nxt(), shape, f32)

    x = nrm((BATCH, SEQ, D_MODEL), 1.0)
    p = nrm((L, BATCH, SEQ, PLE_DIM), 1.0)
    g_mix = gain((L, D_MODEL))
    w_in = nrm((L, D_MODEL, IN_WIDTH), D_MODEL ** -0.5)
    nsa_q_gain = gain((L, HEAD_DIM))
    nsa_kc_gain = gain((L, HEAD_DIM))
    nsa_ks_gain = gain((L, HEAD_DIM))
    nsa_kw_gain = gain((L, HEAD_DIM))
    cmp_pe_k = nrm((L, CMP_LEN, HEAD_DIM), 0.02)
    cmp_pe_v = nrm((L, CMP_LEN, HEAD_DIM), 0.02)
    cmp_wk1 = nrm((L, CMP_LEN, HEAD_DIM, CMP_HIDDEN), (CMP_LEN * HEAD_DIM) ** -0.5)
    cmp_wk2 = nrm((L, CMP_HIDDEN, HEAD_DIM), CMP_HIDDEN ** -0.5)
    cmp_wv1 = nrm((L, CMP_LEN, HEAD_DIM, CMP_HIDDEN), (CMP_LEN * HEAD_DIM) ** -0.5)
    cmp_wv2 = nrm((L, CMP_HIDDEN, HEAD_DIM), CMP_HIDDEN ** -0.5)
    gdn_conv_w = nrm((L, GDN_CONV, 3 * GDN_WIDTH), GDN_CONV ** -0.5)
    gdn_a_log = jnp.log(jax.random.uniform(nxt(), (L, GDN_HEADS), f32, 1.0, 16.0))
    dt = jnp.exp(jax.random.uniform(nxt(), (L, GDN_HEADS), f32, math.log(1e-3), math.log(1e-1)))
    gdn_dt_bias = dt + jnp.log(-jnp.expm1(-dt))
    gdn_o_gain = gain((L, GDN_HEAD_DIM))
    w_up_nsa = nrm((L, NSA_WIDTH, D_MODEL), NSA_WIDTH ** -0.5)
    w_up_gdn = nrm((L, GDN_WIDTH, D_MODEL), GDN_WIDTH ** -0.5)
    w_out = nrm((L, D_MODEL, D_MODEL), D_MODEL ** -0.5)
    g_mlp = gain((L, D_MODEL))
    w_mlp_in = nrm((L, D_MODEL, MLP_HIDDEN), D_MODEL ** -0.5)
    w_mlp_out = nrm((L, MLP_HIDDEN, D_MODEL), MLP_HIDDEN ** -0.5)
    g_ple = gain((L, D_MODEL))
    w_ple_gate = nrm((L, D_MODEL, D_MODEL), D_MODEL ** -0.5)
    w_ple_proj = nrm((L, PLE_DIM, D_MODEL), PLE_DIM ** -0.5)
    return {'x': x, 'p': p, 'g_mix': g_mix, 'w_in': w_in,
            'nsa_q_gain': nsa_q_gain, 'nsa_kc_gain': nsa_kc_gain, 'nsa_ks_gain': nsa_ks_gain, 'nsa_kw_gain': nsa_kw_gain,
            'cmp_pe_k': cmp_pe_k, 'cmp_pe_v': cmp_pe_v, 'cmp_wk1': cmp_wk1, 'cmp_wk2': cmp_wk2,
            'cmp_wv1': cmp_wv1, 'cmp_wv2': cmp_wv2,
            'gdn_conv_w': gdn_conv_w, 'gdn_a_log': gdn_a_log, 'gdn_dt_bias': gdn_dt_bias, 'gdn_o_gain': gdn_o_gain,
            'w_up_nsa': w_up_nsa, 'w_up_gdn': w_up_gdn, 'w_out': w_out,
            'g_mlp': g_mlp, 'w_mlp_in': w_mlp_in, 'w_mlp_out': w_mlp_out,
            'g_ple': g_ple, 'w_ple_gate': w_ple_gate, 'w_ple_proj': w_ple_proj}


def reference(x, p, g_mix, w_in, nsa_q_gain, nsa_kc_gain, nsa_ks_gain, nsa_kw_gain, cmp_pe_k, cmp_pe_v,
              cmp_wk1, cmp_wk2, cmp_wv1, cmp_wv2, gdn_conv_w, gdn_a_log, gdn_dt_bias, gdn_o_gain,
              w_up_nsa, w_up_gdn, w_out, g_mlp, w_mlp_in, w_mlp_out, g_ple, w_ple_gate, w_ple_proj):
    offsets = np.cumsum(IN_SPLITS)[:-1].tolist()
    for i in range(DEPTH):
        h = rms_norm(x, g_mix[i])
        proj = h @ w_in[i]
        q_a, kv_a, gl_a, qkv_b, z_b, beta_b, decay_b, merge = jnp.split(proj, offsets, axis=-1)
        o_a = nsa_attention(q_a, kv_a, gl_a, nsa_q_gain[i], nsa_kc_gain[i], nsa_ks_gain[i], nsa_kw_gain[i],
                            cmp_pe_k[i], cmp_pe_v[i], cmp_wk1[i], cmp_wk2[i], cmp_wv1[i], cmp_wv2[i]).astype(x.dtype)
        o_b = gated_deltanet(qkv_b, z_b, beta_b, decay_b, gdn_conv_w[i], gdn_a_log[i], gdn_dt_bias[i],
                             gdn_o_gain[i]).astype(x.dtype)
        gate_a, gate_b = jnp.split(merge, 2, axis=-1)
        mixed = jax.nn.sigmoid(gate_a) * (o_a @ w_up_nsa[i]) + jax.nn.sigmoid(gate_b) * (o_b @ w_up_gdn[i])
        x = x + mixed @ w_out[i]
        hm = rms_norm(x, g_mlp[i])
        x = x + jnp.square(jax.nn.relu(hm @ w_mlp_in[i])) @ w_mlp_out[i]
        hp = rms_norm(x, g_ple[i])
        x = x + jax.nn.sigmoid(hp @ w_ple_gate[i]) * (p[i] @ w_ple_proj[i])
    return x
```

```python
from contextlib import ExitStack
import numpy as np
import concourse.bass as bass
import concourse.mybir as mybir
from concourse.bass_utils import run_bass_kernel_spmd

F32 = mybir.dt.float32
BF16 = mybir.dt.bfloat16
ALU = mybir.AluOpType
AF = mybir.ActivationFunctionType
EPS = 1e-6


class Cfg:
    def __init__(s, D=4096, S=8192, NB=2, HID=16384, PLE=256, G=4):
        s.D, s.S, s.NB, s.HID, s.PLE, s.G = D, S, NB, HID, PLE, G
        s.TO = S // G
        s.KC = D // 128
        s.TT = 512
        s.MW = 2048


class Lane:
    def __init__(self, name, sem, step):
        self.name, self.sem, self.step, self.count = name, sem, step, 0


class Buf:
    __slots__ = ("name", "w", "r", "lane")

    def __init__(self, name=""):
        self.name, self.w, self.r, self.lane = name, None, {}, None


class Eng:
    def __init__(self, name, lane):
        self.name, self.lane, self.waited, self.prog = name, lane, {}, []


class FW:
    ENG = ("pe", "act", "dve", "pool", "sp")

    def __init__(self, nc, es):
        self.nc, self.es = nc, es
        self.eng = {}
        for n in self.ENG:
            sem = es.enter_context(nc.semaphore("s_" + n))
            self.eng[n] = Eng(n, Lane(n, sem, 1))
        self.lanes = {}

    def dma_lane(self, name):
        if name not in self.lanes:
            sem = self.es.enter_context(self.nc.semaphore("d_" + name))
            self.lanes[name] = Lane(name, sem, 16)
        return self.lanes[name]

    def _deps(self, e, reads, writes, pe_acc=False):
        need = {}

        def add(lv):
            if lv is not None and need.get(lv[0], 0) < lv[1]:
                need[lv[0]] = lv[1]
        for b in reads:
            add(b.w)
        for b in writes:
            if not (pe_acc and b.w is not None and b.w[0] is e.lane):
                add(b.w)
            for lane, v in b.r.items():
                add((lane, v))
        waits = []
        for lane, v in need.items():
            if e.waited.get(lane, 0) < v:
                e.waited[lane] = v
                waits.append((lane.sem, v))
        return waits

    @staticmethod
    def _mark(lane, val, reads, writes):
        for b in reads:
            if b.r.get(lane, 0) < val:
                b.r[lane] = val
        for b in writes:
            b.w, b.r = (lane, val), {}

    def op(self, ename, fn, reads=(), writes=(), pe_acc=False):
        e = self.eng[ename]
        waits = self._deps(e, reads, writes, pe_acc)
        e.lane.count += 1
        sem = e.lane.sem

        def run(h, waits=waits, fn=fn, sem=sem):
            for s, v in waits:
                h.wait_ge(s, v)
            fn(h).then_inc(sem, 1)
        e.prog.append(run)
        self._mark(e.lane, e.lane.count, reads, writes)

    def dma(self, qname, lane_name, fn, reads=(), writes=()):
        e = self.eng[qname]
        b0 = writes[0]
        if b0.lane is None:
            self._nl = getattr(self, "_nl", 0) + 1
            b0.lane = self.dma_lane("%s_%d" % (lane_name, self._nl))
        lane = b0.lane
        waits = self._deps(e, reads, writes)
        lane.count += 1
        sem = lane.sem

        def run(h, waits=waits, fn=fn, sem=sem):
            for s, v in waits:
                h.wait_ge(s, v)
            fn(h).then_inc(sem, 16)
        e.prog.append(run)
        self._mark(lane, lane.count * 16, reads, writes)

    def flush(self, barrier=True):
        if barrier:
            finals = [(e.lane.sem, e.lane.count, e.lane) for e in self.eng.values() if e.lane.count]
            finals += [(l.sem, l.count * 16, l) for l in self.lanes.values() if l.count]
            for e in self.eng.values():
                ws = []
                for sem, v, lane in finals:
                    if lane is e.lane:
                        continue
                    if e.waited.get(lane, 0) < v:
                        e.waited[lane] = v
                        ws.append((sem, v))

                def bar(h, ws=ws):
                    for s, v in ws:
                        h.wait_ge(s, v)
                e.prog.append(bar)
        progs = {n: self.eng[n].prog for n in self.ENG}
        for n in self.ENG:
            self.eng[n].prog = []
        with self.nc.Block() as block:
            @block.tensor
            def _(h):
                for f in progs["pe"]:
                    f(h)

            @block.scalar
            def _(h):
                for f in progs["act"]:
                    f(h)

            @block.vector
            def _(h):
                for f in progs["dve"]:
                    f(h)

            @block.gpsimd
            def _(h):
                for f in progs["pool"]:
                    f(h)

            @block.sync
            def _(h):
                for f in progs["sp"]:
                    f(h)


class Ring:
    def __init__(self, es, nc, name, shape, dt, n, psum=False):
        mk = nc.psum_tensor if psum else nc.sbuf_tensor
        self.t = [es.enter_context(mk("%s%d" % (name, i), list(shape), dt)) for i in range(n)]
        self.b = [Buf("%s%d" % (name, i)) for i in range(n)]
        self.i = -1

    def next(self):
        self.i = (self.i + 1) % len(self.t)
        return self.t[self.i], self.b[self.i]


def blk_w(w):
    K, N = w.shape
    return np.ascontiguousarray(w.reshape(K // 128, 128, N // 128, 128).transpose(2, 1, 0, 3))


def col_vec(v):
    return np.ascontiguousarray(v.reshape(-1, 128).T)


SCALE = 128.0 ** -0.5
BIG = 1.0e4
GK = 2.0 * (2.0 / np.pi) ** 0.5


def rope_tab(t):
    inv = (np.float32(10000.0) ** (-np.arange(64, dtype=np.float32) / np.float32(64))).astype(np.float32)
    ang = t.astype(np.float32)[:, None] * inv[None, :]
    return np.cos(ang).astype(np.float32), np.sin(ang).astype(np.float32)


def nsa_tables(cfg, g):
    S, TO = cfg.S, cfg.TO
    PAD = S - (g + 1) * TO
    idx = np.arange(S)
    t = (idx - PAD).astype(np.float32)
    c, s = rope_tab(t)
    cosk = np.ascontiguousarray(np.concatenate([c, c], 1).T)
    sink = np.ascontiguousarray(np.concatenate([s, s], 1).T)
    NCP = S // 16
    n_true = np.arange(NCP) - PAD // 16
    cend = (16 * n_true + 31).astype(np.float32)
    cc, sc = rope_tab(cend)
    thr = np.where(n_true >= 0, cend, 1e9).astype(np.float32)
    q_idx = np.arange(S - TO, S)
    tq = (q_idx - PAD).astype(np.float32)
    NSB = S // 64
    j0 = PAD // 64
    jp = np.arange(NSB)
    bc = lambda v: np.ascontiguousarray(np.broadcast_to(v[None, :].astype(np.float32), (128, v.shape[0])))
    colf = lambda v: np.ascontiguousarray(v.astype(np.float32).reshape(-1, 128).T)
    return {
        "cosk": cosk, "sink": sink, "cosc": cc, "sinc": sc,
        "cthr_row": bc(thr), "cthr_col": colf(thr),
        "tq_row": bc(tq), "tq_col": colf(tq),
        "j_row": bc(jp), "curp_col": colf(q_idx // 64), "validj_row": bc((jp >= j0)), "e0_row": bc((jp == j0)),
        "kvalid_col": colf((idx >= PAD)),
    }


def nsa_consts(cfg):
    S = cfg.S
    NSB = S // 64
    Eb = (np.arange(S)[None, :] // 64 == np.arange(NSB)[:, None]).astype(np.float32)
    p = np.arange(128)[:, None, None]
    rel = np.arange(8)[None, :, None]
    q = np.arange(512)[None, None, :]
    caus = ((128 * rel[:, :4] + p) <= q).astype(np.float32)
    d = q - (128 * (rel - 4) + p)
    band = ((d >= 0) & (d < 512)).astype(np.float32)
    prot = np.zeros((128, 128), np.float32)
    for m in range(64):
        prot[m + 64, m] = -1.0
        prot[m, m + 64] = 1.0
    return {"ebig": Eb, "caus": np.ascontiguousarray(caus), "band": np.ascontiguousarray(band), "prot": prot,
            "ident128": np.eye(128, dtype=np.float32)}


class Common:
    def __init__(self, nc, fw, cfg, dr, es, pfx, with_bkb=True, nw=3):
        self.nc, self.fw, self.cfg, self.dr = nc, fw, cfg, dr
        KC = cfg.KC
        sb = lambda n, s, d: es.enter_context(nc.sbuf_tensor(pfx + n, list(s), d))
        self.sb = sb
        self.hT = sb("h", [128, KC, 512], BF16); self.b_h = [Buf() for _ in range(KC)]
        self.gcol = sb("g", [128, KC], F32); self.b_g = Buf()
        self.ones = sb("ones", [128, 128], BF16); self.b_c = Buf()
        self.epsc = sb("eps", [128, 1], F32)
        self.rstd = sb("rstd", [128, 512], F32); self.b_rstd = Buf()
        self.xr = Ring(es, nc, pfx + "xr", [128, 512], F32, 3)
        self.sq = Ring(es, nc, pfx + "sq", [128, 512], BF16, 2)
        self.wring = Ring(es, nc, pfx + "w", [128, KC, 128], BF16, nw)
        self.bank = Ring(es, nc, pfx + "bk", [128, 512], F32, 6, psum=True)
        self.bkb = Ring(es, nc, pfx + "bkb", [128, 1024], BF16, 2, psum=True) if with_bkb else None
        fw.dma("sp", "cst", lambda h: h.dma_start(out=self.gcol[:], in_=dr["gmix_col"]), writes=[self.b_g])
        fw.op("dve", lambda h: h.memset(self.ones[:], 1.0), writes=[self.b_c])
        fw.op("dve", lambda h: h.memset(self.epsc[:], EPS), writes=[self.b_g])

    def make_h(self, col0):
        fw, KC, D = self.fw, self.cfg.KC, self.cfg.D
        ts = slice(col0, col0 + 512)
        bk, bb = self.bank.next()
        for k in range(KC):
            x, xb = self.xr.next()
            fw.dma("sp", "x", lambda h, k=k, x=x, ts=ts: h.dma_start(out=x[:], in_=self.dr["xTpad"][k * 128:(k + 1) * 128, ts]), writes=[xb])
            st, sbf = self.sq.next()
            fw.op("act", lambda h, x=x, st=st: h.activation(out=st[:], in_=x[:], func=AF.Square), reads=[xb], writes=[sbf])
            fw.op("pe", lambda h, k=k, st=st, bk=bk: h.matmul(bk[:, :], lhsT=self.ones[:], rhs=st[:], start=(k == 0), stop=(k == KC - 1)),
                  reads=[self.b_c, sbf], writes=[bb], pe_acc=(k > 0))
        fw.op("act", lambda h, bk=bk: h.activation(out=self.rstd[:], in_=bk[:, :], func=AF.Sqrt, bias=self.epsc[:], scale=1.0 / D), reads=[bb, self.b_g], writes=[self.b_rstd])
        fw.op("dve", lambda h: h.reciprocal(out=self.rstd[:], in_=self.rstd[:]), reads=[self.b_rstd], writes=[self.b_rstd])
        for k in range(KC):
            x, xb = self.xr.next()
            fw.dma("sp", "x", lambda h, k=k, x=x, ts=ts: h.dma_start(out=x[:], in_=self.dr["xTpad"][k * 128:(k + 1) * 128, ts]), writes=[xb])
            fw.op("dve", lambda h, k=k, x=x: h.scalar_tensor_tensor(out=self.hT[:, k, :], in0=x[:], scalar=self.gcol[:, k:k + 1], in1=self.rstd[:], op0=ALU.mult, op1=ALU.mult),
                  reads=[xb, self.b_g, self.b_rstd], writes=[self.b_h[k]])

    def proj_f(self, wd, cb):
        fw, KC = self.fw, self.cfg.KC
        wt, wb = self.wring.next()
        fw.dma("pool", "w", lambda h: h.dma_start(out=wt[:], in_=wd[cb]), writes=[wb])
        bk, bb = self.bank.next()
        for k in range(KC):
            fw.op("pe", lambda h, k=k: h.matmul(bk[:, :], lhsT=wt[:, k, :], rhs=self.hT[:, k, :], start=(k == 0), stop=(k == KC - 1)),
                  reads=[wb, self.b_h[k]], writes=[bb], pe_acc=(k > 0))
        return bk, bb

    def mm(self, out, lhsT, rhs, rd, wr, start=True, stop=True, acc=False):
        self.fw.op("pe", lambda h: h.matmul(out, lhsT=lhsT, rhs=rhs, start=start, stop=stop), reads=rd, writes=[wr], pe_acc=acc)

    def tr(self, out, in_, ident, rd, wr):
        self.fw.op("pe", lambda h: h.transpose(out, in_, ident), reads=rd, writes=[wr])

    def rsq(self, ss, ssb, n, scale):
        fw = self.fw
        fw.op("act", lambda h: h.activation(out=ss, in_=ss, func=AF.Sqrt, bias=self.epsc[0:n, :], scale=scale), reads=[ssb, self.b_g], writes=[ssb])
        fw.op("dve", lambda h: h.reciprocal(out=ss, in_=ss), reads=[ssb], writes=[ssb])

    def norm_rope_f(self, bk, bb, gain_col, cosT, sinT, b_tab, prot, out, outb, scratch):
        fw = self.fw
        f32r, b16r = scratch
        xq, xqb = f32r.next()
        fw.op("act", lambda h: h.activation(out=xq[:], in_=bk[:, :], func=AF.Copy), reads=[bb], writes=[xqb])
        s2, s2b = b16r.next()
        fw.op("act", lambda h: h.activation(out=s2[:], in_=bk[:, :], func=AF.Square), reads=[bb], writes=[s2b])
        b2, b2b = self.bank.next()
        self.mm(b2[:, :], self.ones[:], s2[:], [self.b_c, s2b], b2b)
        rn, rnb = f32r.next()
        fw.op("act", lambda h: h.activation(out=rn[:], in_=b2[:, :], func=AF.Sqrt, bias=self.epsc[:], scale=1.0 / 128), reads=[b2b, self.b_g], writes=[rnb])
        fw.op("dve", lambda h: h.reciprocal(out=rn[:], in_=rn[:]), reads=[rnb], writes=[rnb])
        fw.op("dve", lambda h: h.scalar_tensor_tensor(out=xq[:], in0=xq[:], scalar=gain_col, in1=rn[:], op0=ALU.mult, op1=ALU.mult), reads=[xqb, rnb, self.b_c], writes=[xqb])
        xb16, xb16b = b16r.next()
        fw.op("act", lambda h: h.activation(out=xb16[:], in_=xq[:], func=AF.Copy), reads=[xqb], writes=[xb16b])
        b3, b3b = self.bank.next()
        self.mm(b3[:, :], prot, xb16[:], [self.b_c, xb16b], b3b)
        fw.op("dve", lambda h: h.tensor_tensor(out=xq[:], in0=xq[:], in1=cosT, op=ALU.mult), reads=[xqb, b_tab], writes=[xqb])
        fw.op("dve", lambda h: h.tensor_tensor(out=rn[:], in0=b3[:, :], in1=sinT, op=ALU.mult), reads=[b3b, b_tab], writes=[rnb])
        fw.op("dve", lambda h: h.tensor_tensor(out=out, in0=xq[:], in1=rn[:], op=ALU.add), reads=[xqb, rnb], writes=[outb])


def emit_nsa_k(nc, fw, cfg, dr):
    D, KC, S = cfg.D, cfg.KC, cfg.S
    NBLK = S // 512
    b_scr = Buf()
    with ExitStack() as es:
        cm = Common(nc, fw, cfg, dr, es, "n1_", nw=2)
        sb = cm.sb
        gains = sb("gains", [128, 3], F32)
        kcg = sb("kcg", [32, 128], F32)
        prot = sb("prot", [128, 128], BF16)
        idb = sb("idb", [128, 128], BF16)
        wvr = Ring(es, nc, "n1_wv", [128, KC, 512], BF16, 1)
        w1 = {"k": sb("w1k", [128, 32, 512], BF16), "v": sb("w1v", [128, 32, 512], BF16)}
        w2 = {"k": sb("w2k", [128, 4, 128], BF16), "v": sb("w2v", [128, 4, 128], BF16)}
        peT = {"k": sb("pek", [128, 32], BF16), "v": sb("pev", [128, 32], BF16)}
        cbias = {"k": sb("cbk", [128, 4], F32), "v": sb("cbv", [128, 4], F32)}
        b_w1 = Buf()
        tabs = Ring(es, nc, "n1_tab", [128, 2, 512], F32, 2)
        ctab = Ring(es, nc, "n1_ctab", [32, 2, 64], F32, 2)
        f32r = Ring(es, nc, "n1_f32", [128, 512], F32, 4)
        b16r = Ring(es, nc, "n1_b16", [128, 512], BF16, 4)
        outr = Ring(es, nc, "n1_out", [128, 512], BF16, 3)
        vout = Ring(es, nc, "n1_vout", [128, 8, 132], BF16, 4)
        cT = {(gq, kv): sb("cT%d%s" % (gq, kv), [128, 16 + 512], BF16) for gq in range(4) for kv in "kv"}
        b_cT = {k: Buf() for k in cT}
        hid = Ring(es, nc, "n1_hid", [128, 4, 32], BF16, 2)
        s32 = Ring(es, nc, "n1_s32", [128, 32], F32, 6)
        t32 = Ring(es, nc, "n1_t32", [32, 128], F32, 4)
        tb32 = Ring(es, nc, "n1_tb32", [32, 132], BF16, 3)
        c32 = Ring(es, nc, "n1_c32", [32, 2], F32, 4)
        kco = Ring(es, nc, "n1_kco", [128, 32], BF16, 2)
        zc = sb("zc", [128, 16], BF16)
        zv = sb("zv", [1, 132], BF16)

        fw.dma("sp", "cst", lambda h: h.dma_start(out=gains[:], in_=dr["kgain_cols"]), writes=[cm.b_c])
        fw.dma("sp", "cst", lambda h: h.dma_start(out=kcg[:], in_=dr["kcg_bc"]), writes=[cm.b_c])
        fw.dma("pool", "w", lambda h: h.dma_start(out=prot[:], in_=dr["prot"]), writes=[cm.b_c])
        fw.dma("pool", "w", lambda h: h.dma_start(out=idb[:], in_=dr["ident128"]), writes=[cm.b_c])
        for kv in "kv":
            for l0 in range(0, 32, 4):
                fw.dma("pool", "w", lambda h, kv=kv, l0=l0: h.dma_start(out=w1[kv][:, l0:l0 + 4, :], in_=dr["w1" + kv][:, l0:l0 + 4, :]), writes=[b_w1])
            fw.dma("pool", "w", lambda h, kv=kv: h.dma_start(out=w2[kv][:], in_=dr["w2" + kv]), writes=[b_w1])
            fw.dma("pool", "w", lambda h, kv=kv: h.dma_start(out=peT[kv][:], in_=dr["pe" + kv]), writes=[b_w1])
        for k in cT:
            fw.op("dve", lambda h, k=k: h.memset(cT[k][:, 0:16], 0.0), writes=[b_cT[k]])
        fw.op("dve", lambda h: h.memset(zc[:], 0.0), writes=[cm.b_c])
        fw.op("dve", lambda h: h.memset(zv[:], 0.0), writes=[cm.b_c])
        for i_ in range(2):
            fw.op("dve", lambda h, i_=i_: h.memset(ctab.t[i_][:], 0.0), writes=[ctab.b[i_]])
        for i_ in range(4):
            fw.op("dve", lambda h, i_=i_: h.memset(vout.t[i_][:, :, 128:132], 1.0), writes=[vout.b[i_]])
        for i_ in range(3):
            fw.op("dve", lambda h, i_=i_: h.memset(tb32.t[i_][:, 128:132], 1.0), writes=[tb32.b[i_]])
        for kv in "kv":
            bk, bb = cm.bank.next()
            for hb in range(4):
                for l in range(32):
                    cm.mm(bk[:, hb:hb + 1], w1[kv][:, l, hb * 128:(hb + 1) * 128], peT[kv][:, l:l + 1], [b_w1], bb, start=(l == 0), stop=(l == 31), acc=not (hb == 0 and l == 0))
            fw.op("act", lambda h, kv=kv, bk=bk: h.activation(out=cbias[kv][:], in_=bk[:, 0:4], func=AF.Copy), reads=[bb], writes=[b_w1])
        for gq in range(4):
            fw.dma("sp", "scr", lambda h, gq=gq: h.dma_start(out=dr["KC"][gq][:, S // 16 - 16:S // 16], in_=zc[:]), reads=[cm.b_c], writes=[b_scr])
            fw.dma("sp", "scr", lambda h, gq=gq: h.dma_start(out=dr["VC"][gq][127:128, (S // 16 - 1) // 128, :], in_=zv[0:1, :]), reads=[cm.b_c], writes=[b_scr])

        for blk in range(NBLK):
            c0 = blk * 512
            cm.make_h(c0)
            tb, tbb = tabs.next()
            fw.dma("sp", "tab", lambda h, tb=tb, c0=c0: h.dma_start(out=tb[:, 0, :], in_=dr["cosk"][:, c0:c0 + 512]), writes=[tbb])
            fw.dma("sp", "tab", lambda h, tb=tb, c0=c0: h.dma_start(out=tb[:, 1, :], in_=dr["sink"][:, c0:c0 + 512]), writes=[tbb])
            ct, ctb = ctab.next()
            n0 = c0 // 16 - 1
            j_lo = 1 if blk == 0 else 0
            fw.dma("sp", "tab", lambda h, ct=ct, n0=n0, j_lo=j_lo: h.dma_start(out=ct[j_lo:32, 0, :], in_=dr["cosc"][n0 + j_lo:n0 + 32, :]), writes=[ctb])
            fw.dma("sp", "tab", lambda h, ct=ct, n0=n0, j_lo=j_lo: h.dma_start(out=ct[j_lo:32, 1, :], in_=dr["sinc"][n0 + j_lo:n0 + 32, :]), writes=[ctb])
            vos = [vout.next() for _ in range(4)]
            for half in range(2):
                wv, b_wv = wvr.next()
                fw.dma("pool", "w", lambda h, wv=wv, half=half: h.dma_start(out=wv[:], in_=dr["w_v_t"][:, :, half * 512:(half + 1) * 512]), writes=[b_wv])
                for tl in range(4):
                    vo, vob = vos[tl]
                    bk, bb = cm.bank.next()
                    for k in range(KC):
                        cm.mm(bk[:, :], cm.hT[:, k, tl * 128:(tl + 1) * 128], wv[:, k, :], [cm.b_h[k], b_wv], bb, start=(k == 0), stop=(k == KC - 1), acc=(k > 0))
                    fw.op("act", lambda h, vo=vo, bk=bk, half=half: h.activation(out=vo[:, half * 4:(half + 1) * 4, 0:128], in_=bk[:, :].rearrange("p (g c) -> p g c", c=128), func=AF.Copy), reads=[bb], writes=[vob])
            for tl in range(4):
                vo, vob = vos[tl]
                tile_ = c0 // 128 + tl
                for g4 in range(4):
                    fw.dma("sp", "scr", lambda h, vo=vo, g4=g4, tile_=tile_: h.dma_start(out=dr["VS"][g4][:, tile_, :], in_=vo[:, g4, :]), reads=[vob], writes=[b_scr])
                    fw.dma("sp", "scr", lambda h, vo=vo, g4=g4, tile_=tile_: h.dma_start(out=dr["VW"][g4][:, tile_, :], in_=vo[:, 4 + g4, :]), reads=[vob], writes=[b_scr])
            for gq in range(4):
                for nm, gi, dst in (("w_ks", 0, "KS"), ("w_kw", 1, "KW")):
                    bk, bb = cm.proj_f(dr[nm], gq)
                    o, ob = outr.next()
                    cm.norm_rope_f(bk, bb, gains[:, gi:gi + 1], tb[:, 0, :], tb[:, 1, :], tbb, prot[:], o[:], ob, (f32r, b16r))
                    fw.dma("sp", "scr", lambda h, o=o, dst=dst, gq=gq, c0=c0: h.dma_start(out=dr[dst][gq][:, c0:c0 + 512], in_=o[:]), reads=[ob], writes=[b_scr])
                for kv, nm in (("k", "w_kc"), ("v", "w_vc")):
                    bk, bb = cm.proj_f(dr[nm], gq)
                    t_, tb_ = cT[(gq, kv)], b_cT[(gq, kv)]
                    fw.op("act", lambda h, t_=t_, bk=bk: h.activation(out=t_[:, 16:16 + 512], in_=bk[:, :], func=AF.Copy), reads=[bb], writes=[tb_])
                    hd_, hdb = hid.next()
                    for hb in range(4):
                        b2, b2b = cm.bank.next()
                        for l in range(32):
                            cm.mm(b2[:, 0:32], w1[kv][:, l, hb * 128:(hb + 1) * 128], t_[:, l:l + 16 * 31 + 1:16], [b_w1, tb_], b2b, start=(l == 0), stop=(l == 31), acc=(l > 0))
                        x_, xb_ = s32.next(); y_, yb_ = s32.next()
                        fw.op("act", lambda h, x_=x_, b2=b2, kv=kv, hb=hb: h.activation(out=x_[:], in_=b2[:, 0:32], func=AF.Identity, bias=cbias[kv][:, hb:hb + 1]), reads=[b2b, b_w1], writes=[xb_])
                        fw.op("dve", lambda h, x_=x_, y_=y_: h.tensor_tensor(out=y_[:], in0=x_[:], in1=x_[:], op=ALU.mult), reads=[xb_], writes=[yb_])
                        fw.op("dve", lambda h, y_=y_: h.tensor_scalar(out=y_[:], in0=y_[:], scalar1=0.044715, scalar2=1.0, op0=ALU.mult, op1=ALU.add), reads=[yb_], writes=[yb_])
                        fw.op("dve", lambda h, x_=x_, y_=y_: h.tensor_tensor(out=y_[:], in0=y_[:], in1=x_[:], op=ALU.mult), reads=[xb_, yb_], writes=[yb_])
                        fw.op("act", lambda h, y_=y_: h.activation(out=y_[:], in_=y_[:], func=AF.Sigmoid, scale=float(GK)), reads=[yb_], writes=[yb_])
                        fw.op("dve", lambda h, x_=x_, y_=y_, hd_=hd_, hb=hb: h.tensor_tensor(out=hd_[:, hb, :], in0=x_[:], in1=y_[:], op=ALU.mult), reads=[xb_, yb_], writes=[hdb])
                    fw.op("dve", lambda h, t_=t_: h.tensor_copy(out=t_[:, 0:16], in_=t_[:, 512:528]), reads=[tb_], writes=[tb_])
                    b3, b3b = cm.bank.next()
                    for hb in range(4):
                        cm.mm(b3[0:32, 0:128], hd_[:, hb, :], w2[kv][:, hb, :], [hdb, b_w1], b3b, start=(hb == 0), stop=(hb == 3), acc=(hb > 0))
                    if kv == "v":
                        vc, vcb = tb32.next()
                        fw.op("act", lambda h, vc=vc, b3=b3: h.activation(out=vc[:, 0:128], in_=b3[0:32, 0:128], func=AF.Copy), reads=[b3b], writes=[vcb])
                        r0 = n0 + j_lo
                        while r0 < n0 + 32:
                            r1 = min(n0 + 32, (r0 // 128 + 1) * 128)
                            fw.dma("sp", "scr", lambda h, vc=vc, gq=gq, n0=n0, r0=r0, r1=r1: h.dma_start(out=dr["VC"][gq][r0 % 128:r0 % 128 + (r1 - r0), r0 // 128, :], in_=vc[r0 - n0:r1 - n0, :]), reads=[vcb], writes=[b_scr])
                            r0 = r1
                    else:
                        cs_, csb = c32.next()
                        jk, jkb = tb32.next()
                        fw.op("act", lambda h, jk=jk, b3=b3, cs_=cs_: h.activation(out=jk[:, 0:128], in_=b3[0:32, 0:128], func=AF.Square, accum_out=cs_[:, 0:1]), reads=[b3b], writes=[jkb, csb])
                        cm.rsq(cs_[:, 0:1], csb, 32, 1.0 / 128)
                        kn, knb = t32.next()
                        fw.op("dve", lambda h, kn=kn, b3=b3, cs_=cs_: h.scalar_tensor_tensor(out=kn[:], in0=b3[0:32, 0:128], scalar=cs_[:, 0:1], in1=kcg[:], op0=ALU.mult, op1=ALU.mult), reads=[b3b, csb, cm.b_c], writes=[knb])
                        r1, r1b = t32.next(); r2, r2b = t32.next()
                        fw.op("dve", lambda h, r1=r1, kn=kn, ct=ct: h.tensor_tensor(out=r1[:, 0:64], in0=kn[:, 0:64], in1=ct[:, 0, :], op=ALU.mult), reads=[knb, ctb], writes=[r1b])
                        fw.op("dve", lambda h, r1=r1, kn=kn, ct=ct: h.tensor_tensor(out=r1[:, 64:128], in0=kn[:, 64:128], in1=ct[:, 0, :], op=ALU.mult), reads=[knb, ctb], writes=[r1b])
                        fw.op("dve", lambda h, r2=r2, kn=kn, ct=ct: h.tensor_tensor(out=r2[:, 0:64], in0=kn[:, 64:128], in1=ct[:, 1, :], op=ALU.mult), reads=[knb, ctb], writes=[r2b])
                        fw.op("dve", lambda h, r2=r2, kn=kn, ct=ct: h.tensor_tensor(out=r2[:, 64:128], in0=kn[:, 0:64], in1=ct[:, 1, :], op=ALU.mult), reads=[knb, ctb], writes=[r2b])
                        kr, krb = tb32.next()
                        fw.op("dve", lambda h, kr=kr, r1=r1, r2=r2: h.tensor_tensor(out=kr[:, 0:64], in0=r1[:, 0:64], in1=r2[:, 0:64], op=ALU.subtract), reads=[r1b, r2b], writes=[krb])
                        fw.op("dve", lambda h, kr=kr, r1=r1, r2=r2: h.tensor_tensor(out=kr[:, 64:128], in0=r1[:, 64:128], in1=r2[:, 64:128], op=ALU.add), reads=[r1b, r2b], writes=[krb])
                        pb, pbb = cm.bkb.next()
                        cm.tr(pb[:, 0:32], kr[:, 0:128], idb[0:32, 0:32], [krb, cm.b_c], pbb)
                        ko, kob = kco.next()
                        fw.op("act", lambda h, ko=ko, pb=pb: h.activation(out=ko[:], in_=pb[:, 0:32], func=AF.Copy), reads=[pbb], writes=[kob])
                        fw.dma("sp", "scr", lambda h, ko=ko, gq=gq, n0=n0, j_lo=j_lo: h.dma_start(out=dr["KC"][gq][:, n0 + j_lo:n0 + 32], in_=ko[:, j_lo:32]), reads=[kob], writes=[b_scr])
        fw.flush(barrier=True)
    return b_scr


def emit_nsa_q(nc, fw, cfg, dr):
    D, KC, S, TO = cfg.D, cfg.KC, cfg.S, cfg.TO
    NOWN = TO // 512
    NCP, NSB, NKT = S // 16, S // 64, S // 128
    NCT = max(NCP // 128, 1)
    CW = min(NCP, 512)
    b_out = Buf()
    with ExitStack() as es:
        cm = Common(nc, fw, cfg, dr, es, "n2_", with_bkb=False, nw=2)
        sb = cm.sb
        qg = sb("qg", [128, 1], F32)
        prot = sb("prot", [128, 128], BF16)
        idb = sb("idb", [128, 128], BF16)
        idf = sb("idf", [128, 128], F32)
        wgt = sb("wgt", [128, KC, 48], BF16); b_wgt = Buf()
        ebig = sb("ebig", [NSB, S], BF16)
        caus = sb("caus", [128, 4, 512], BF16)
        band = sb("band", [128, 8, 512], BF16)
        tq_row = sb("tqr", [128, TO], F32)
        colt = sb("colt", [128, 3, max(TO // 128, NKT, NCT)], F32)
        kval = sb("kval", [128, NKT], F32)
        cthc = sb("cthc", [128, NCT], F32)
        rowt = sb("rowt", [128, 4, NSB], F32)
        cthr = sb("cthr", [128, NCP], F32)
        tabs = Ring(es, nc, "n2_tab", [128, 2, 512], F32, 1)
        f32r = Ring(es, nc, "n2_f32", [128, 512], F32, 4)
        b16r = Ring(es, nc, "n2_b16", [128, 512], BF16, 4)
        qT = [sb("qT%d" % i, [128, 512], BF16) for i in range(4)]; b_qT = [Buf() for _ in range(4)]
        gate = sb("gate", [128, 4, 48], F32); b_gate = Buf()
        KSs = sb("KSs", [128, S], BF16); b_KS = Buf()
        VSs = sb("VSs", [128, NKT, 132], BF16); b_VS = Buf()
        KWs = sb("KWs", [128, 1024], BF16); b_KW = Buf()
        VWs = sb("VWs", [128, 8, 132], BF16); b_VW = Buf()
        KCs = sb("KCs", [128, NCP], BF16); b_KC = Buf()
        VCs = sb("VCs", [128, NCT, 132], BF16); b_VC = Buf()
        mrow = sb("mrow", [128, 4, CW], F32); b_mrow = Buf()
        mcT = sb("mcT", [128, NCT, 512], BF16); b_mcT = Buf()
        mW = sb("mW", [128, 8, 512], BF16); b_mW = Buf()
        p4 = sb("p4", [128, NCP + 4], F32); b_p4 = Buf()
        selT = sb("selT", [NSB, 512], BF16); b_selT = Buf()
        mS = Ring(es, nc, "n2_mS", [128, 512], BF16, 3)
        pT = Ring(es, nc, "n2_pT", [128, 512], BF16, 4)
        pr = Ring(es, nc, "n2_pr", [128, CW], F32, 3)
        sc = Ring(es, nc, "n2_sc", [128, 4, NSB], F32, 2)
        c8 = Ring(es, nc, "n2_c8", [128, 16], F32, 6)
        oacc = sb("oacc", [128, 4, 128], F32); b_oacc = Buf()
        ob16 = Ring(es, nc, "n2_ob16", [128, 128], BF16, 2)
        oT = Ring(es, nc, "n2_oT", [128, 512], BF16, 2)
        accb = Ring(es, nc, "n2_acc", [128, 512], F32, 2, psum=True)

        ld = lambda dst, src, q="sp", lane="cst", wr=cm.b_c: fw.dma(q, lane, lambda h: h.dma_start(out=dst, in_=src), writes=[wr])
        ld(qg[:], dr["qgain_col"]); ld(tq_row[:], dr["tq_row"]); ld(colt[:, 0, 0:TO // 128], dr["tq_col"]); ld(colt[:, 1, 0:TO // 128], dr["curp_col"])
        ld(kval[:], dr["kvalid_col"]); ld(cthc[:], dr["cthr_col"]); ld(cthr[:], dr["cthr_row"]); ld(idf[:], dr["ident128"])
        ld(rowt[:, 0, :], dr["j_row"]); ld(rowt[:, 1, :], dr["validj_row"]); ld(rowt[:, 2, :], dr["e0_row"])
        for dst, nm in ((prot[:], "prot"), (idb[:], "ident128"), (caus[:], "caus"), (band[:], "band")):
            ld(dst, dr[nm], q="pool", lane="w")
        for e0 in range(0, S, 2048):
            ld(ebig[:, e0:e0 + 2048], dr["ebig"][:, e0:e0 + 2048], q="pool", lane="w")
        ld(wgt[:], dr["w_gate_t"], q="pool", lane="w", wr=b_wgt)
        fw.op("dve", lambda h: h.memset(p4[:], 0.0), writes=[b_p4])

        for ob_ in range(NOWN):
            blk = S // 512 - NOWN + ob_
            c0 = blk * 512
            cm.make_h(c0)
            tb, tbb = tabs.next()
            fw.dma("sp", "tab", lambda h, tb=tb, c0=c0: h.dma_start(out=tb[:, 0, :], in_=dr["cosk"][:, c0:c0 + 512]), writes=[tbb])
            fw.dma("sp", "tab", lambda h, tb=tb, c0=c0: h.dma_start(out=tb[:, 1, :], in_=dr["sink"][:, c0:c0 + 512]), writes=[tbb])
            qs = slice(ob_ * 512, ob_ * 512 + 512)
            nkt = (c0 + 512) // 128
            nct = min(NCT, (c0 + 512) // 16 // 128 + 1)
            for tl in range(4):
                bk, bb = cm.bank.next()
                for k in range(KC):
                    cm.mm(bk[:, 0:48], cm.hT[:, k, tl * 128:(tl + 1) * 128], wgt[:, k, :], [cm.b_h[k], b_wgt], bb, start=(k == 0), stop=(k == KC - 1), acc=(k > 0))
                fw.op("act", lambda h, bk=bk, tl=tl: h.activation(out=gate[:, tl, :], in_=bk[:, 0:48], func=AF.Sigmoid), reads=[bb], writes=[b_gate])
            for qt in range(4):
                fw.op("dve", lambda h, qt=qt, ob_=ob_: h.tensor_scalar(out=mrow[:, qt, :], in0=cthr[:, 0:CW], scalar1=colt[:, 0, ob_ * 4 + qt:ob_ * 4 + qt + 1], scalar2=None, op0=ALU.is_le),
                      reads=[cm.b_c], writes=[b_mrow])
            for ct in range(nct):
                fw.op("dve", lambda h, ct=ct, qs=qs: h.tensor_scalar(out=mcT[:, ct, :], in0=tq_row[:, qs], scalar1=cthc[:, ct:ct + 1], scalar2=None, op0=ALU.is_ge),
                      reads=[cm.b_c], writes=[b_mcT])
            for wt in range(8):
                kt = c0 // 128 - 4 + wt
                fw.op("dve", lambda h, wt=wt, kt=kt: h.tensor_scalar(out=mW[:, wt, :], in0=band[:, wt, :], scalar1=kval[:, kt:kt + 1], scalar2=None, op0=ALU.mult),
                      reads=[cm.b_c], writes=[b_mW])
            for gq in range(4):
                fw.dma("sp", "kv", lambda h, gq=gq, nkt=nkt: h.dma_start(out=KSs[:, 0:nkt * 128], in_=dr["KS"][gq][:, 0:nkt * 128]), reads=[dr["_b_scr"]], writes=[b_KS])
                fw.dma("sp", "kv", lambda h, gq=gq, nkt=nkt: h.dma_start(out=VSs[:, 0:nkt, :], in_=dr["VS"][gq][:, 0:nkt, :]), reads=[dr["_b_scr"]], writes=[b_VS])
                fw.dma("sp", "kv", lambda h, gq=gq, c0=c0: h.dma_start(out=KWs[:], in_=dr["KW"][gq][:, c0 - 512:c0 + 512]), reads=[dr["_b_scr"]], writes=[b_KW])
                fw.dma("sp", "kv", lambda h, gq=gq, c0=c0: h.dma_start(out=VWs[:, :, :], in_=dr["VW"][gq][:, c0 // 128 - 4:c0 // 128 + 4, :]), reads=[dr["_b_scr"]], writes=[b_VW])
                fw.dma("sp", "kv", lambda h, gq=gq: h.dma_start(out=KCs[:], in_=dr["KC"][gq]), reads=[dr["_b_scr"]], writes=[b_KC])
                fw.dma("sp", "kv", lambda h, gq=gq: h.dma_start(out=VCs[:, :, :], in_=dr["VC"][gq]), reads=[dr["_b_scr"]], writes=[b_VC])
                for hl in range(4):
                    bk, bb = cm.proj_f(dr["w_q"], gq * 4 + hl)
                    cm.norm_rope_f(bk, bb, qg[:, 0:1], tb[:, 0, :], tb[:, 1, :], tbb, prot[:], qT[hl][:], b_qT[hl], (f32r, b16r))
                for qt in range(4):
                    for hl in range(4):
                        bk, bb = cm.bank.next()
                        cm.mm(bk[:, 0:CW], qT[hl][:, qt * 128:(qt + 1) * 128], KCs[:, 0:CW], [b_qT[hl], b_KC], bb)
                        p_, pb_ = pr.next()
                        fw.op("act", lambda h, p_=p_, bk=bk: h.activation(out=p_[:], in_=bk[:, 0:CW], func=AF.Exp, scale=SCALE), reads=[bb], writes=[pb_])
                        c_, cb_ = c8.next()
                        fw.op("dve", lambda h, p_=p_, qt=qt, c_=c_: h.scalar_tensor_tensor(out=p_[:], in0=p_[:], scalar=1.0, in1=mrow[:, qt, :], op0=ALU.mult, op1=ALU.mult, accum_out=c_[:, 0:1]),
                              reads=[pb_, b_mrow], writes=[pb_, cb_])
                        fw.op("dve", lambda h, c_=c_: h.tensor_scalar(out=c_[:, 0:1], in0=c_[:, 0:1], scalar1=1e-30, scalar2=None, op0=ALU.max), reads=[cb_], writes=[cb_])
                        fw.op("dve", lambda h, c_=c_: h.reciprocal(out=c_[:, 0:1], in_=c_[:, 0:1]), reads=[cb_], writes=[cb_])
                        if hl == 0:
                            fw.op("dve", lambda h, p_=p_, c_=c_: h.tensor_scalar(out=p4[:, 4:4 + CW], in0=p_[:], scalar1=c_[:, 0:1], scalar2=None, op0=ALU.mult), reads=[pb_, cb_], writes=[b_p4])
                        else:
                            fw.op("dve", lambda h, p_=p_, c_=c_: h.scalar_tensor_tensor(out=p4[:, 4:4 + CW], in0=p_[:], scalar=c_[:, 0:1], in1=p4[:, 4:4 + CW], op0=ALU.mult, op1=ALU.add), reads=[pb_, cb_, b_p4], writes=[b_p4])
                    s_, sb_ = sc.next()
                    NJ = CW // 4
                    fw.op("dve", lambda h, s_=s_: h.tensor_reduce(out=s_[:, 0, 0:NJ], in_=p4[:, 4:4 + CW].rearrange("p (j r) -> p j r", r=4), axis=mybir.AxisListType.X, op=ALU.add), reads=[b_p4], writes=[sb_])
                    fw.op("dve", lambda h, s_=s_: h.tensor_tensor(out=s_[:, 0, 0:NJ], in0=s_[:, 0, 0:NJ], in1=p4[:, 0:CW].rearrange("p (j r) -> p j r", r=4)[:, :, 3], op=ALU.add), reads=[b_p4, sb_], writes=[sb_])
                    col = colt[:, 1, ob_ * 4 + qt:ob_ * 4 + qt + 1]
                    fw.op("dve", lambda h, s_=s_, col=col: h.scalar_tensor_tensor(out=s_[:, 1, :], in0=rowt[:, 0, :], scalar=col, in1=rowt[:, 1, :], op0=ALU.is_le, op1=ALU.mult), reads=[cm.b_c], writes=[sb_])
                    fw.op("dve", lambda h, s_=s_, col=col: h.scalar_tensor_tensor(out=s_[:, 2, :], in0=rowt[:, 0, :], scalar=col, in1=rowt[:, 2, :], op0=ALU.is_equal, op1=ALU.add), reads=[cm.b_c], writes=[sb_])
                    fw.op("dve", lambda h, s_=s_, col=col: h.tensor_scalar(out=s_[:, 3, :], in0=rowt[:, 0, :], scalar1=1.0, scalar2=col, op0=ALU.add, op1=ALU.is_equal), reads=[cm.b_c], writes=[sb_])
                    fw.op("dve", lambda h, s_=s_: h.tensor_tensor(out=s_[:, 2, :], in0=s_[:, 2, :], in1=s_[:, 3, :], op=ALU.add), reads=[sb_], writes=[sb_])
                    fw.op("dve", lambda h, s_=s_: h.scalar_tensor_tensor(out=s_[:, 0, :], in0=s_[:, 0, :], scalar=1.0, in1=s_[:, 1, :], op0=ALU.add, op1=ALU.mult), reads=[sb_], writes=[sb_])
                    fw.op("dve", lambda h, s_=s_: h.scalar_tensor_tensor(out=s_[:, 0, :], in0=s_[:, 2, :], scalar=BIG, in1=s_[:, 0, :], op0=ALU.mult, op1=ALU.add), reads=[sb_], writes=[sb_])
                    m8, m8b = c8.next()
                    fw.op("dve", lambda h, s_=s_, m8=m8: h.max(out=m8[:, 0:8], in_=s_[:, 0, :]), reads=[sb_], writes=[m8b])
                    fw.op("dve", lambda h, s_=s_, m8=m8: h.match_replace(out=s_[:, 3, :], in_to_replace=m8[:, 0:8], in_values=s_[:, 0, :], imm_value=-1.0), reads=[sb_, m8b], writes=[sb_])
                    fw.op("dve", lambda h, s_=s_, m8=m8: h.max(out=m8[:, 8:16], in_=s_[:, 3, :]), reads=[sb_], writes=[m8b])
                    fw.op("dve", lambda h, s_=s_, m8=m8: h.scalar_tensor_tensor(out=s_[:, 3, :], in0=s_[:, 0, :], scalar=m8[:, 15:16], in1=s_[:, 1, :], op0=ALU.is_ge, op1=ALU.mult), reads=[sb_, m8b], writes=[sb_])
                    bt, btb = cm.bank.next()
                    cm.tr(bt[0:NSB, 0:128], s_[:, 3, :], idf[:, :], [sb_, cm.b_c], btb)
                    fw.op("act", lambda h, bt=bt, qt=qt: h.activation(out=selT[:, qt * 128:(qt + 1) * 128], in_=bt[0:NSB, 0:128], func=AF.Copy), reads=[btb], writes=[b_selT])
                for hl in range(4):
                    hd = gq * 4 + hl
                    for br in range(3):
                        npc = min(128, NCP)
                        if br == 0:
                            tiles = [(KCs[:, ct * 128:ct * 128 + npc], VCs[0:npc, ct, 0:129], mcT[0:npc, ct, :], b_KC, b_VC, [b_mcT], npc) for ct in range(nct)]
                        elif br == 2:
                            tiles = [(KWs[:, wt * 128:(wt + 1) * 128], VWs[:, wt, 0:129], mW[:, wt, :], b_KW, b_VW, [b_mW], 128) for wt in range(8)]
                        else:
                            tiles = [(KSs[:, kt * 128:(kt + 1) * 128], VSs[:, kt, 0:129], None, b_KS, b_VS, [], 128) for kt in range(nkt)]
                        acA, acAb = accb.next(); acB, acBb = accb.next()
                        accs = [(acA, acAb, 0), (acA, acAb, 132), (acB, acBb, 0), (acB, acBb, 132)]
                        for ti, (kap, vap, msk, kb_, vb_, mb_l, np_) in enumerate(tiles):
                            if br == 1:
                                bm, bmb = cm.bank.next()
                                cm.mm(bm[:, :], ebig[:, ti * 128:(ti + 1) * 128], selT[:, :], [cm.b_c, b_selT], bmb)
                                m_, mb_ = mS.next()
                                rel = ti - c0 // 128
                                if rel >= 0:
                                    fw.op("dve", lambda h, m_=m_, bm=bm, rel=rel: h.tensor_tensor(out=m_[:], in0=bm[:, :], in1=caus[:, rel, :], op=ALU.mult), reads=[bmb, cm.b_c], writes=[mb_])
                                else:
                                    fw.op("act", lambda h, m_=m_, bm=bm: h.activation(out=m_[:], in_=bm[:, :], func=AF.Copy), reads=[bmb], writes=[mb_])
                                msk, mb_l = m_[:], [mb_]
                            bs, bsb = cm.bank.next()
                            cm.mm(bs[0:np_, :], kap, qT[hl][:], [kb_, b_qT[hl]], bsb)
                            p_, pb_ = pT.next()
                            fw.op("act", lambda h, p_=p_, bs=bs, np_=np_: h.activation(out=p_[0:np_, :], in_=bs[0:np_, :], func=AF.Exp, scale=SCALE), reads=[bsb], writes=[pb_])
                            fw.op("dve", lambda h, p_=p_, msk=msk, np_=np_: h.tensor_tensor(out=p_[0:np_, :], in0=p_[0:np_, :], in1=msk, op=ALU.mult), reads=[pb_] + mb_l, writes=[pb_])
                            for qt in range(4):
                                a, ab, off = accs[qt]
                                cm.mm(a[:, off:off + 129], p_[0:np_, qt * 128:(qt + 1) * 128], vap, [pb_, vb_], ab,
                                      start=(ti == 0 and qt in (0, 2)), stop=(ti == len(tiles) - 1), acc=(ti > 0 or qt in (1, 3)))
                        for qt in range(4):
                            a, ab, off = accs[qt]
                            c_, cb_ = c8.next()
                            fw.op("dve", lambda h, c_=c_, a=a, off=off: h.tensor_scalar(out=c_[:, 0:1], in0=a[:, off + 128:off + 129], scalar1=1e-30, scalar2=None, op0=ALU.max), reads=[ab], writes=[cb_])
                            fw.op("dve", lambda h, c_=c_: h.reciprocal(out=c_[:, 0:1], in_=c_[:, 0:1]), reads=[cb_], writes=[cb_])
                            gi = hd * 3 + br
                            fw.op("dve", lambda h, c_=c_, qt=qt, gi=gi: h.tensor_tensor(out=c_[:, 0:1], in0=c_[:, 0:1], in1=gate[:, qt, gi:gi + 1], op=ALU.mult), reads=[cb_, b_gate], writes=[cb_])
                            if br == 0:
                                fw.op("dve", lambda h, c_=c_, a=a, off=off, qt=qt: h.tensor_scalar(out=oacc[:, qt, :], in0=a[:, off:off + 128], scalar1=c_[:, 0:1], scalar2=None, op0=ALU.mult), reads=[ab, cb_], writes=[b_oacc])
                            else:
                                fw.op("dve", lambda h, c_=c_, a=a, off=off, qt=qt: h.scalar_tensor_tensor(out=oacc[:, qt, :], in0=a[:, off:off + 128], scalar=c_[:, 0:1], in1=oacc[:, qt, :], op0=ALU.mult, op1=ALU.add), reads=[ab, cb_, b_oacc], writes=[b_oacc])
                    o_, ob2 = oT.next()
                    for qt in range(4):
                        bt, btb = cm.bank.next()
                        cm.tr(bt[:, 0:128], oacc[:, qt, :], idf[:, :], [b_oacc, cm.b_c], btb)
                        fw.op("act", lambda h, o_=o_, bt=bt, qt=qt: h.activation(out=o_[:, qt * 128:(qt + 1) * 128], in_=bt[:, 0:128], func=AF.Copy), reads=[btb], writes=[ob2])
                    fw.dma("sp", "oa", lambda h, o_=o_, hd=hd, qs=qs: h.dma_start(out=dr["oaT"][hd * 128:(hd + 1) * 128, qs], in_=o_[:]), reads=[ob2], writes=[b_out])
        fw.flush(barrier=True)
    return b_out


def nsa_inputs(cfg, inp, c):
    D, S, TO, G = cfg.D, cfg.S, cfg.TO, cfg.G
    b, g = c // G, c % G
    n = (g + 1) * TO
    xp = np.zeros((D, S), np.float32)
    xp[:, S - n:] = inp["x"][b, :n].T
    w_in = inp["w_in"][0]
    kv = lambda i: w_in[:, 2048 + i * 512:2048 + (i + 1) * 512]
    tmaj = lambda w: np.ascontiguousarray(w.reshape(D // 128, 128, w.shape[1]).transpose(1, 0, 2))
    rep = lambda v, n_: np.ascontiguousarray(np.broadcast_to(v[None, :], (n_, v.shape[0])))
    z = np.zeros(128, np.float32)
    m = {
        "xTpad": xp, "gmix_col": col_vec(inp["g_mix"][0]),
        "w_kc": blk_w(kv(0)), "w_vc": blk_w(kv(1)), "w_ks": blk_w(kv(2)), "w_kw": blk_w(kv(4)),
        "w_v_t": tmaj(np.concatenate([kv(3), kv(5)], axis=1)),
        "w_q": blk_w(w_in[:, 0:2048]), "w_gate_t": tmaj(w_in[:, 5120:5168]),
        "kgain_cols": np.ascontiguousarray(np.stack([inp["nsa_ks_gain"][0], inp["nsa_kw_gain"][0], z], axis=1)),
        "kcg_bc": rep(inp["nsa_kc_gain"][0], 32), "qgain_col": np.ascontiguousarray(inp["nsa_q_gain"][0][:, None]),
        "w1k": np.ascontiguousarray(inp["cmp_wk1"][0].transpose(1, 0, 2)), "w1v": np.ascontiguousarray(inp["cmp_wv1"][0].transpose(1, 0, 2)),
        "w2k": np.ascontiguousarray(inp["cmp_wk2"][0].reshape(4, 128, 128).transpose(1, 0, 2)),
        "w2v": np.ascontiguousarray(inp["cmp_wv2"][0].reshape(4, 128, 128).transpose(1, 0, 2)),
        "pek": np.ascontiguousarray(inp["cmp_pe_k"][0].T), "pev": np.ascontiguousarray(inp["cmp_pe_v"][0].T),
    }
    m.update(nsa_tables(cfg, g))
    m.update(nsa_consts(cfg))
    return m


TDT = BF16


def gdn_consts():
    i = np.arange(64)
    ident = np.eye(64, dtype=np.float32)
    MU = (i[:, None] <= i[None, :]).astype(np.float32)
    MsU = (i[:, None] < i[None, :]).astype(np.float32)
    MsL = (i[:, None] > i[None, :]).astype(np.float32)
    ones = np.ones((64, 64), np.float32)
    return np.ascontiguousarray(np.stack([ident, MU, MsU, MsL, ones], axis=1))


def emit_gdn(nc, fw, cfg, dr, heads=None):
    D, KC, S, TO = cfg.D, cfg.KC, cfg.S, cfg.TO
    TB = 512
    NBLK, NOWN = S // TB, TO // TB
    NH = 16
    with ExitStack() as es:
        cm = Common(nc, fw, cfg, dr, es, "g_", nw=2)
        sb = lambda n, s, d: es.enter_context(nc.sbuf_tensor(n, list(s), d))
        hT, b_h, b_g, epsc = cm.hT, cm.b_h, cm.b_g, cm.epsc
        cst = sb("g_cst", [64, 5, 64], F32); b_cst = Buf()
        ones32 = sb("g_ones32", [64, 128], F32)
        mk2 = sb("g_mk2", [64, 128], F32)
        idb = sb("g_idb", [128, 128], BF16)
        idt = sb("g_idt", [64, 64], TDT)
        cw = sb("g_cw", [128, NH, 3, 4], F32)
        bc16 = sb("g_bc16", [64, 3, NH], F32)
        ogain = sb("g_og", [64, 128], F32)
        wbd = sb("g_wbd", [128, KC, 32], BF16); b_wbd = Buf()
        Sf = sb("g_Sf", [128, NH, 128], F32); Sb = sb("g_Sb", [128, NH, 128], BF16)
        b_Sf = [Buf() for _ in range(NH)]; b_Sb = [Buf() for _ in range(NH)]
        halo = sb("g_halo", [128, NH, 3, 3], F32); b_halo = [[Buf() for _ in range(3)] for _ in range(NH)]
        NCH = 8
        zsr = Ring(es, nc, "g_zs", [128, TB], BF16, NCH)
        cin = Ring(es, nc, "g_cin", [128, 3 + TB], F32, 3)
        cac = Ring(es, nc, "g_cac", [128, TB], F32, 2)
        csl = {i: Ring(es, nc, "g_cs%d" % i, [128, TB], BF16, NCH) for i in range(3)}
        gat = Ring(es, nc, "g_gat", [64, 8, NH], F32, 8)
        gst = Ring(es, nc, "g_gst", [64, 3, NH], F32, 8)
        etot = Ring(es, nc, "g_etot", [128, NH], F32, 8)
        RES = [{"t64": Ring(es, nc, "g_t64_%d" % i_, [64, 128], F32, 5), "tb64": Ring(es, nc, "g_tb64_%d" % i_, [64, 128], BF16, 12),
                "tt": Ring(es, nc, "g_tt_%d" % i_, [64, 64], TDT, 8), "t128": Ring(es, nc, "g_t128_%d" % i_, [128, 64], BF16, 6),
                "f128": Ring(es, nc, "g_f128_%d" % i_, [128, 64], F32, 2), "col": Ring(es, nc, "g_col_%d" % i_, [64, 4], F32, 8)} for i_ in range(NCH)]
        obr = Ring(es, nc, "g_ob", [128, TB], BF16, NCH)
        bank, bkb = cm.bank, cm.bkb
        b_out = Buf()

        fw.dma("sp", "cst", lambda h: h.dma_start(out=cst[:], in_=dr["gcst"]), writes=[b_cst])
        fw.dma("sp", "cst", lambda h: h.dma_start(out=cw[:], in_=dr["cw"]), writes=[b_cst])
        fw.dma("sp", "cst", lambda h: h.dma_start(out=bc16[:, 0:2, :], in_=dr["adt"]), writes=[b_cst])
        fw.dma("sp", "cst", lambda h: h.dma_start(out=ogain[:], in_=dr["ogain"]), writes=[b_cst])
        fw.dma("pool", "w", lambda h: h.dma_start(out=wbd[:], in_=dr["w_bd"]), writes=[b_wbd])
        fw.dma("pool", "w", lambda h: h.dma_start(out=idb[:], in_=dr["ident128"]), writes=[b_cst])
        fw.op("dve", lambda h: h.memset(ones32[:], 1.0), writes=[b_cst])
        fw.op("dve", lambda h: h.memset(Sf[:], 0.0), writes=b_Sf)
        fw.op("dve", lambda h: h.memset(Sb[:], 0.0), writes=b_Sb)
        fw.op("dve", lambda h: h.memset(halo[:], 0.0), writes=[b for r in b_halo for b in r])
        fw.op("dve", lambda h: h.tensor_copy(out=mk2[:, 0:64], in_=cst[:, 1, :]), reads=[b_cst], writes=[b_cst])
        fw.op("dve", lambda h: h.tensor_copy(out=mk2[:, 64:128], in_=cst[:, 2, :]), reads=[b_cst], writes=[b_cst])
        fw.op("dve", lambda h: h.tensor_copy(out=idt[:], in_=cst[:, 0, :]), reads=[b_cst], writes=[b_cst])
        fw.op("act", lambda h: h.activation(out=bc16[:, 2, :], in_=bc16[:, 0, :], func=AF.Exp), reads=[b_cst], writes=[b_cst])
        fw.op("dve", lambda h: h.tensor_scalar(out=bc16[:, 2, :], in0=bc16[:, 2, :], scalar1=-1.0, scalar2=None, op0=ALU.mult), reads=[b_cst], writes=[b_cst])

        proj_f = cm.proj_f

        def mm(out, lhsT, rhs, rd, wr, start=True, stop=True, acc=False):
            fw.op("pe", lambda h: h.matmul(out, lhsT=lhsT, rhs=rhs, start=start, stop=stop), reads=rd, writes=[wr], pe_acc=acc)

        def tr(out, in_, ident, rd, wr):
            fw.op("pe", lambda h: h.transpose(out, in_, ident), reads=rd + [b_cst], writes=[wr])

        def rsq(ss, ssb, n, scale):
            fw.op("act", lambda h: h.activation(out=ss, in_=ss, func=AF.Sqrt, bias=epsc[0:n, :], scale=scale), reads=[ssb, b_g], writes=[ssb])
            fw.op("dve", lambda h: h.reciprocal(out=ss, in_=ss), reads=[ssb], writes=[ssb])

        for blk in range(NBLK):
            own = blk >= NBLK - NOWN
            cm.make_h(blk * TB)
            G = []
            for c in range(8):
                cs = slice(c * 64, c * 64 + 64)
                bk, bb = bank.next()
                for k in range(KC):
                    mm(bk[0:64, 0:32], hT[:, k, cs], wbd[:, k, :], [b_h[k], b_wbd], bb, start=(k == 0), stop=(k == KC - 1), acc=(k > 0))
                ga, gab = gat.next()
                fw.op("act", lambda h, ga=ga, bk=bk: h.activation(out=ga[:, 0, :], in_=bk[0:64, 0:16], func=AF.Exp, scale=-1.0), reads=[bb], writes=[gab])
                fw.op("act", lambda h, ga=ga: h.activation(out=ga[:, 0, :], in_=ga[:, 0, :], func=AF.Ln, bias=1.0), reads=[gab], writes=[gab])
                fw.op("act", lambda h, ga=ga: h.activation(out=ga[:, 2, :], in_=ga[:, 0, :], func=AF.Exp, scale=-1.0), reads=[gab], writes=[gab])
                fw.op("dve", lambda h, ga=ga: h.tensor_scalar(out=ga[:, 1, :], in0=ga[:, 0, :], scalar1=-1.0, scalar2=None, op0=ALU.mult), reads=[gab], writes=[gab])
                fw.op("dve", lambda h, ga=ga, bk=bk: h.tensor_tensor(out=ga[:, 3, :], in0=bk[0:64, 16:32], in1=bc16[:, 1, :], op=ALU.add), reads=[bb, b_cst], writes=[gab])
                fw.op("act", lambda h, ga=ga: h.activation(out=ga[:, 3, :], in_=ga[:, 3, :], func=AF.Exp), reads=[gab], writes=[gab])
                fw.op("act", lambda h, ga=ga: h.activation(out=ga[:, 3, :], in_=ga[:, 3, :], func=AF.Ln, bias=1.0), reads=[gab], writes=[gab])
                fw.op("dve", lambda h, ga=ga: h.tensor_tensor(out=ga[:, 3, :], in0=ga[:, 3, :], in1=bc16[:, 2, :], op=ALU.mult), reads=[gab, b_cst], writes=[gab])
                b2, b2b = bank.next()
                mm(b2[0:64, 0:16], cst[:, 1, :], ga[:, 3, :], [b_cst, gab], b2b)
                mm(b2[0:64, 16:32], cst[:, 3, :], ga[:, 3, :], [b_cst, gab], b2b, acc=True)
                mm(b2[:, 32:48], ones32[:, :], ga[:, 3, :], [b_cst, gab], b2b, acc=True)
                gs, gsb = gst.next()
                et, etb = etot.next()
                fw.op("act", lambda h, gs=gs, b2=b2: h.activation(out=gs[:, 0, :], in_=b2[0:64, 0:16], func=AF.Copy), reads=[b2b], writes=[gsb])
                fw.op("act", lambda h, gs=gs, b2=b2: h.activation(out=gs[:, 1, :], in_=b2[0:64, 0:16], func=AF.Exp), reads=[b2b], writes=[gsb])
                fw.op("act", lambda h, gs=gs, b2=b2: h.activation(out=gs[:, 2, :], in_=b2[0:64, 16:32], func=AF.Exp), reads=[b2b], writes=[gsb])
                fw.op("act", lambda h, et=et, b2=b2: h.activation(out=et[:], in_=b2[:, 32:48], func=AF.Exp), reads=[b2b], writes=[etb])
                fw.op("dve", lambda h, ga=ga, gs=gs: h.tensor_tensor(out=ga[:, 4, :], in0=ga[:, 2, :], in1=gs[:, 1, :], op=ALU.mult), reads=[gab, gsb], writes=[gab])
                G.append((ga, gab, gs, gsb, et, etb))
            for hg in range(16 // NCH):
                if heads is not None and not any(hg * NCH + q in heads for q in range(NCH)):
                    continue
                CS = {}
                OB = {}
                for hl in range(NCH):
                    hd = hg * NCH + hl
                    if heads is not None and hd not in heads:
                        continue
                    cs_t = CS.setdefault(hd, {})
                    for i, wn in ((1, "w_gk"), (2, "w_gv"), (0, "w_gq")):
                        if i == 0 and not (own or blk == NBLK - NOWN - 1):
                            continue
                        bk, bb = proj_f(dr[wn], hd)
                        ci, cib = cin.next()
                        fw.op("act", lambda h, ci=ci, bk=bk: h.activation(out=ci[:, 3:3 + TB], in_=bk[:, :], func=AF.Copy), reads=[bb], writes=[cib])
                        fw.op("dve", lambda h, ci=ci, i=i, hd=hd: h.tensor_copy(out=ci[:, 0:3], in_=halo[:, hd, i, :]), reads=[b_halo[hd][i]], writes=[cib])
                        fw.op("dve", lambda h, ci=ci, i=i, hd=hd: h.tensor_copy(out=halo[:, hd, i, :], in_=ci[:, TB:TB + 3]), reads=[cib], writes=[b_halo[hd][i]])
                        if i == 0 and not own:
                            continue
                        ac, acb = cac.next()
                        fw.op("dve", lambda h, ci=ci, ac=ac, i=i, hd=hd: h.tensor_scalar(out=ac[:], in0=ci[:, 0:TB], scalar1=cw[:, hd, i, 0:1], scalar2=None, op0=ALU.mult), reads=[cib, b_cst], writes=[acb])
                        for j in (1, 2, 3):
                            fw.op("dve", lambda h, ci=ci, ac=ac, i=i, hd=hd, j=j: h.scalar_tensor_tensor(out=ac[:], in0=ci[:, j:j + TB], scalar=cw[:, hd, i, j:j + 1], in1=ac[:], op0=ALU.mult, op1=ALU.add),
                                  reads=[cib, b_cst, acb], writes=[acb])
                        st, stb = csl[i].next()
                        fw.op("act", lambda h, st=st, ac=ac: h.activation(out=st[:], in_=ac[:], func=AF.Silu), reads=[acb], writes=[stb])
                        cs_t[i] = (st, stb)
                    OB[hd] = (obr.next() if own else (None, None))
                    if own:
                        bk, bb = proj_f(dr["w_gz"], hd)
                        zs, zsb = zsr.next()
                        fw.op("act", lambda h, zs=zs, bk=bk: h.activation(out=zs[:], in_=bk[:, :], func=AF.Silu), reads=[bb], writes=[zsb])
                        CS[hd]["z"] = (zs, zsb)

                def chain(hd, hl, c, R):
                    cs_t = CS[hd]
                    ob, obb = OB[hd]
                    cs = slice(c * 64, c * 64 + 64)
                    ga, gab, gs, gsb, et, etb = G[c][:6]
                    kst, kstb = cs_t[1]
                    pb, pbb = bkb.next()
                    tr(pb[0:64, 0:128], kst[:, cs], idb[:, :], [kstb], pbb)
                    cl, clb = R["col"].next()
                    junk, jb = R["tb64"].next()
                    fw.op("act", lambda h, junk=junk, pb=pb, cl=cl: h.activation(out=junk[:], in_=pb[0:64, 0:128], func=AF.Square, accum_out=cl[:, 0:1]), reads=[pbb], writes=[jb, clb])
                    rsq(cl[:, 0:1], clb, 64, 1.0)
                    kn, knb = R["tb64"].next()
                    fw.op("dve", lambda h, kn=kn, pb=pb, cl=cl: h.tensor_scalar(out=kn[:], in0=pb[0:64, 0:128], scalar1=cl[:, 0:1], scalar2=None, op0=ALU.mult), reads=[pbb, clb], writes=[knb])
                    yield
                    pb2, pb2b = bkb.next()
                    tr(pb2[:, 0:64], kn[:], idb[0:64, 0:64], [knb], pb2b)
                    kT, kTb = R["t128"].next()
                    fw.op("act", lambda h, kT=kT, pb2=pb2: h.activation(out=kT[:], in_=pb2[:, 0:64], func=AF.Copy), reads=[pb2b], writes=[kTb])
                    yield
                    vst, vstb = cs_t[2]
                    pb3, pb3b = bkb.next()
                    tr(pb3[0:64, 0:128], vst[:, cs], idb[:, :], [vstb], pb3b)
                    vb, vbb = R["tb64"].next()
                    fw.op("dve", lambda h, vb=vb, pb3=pb3, ga=ga, hd=hd: h.tensor_scalar(out=vb[:], in0=pb3[0:64, 0:128], scalar1=ga[:, 2, hd:hd + 1], scalar2=None, op0=ALU.mult), reads=[pb3b, gab], writes=[vbb])
                    yield
                    r12, r12b = R["t64"].next()
                    fw.op("dve", lambda h, r12=r12, ga=ga, hd=hd: h.tensor_scalar(out=r12[:, 0:64], in0=cst[:, 1, :], scalar1=ga[:, 3, hd:hd + 1], scalar2=None, op0=ALU.mult), reads=[b_cst, gab], writes=[r12b])
                    fw.op("dve", lambda h, r12=r12, ga=ga, hd=hd: h.scalar_tensor_tensor(out=r12[:, 64:128], in0=cst[:, 0, :], scalar=ga[:, 1, hd:hd + 1], in1=r12[:, 0:64], op0=ALU.mult, op1=ALU.add), reads=[b_cst, gab, r12b], writes=[r12b])
                    pg, pgb = bank.next()
                    mm(pg[:, 0:64], ones32[:, :], r12[:, 0:64], [b_cst, r12b], pgb)
                    mm(pg[0:64, 64:128], ones32[:, 0:64], r12[:, 64:128], [b_cst, r12b], pgb, acc=True)
                    if own:
                        eg, egb = R["f128"].next()
                        fw.op("act", lambda h, eg=eg, pg=pg: h.activation(out=eg[:], in_=pg[:, 0:64], func=AF.Exp), reads=[pgb], writes=[egb])
                    dm, dmb = R["t64"].next()
                    fw.op("dve", lambda h, dm=dm, pg=pg, gs=gs, hd=hd: h.tensor_scalar(out=dm[:], in0=pg[0:64, 0:128], scalar1=gs[:, 0, hd:hd + 1], scalar2=0.0, op0=ALU.subtract, op1=ALU.min), reads=[pgb, gsb], writes=[dmb])
                    yield
                    fw.op("act", lambda h, dm=dm: h.activation(out=dm[:], in_=dm[:], func=AF.Exp), reads=[dmb], writes=[dmb])
                    fw.op("dve", lambda h, dm=dm: h.tensor_tensor(out=dm[:], in0=dm[:], in1=mk2[:], op=ALU.mult), reads=[dmb, b_cst], writes=[dmb])
                    pk, pkb = bank.next()
                    mm(pk[0:64, 0:64], kT[:, :], kT[:, :], [kTb], pkb)
                    X, Xb = R["tt"].next()
                    fw.op("dve", lambda h, X=X, pk=pk, dm=dm: h.tensor_tensor(out=X[:], in0=pk[0:64, 0:64], in1=dm[:, 64:128], op=ALU.mult), reads=[pkb, dmb], writes=[Xb])
                    yield
                    pa, pab = bkb.next()
                    tr(pa[0:64, 0:64], X[:], idb[0:64, 0:64], [Xb], pab)
                    Y, Yb = R["tt"].next()
                    fw.op("act", lambda h, Y=Y, pa=pa: h.activation(out=Y[:], in_=pa[0:64, 0:64], func=AF.Copy), reads=[pab], writes=[Yb])
                    P, Pb = R["tt"].next()
                    fw.op("dve", lambda h, P=P, X=X: h.tensor_tensor(out=P[:], in0=idt[:], in1=X[:], op=ALU.subtract), reads=[Xb, b_cst], writes=[Pb])
                    yield
                    for lvl in range(5):
                        py, pyb = bank.next()
                        mm(py[0:64, 0:64], X[:], Y[:], [Xb, Yb], pyb)
                        if lvl < 4:
                            mm(py[0:64, 64:128], Y[:], X[:], [Xb, Yb], pyb, acc=True)
                        Y2, Y2b = R["tt"].next()
                        fw.op("act", lambda h, Y2=Y2, py=py: h.activation(out=Y2[:], in_=py[0:64, 0:64], func=AF.Copy), reads=[pyb], writes=[Y2b])
                        if lvl < 4:
                            X2, X2b = R["tt"].next()
                            fw.op("act", lambda h, X2=X2, py=py: h.activation(out=X2[:], in_=py[0:64, 64:128], func=AF.Copy), reads=[pyb], writes=[X2b])
                            yield
                        pp, ppb = bank.next()
                        mm(pp[0:64, 0:64], Y2[:], P[:], [Y2b, Pb], ppb)
                        P2, P2b = R["tt"].next()
                        fw.op("dve", lambda h, P2=P2, P=P, pp=pp: h.tensor_tensor(out=P2[:], in0=P[:], in1=pp[0:64, 0:64], op=ALU.add), reads=[Pb, ppb], writes=[P2b])
                        yield
                        P, Pb = P2, P2b
                        Y, Yb = Y2, Y2b
                        if lvl < 4:
                            X, Xb = X2, X2b
                    if TDT != BF16:
                        Pm, Pmb = R["tb64"].next()
                        fw.op("act", lambda h, Pm=Pm, P=P: h.activation(out=Pm[:, 0:64], in_=P[:], func=AF.Copy), reads=[Pb], writes=[Pmb])
                        Pt = Pm[:, 0:64]; Ptb = Pmb
                    else:
                        Pt = P[:]; Ptb = Pb
                    pu, pub = bank.next()
                    mm(pu[0:64, 0:128], Pt, vb[:], [Ptb, vbb], pub)
                    u, ub = R["t64"].next()
                    fw.op("act", lambda h, u=u, pu=pu: h.activation(out=u[:], in_=pu[0:64, 0:128], func=AF.Copy), reads=[pub], writes=[ub])
                    yield
                    kbg, kbgb = R["tb64"].next()
                    fw.op("dve", lambda h, kbg=kbg, kn=kn, ga=ga, hd=hd: h.tensor_scalar(out=kbg[:], in0=kn[:], scalar1=ga[:, 4, hd:hd + 1], scalar2=None, op0=ALU.mult), reads=[knb, gab], writes=[kbgb])
                    pw, pwb = bank.next()
                    mm(pw[:, 0:64], kbg[:], Pt, [kbgb, Ptb], pwb)
                    wT, wTb = R["t128"].next()
                    fw.op("act", lambda h, wT=wT, pw=pw: h.activation(out=wT[:], in_=pw[:, 0:64], func=AF.Copy), reads=[pwb], writes=[wTb])
                    yield
                    kg, kgb = R["tb64"].next()
                    fw.op("dve", lambda h, kg=kg, kn=kn, gs=gs, hd=hd: h.tensor_scalar(out=kg[:], in0=kn[:], scalar1=gs[:, 2, hd:hd + 1], scalar2=None, op0=ALU.mult), reads=[knb, gsb], writes=[kgb])
                    pws, pwsb = bank.next()
                    mm(pws[0:64, 0:128], wT[:], Sb[:, hd, :], [wTb, b_Sb[hd]], pwsb)
                    vn, vnb = R["tb64"].next()
                    fw.op("dve", lambda h, vn=vn, u=u, pws=pws: h.tensor_tensor(out=vn[:], in0=u[:], in1=pws[0:64, 0:128], op=ALU.subtract), reads=[ub, pwsb], writes=[vnb])
                    yield
                    if own:
                        qst, qstb = cs_t[0]
                        pq, pqb = bkb.next()
                        tr(pq[0:64, 0:128], qst[:, cs], idb[:, :], [qstb], pqb)
                        cq, cqb = R["col"].next()
                        junk2, j2b = R["tb64"].next()
                        fw.op("act", lambda h, junk2=junk2, pq=pq, cq=cq: h.activation(out=junk2[:], in_=pq[0:64, 0:128], func=AF.Square, accum_out=cq[:, 0:1]), reads=[pqb], writes=[j2b, cqb])
                        rsq(cq[:, 0:1], cqb, 64, 1.0)
                        qn, qnb = R["tb64"].next()
                        fw.op("dve", lambda h, qn=qn, pq=pq, cq=cq: h.tensor_scalar(out=qn[:], in0=pq[0:64, 0:128], scalar1=cq[:, 0:1], scalar2=128.0 ** -0.5, op0=ALU.mult, op1=ALU.mult), reads=[pqb, cqb], writes=[qnb])
                        yield
                        pq2, pq2b = bkb.next()
                        tr(pq2[:, 0:64], qn[:], idb[0:64, 0:64], [qnb], pq2b)
                        qT, qTb = R["t128"].next()
                        fw.op("act", lambda h, qT=qT, pq2=pq2: h.activation(out=qT[:], in_=pq2[:, 0:64], func=AF.Copy), reads=[pq2b], writes=[qTb])
                        qg, qgb = R["t128"].next()
                        fw.op("dve", lambda h, qg=qg, qT=qT, eg=eg: h.tensor_tensor(out=qg[:], in0=qT[:], in1=eg[:], op=ALU.mult), reads=[qTb, egb], writes=[qgb])
                        yield
                        pat, patb = bank.next()
                        mm(pat[0:64, 0:64], kT[:, :], qT[:, :], [kTb, qTb], patb)
                        at, atb = R["tb64"].next()
                        fw.op("dve", lambda h, at=at, pat=pat, dm=dm: h.tensor_tensor(out=at[:, 0:64], in0=pat[0:64, 0:64], in1=dm[:, 0:64], op=ALU.mult), reads=[patb, dmb], writes=[atb])
                        yield
                        po, pob = bank.next()
                        mm(po[0:64, 0:128], qg[:], Sb[:, hd, :], [qgb, b_Sb[hd]], pob, start=True, stop=False)
                        mm(po[0:64, 0:128], at[:, 0:64], vn[:], [atb, vnb], pob, start=False, stop=True, acc=True)
                        co, cob = R["col"].next()
                        junk3, j3b = R["tb64"].next()
                        fw.op("act", lambda h, junk3=junk3, po=po, co=co: h.activation(out=junk3[:], in_=po[0:64, 0:128], func=AF.Square, accum_out=co[:, 0:1]), reads=[pob], writes=[j3b, cob])
                        rsq(co[:, 0:1], cob, 64, 1.0 / 128)
                        on, onb = R["t64"].next()
                        fw.op("dve", lambda h, on=on, po=po, co=co: h.scalar_tensor_tensor(out=on[:], in0=po[0:64, 0:128], scalar=co[:, 0:1], in1=ogain[:], op0=ALU.mult, op1=ALU.mult), reads=[pob, cob, b_cst], writes=[onb])
                        yield
                        zs, zsb = cs_t["z"]
                        pz, pzb = bkb.next()
                        tr(pz[0:64, 0:128], zs[:, cs], idb[:, :], [zsb], pzb)
                        obt, obtb = R["tb64"].next()
                        fw.op("dve", lambda h, obt=obt, on=on, pz=pz: h.tensor_tensor(out=obt[:], in0=on[:], in1=pz[0:64, 0:128], op=ALU.mult), reads=[onb, pzb], writes=[obtb])
                        pot, potb = bkb.next()
                        tr(pot[:, 0:64], obt[:], idb[0:64, 0:64], [obtb], potb)
                        fw.op("act", lambda h, ob=ob, pot=pot, cs=cs: h.activation(out=ob[:, cs], in_=pot[:, 0:64], func=AF.Copy), reads=[potb], writes=[obb])
                        yield
                    ps, psb = bank.next()
                    mm(ps[:, 0:128], kg[:], vn[:], [kgb, vnb], psb)
                    fw.op("dve", lambda h, hd=hd, et=et, ps=ps: h.scalar_tensor_tensor(out=Sf[:, hd, :], in0=Sf[:, hd, :], scalar=et[:, hd:hd + 1], in1=ps[:, 0:128], op0=ALU.mult, op1=ALU.add),
                          reads=[b_Sf[hd], etb, psb], writes=[b_Sf[hd]])
                    fw.op("act", lambda h, hd=hd: h.activation(out=Sb[:, hd, :], in_=Sf[:, hd, :], func=AF.Copy), reads=[b_Sf[hd]], writes=[b_Sb[hd]])

                hds = [hg * NCH + hl for hl in range(NCH) if heads is None or hg * NCH + hl in heads]
                for c in range(8):
                    cs = slice(c * 64, c * 64 + 64)
                    gens = [chain(hd, hd % 4, c, RES[i_]) for i_, hd in enumerate(hds)]
                    while gens:
                        nxt = []
                        for g_ in gens:
                            try:
                                next(g_)
                                nxt.append(g_)
                            except StopIteration:
                                pass
                        gens = nxt
                if own:
                    oblk = blk - (NBLK - NOWN)
                    for hd in hds:
                        ob, obb = OB[hd]
                        fw.dma("sp", "ob", lambda h, ob=ob, hd=hd, oblk=oblk: h.dma_start(out=dr["obT"][hd * 128:(hd + 1) * 128, oblk * TB:(oblk + 1) * TB], in_=ob[:]), reads=[obb], writes=[b_out])
        fw.flush(barrier=True)
    return b_out


def gdn_inputs(cfg, inp, c):
    D, S, TO, G = cfg.D, cfg.S, cfg.TO, cfg.G
    b, g = c // G, c % G
    n = (g + 1) * TO
    xp = np.zeros((D, S), np.float32)
    xp[:, S - n:] = inp["x"][b, :n].T
    w_in = inp["w_in"][0]
    o = 5168
    cwt = inp["gdn_conv_w"][0].reshape(4, 3, 16, 128)
    rep = lambda v, n_: np.ascontiguousarray(np.broadcast_to(v[None, :], (n_, v.shape[0])))
    return {
        "xTpad": xp, "gmix_col": col_vec(inp["g_mix"][0]), "gcst": gdn_consts(),
        "ident128": np.eye(128, dtype=np.float32),
        "cw": np.ascontiguousarray(cwt.transpose(3, 2, 1, 0)),
        "adt": np.ascontiguousarray(np.stack([rep(inp["gdn_a_log"][0], 64), rep(inp["gdn_dt_bias"][0], 64)], axis=1)),
        "ogain": rep(inp["gdn_o_gain"][0], 64),
        "w_gq": blk_w(w_in[:, o:o + 2048]), "w_gk": blk_w(w_in[:, o + 2048:o + 4096]), "w_gv": blk_w(w_in[:, o + 4096:o + 6144]),
        "w_gz": blk_w(w_in[:, 11312:11312 + 2048]),
        "w_bd": np.ascontiguousarray(w_in[:, 13360:13392].reshape(D // 128, 128, 32).transpose(1, 0, 2)),
    }


def emit_tail(nc, fw, cfg, dr):
    D, KC, TT, TO, HID, MW = cfg.D, cfg.KC, cfg.TT, cfg.TO, cfg.HID, cfg.MW
    MC, PC, HC = MW // 128, cfg.PLE // 128, HID // 128
    HH = min(HC, 32)
    with ExitStack() as es:
        sb = lambda n, s, d: es.enter_context(nc.sbuf_tensor(n, list(s), d))
        xT = sb("t_x", [128, KC, TT], F32); b_x = [Buf("x%d" % k) for k in range(KC)]
        hT = sb("t_h", [128, KC, TT], BF16); b_h = [Buf("h%d" % k) for k in range(KC)]
        big = sb("t_big", [128, max(2 * MC + KC, HH), TT], BF16)
        b_oa, b_ob = Buf("oa"), Buf("ob")
        b_mx = [Buf("mx%d" % k) for k in range(KC)]
        b_hd = [Buf("hd%d" % k) for k in range(HH)]
        pTb = sb("t_p", [128, PC, TT], BF16); b_p = Buf("p")
        gcol = sb("t_g", [128, 3, KC], F32); b_g = Buf("g")
        ones = sb("t_ones", [128, 128], BF16); b_ones = Buf("ones")
        epsc = sb("t_eps", [128, 1], F32)
        rstd = sb("t_rstd", [128, TT], F32); b_rstd = Buf("rstd")
        wring = Ring(es, nc, "t_w", [128, max(KC, HH), 128], BF16, 3)
        sq = Ring(es, nc, "t_sq", [128, TT], BF16, 2)
        tmp = Ring(es, nc, "t_tmp", [128, TT], F32, 4)
        yt = Ring(es, nc, "t_y", [128, TT], F32, 2)
        bank = Ring(es, nc, "t_bk", [128, 512], F32, 8, psum=True)
        b_y = Buf("y")

        fw.dma("sp", "cst", lambda h: h.dma_start(out=gcol[:], in_=dr["gcols"]), writes=[b_g])
        fw.op("dve", lambda h: h.memset(ones[:], 1.0), writes=[b_ones])
        fw.op("dve", lambda h: h.memset(epsc[:], EPS), writes=[b_g])

        def load_w(wd, cb, nk, k0=0):
            wt, wb = wring.next()
            fw.dma("pool", "w", lambda h: h.dma_start(out=wt[:, 0:nk, :], in_=wd[cb, :, k0:k0 + nk, :]), writes=[wb])
            return wt, wb

        def acc(wd, cb, nk, rhs, rbufs, k0=0):
            wt, wb = load_w(wd, cb, nk, k0)
            bk, bb = bank.next()
            for k in range(nk):
                fw.op("pe", lambda h, k=k: h.matmul(bk[:, 0:TT], lhsT=wt[:, k, :], rhs=rhs(k), start=(k == 0), stop=(k == nk - 1)),
                      reads=[wb, rbufs[k]], writes=[bb], pe_acc=(k > 0))
            return bk, bb

        def norm_to_h(gi):
            bk, bb = bank.next()
            for k in range(KC):
                st, sbuf_ = sq.next()
                fw.op("act", lambda h, k=k, st=st: h.activation(out=st[:], in_=xT[:, k, :], func=AF.Square), reads=[b_x[k]], writes=[sbuf_])
                fw.op("pe", lambda h, k=k, st=st: h.matmul(bk[:, 0:TT], lhsT=ones[:], rhs=st[:], start=(k == 0), stop=(k == KC - 1)),
                      reads=[b_ones, sbuf_], writes=[bb], pe_acc=(k > 0))
            fw.op("act", lambda h: h.activation(out=rstd[:], in_=bk[:, 0:TT], func=AF.Sqrt, bias=epsc[:], scale=1.0 / D), reads=[bb, b_g], writes=[b_rstd])
            fw.op("dve", lambda h: h.reciprocal(out=rstd[:], in_=rstd[:]), reads=[b_rstd], writes=[b_rstd])
            for k in range(KC):
                fw.op("dve", lambda h, k=k: h.scalar_tensor_tensor(out=hT[:, k, :], in0=xT[:, k, :], scalar=gcol[:, gi, k:k + 1], in1=rstd[:], op0=ALU.mult, op1=ALU.mult),
                      reads=[b_x[k], b_g, b_rstd], writes=[b_h[k]])

        oaT, obT, mxT = big[:, 0:MC, :], big[:, MC:2 * MC, :], big[:, 2 * MC:2 * MC + KC, :]
        for blk in range(TO // TT):
            ts = slice(blk * TT, (blk + 1) * TT)
            for k in range(KC):
                fw.dma("sp", "x", lambda h, k=k, blk=blk: h.dma_start(out=xT[:, k, :], in_=dr["xTpad"][k * 128:(k + 1) * 128, cfg.S - cfg.TO + blk * TT:cfg.S - cfg.TO + (blk + 1) * TT]), writes=[b_x[k]])
            fw.dma("pool", "mix", lambda h, ts=ts: h.dma_start(out=oaT, in_=dr["oaT"].rearrange("(c p) t -> p c t", p=128)[:, :, ts]), writes=[b_oa])
            fw.dma("pool", "mix", lambda h, ts=ts: h.dma_start(out=obT, in_=dr["obT"].rearrange("(c p) t -> p c t", p=128)[:, :, ts]), writes=[b_ob])
            fw.dma("pool", "mix", lambda h, ts=ts: h.dma_start(out=pTb[:], in_=dr["pT"].rearrange("(c p) t -> p c t", p=128)[:, :, ts]), writes=[b_p])
            norm_to_h(0)
            hr = lambda k: hT[:, k, :]
            for cb in range(KC):
                ga, gab = acc(dr["w_ga"], cb, KC, hr, b_h)
                ua, uab = acc(dr["w_upa"], cb, MC, lambda k: big[:, k, :], [b_oa] * MC)
                gb, gbb = acc(dr["w_gb"], cb, KC, hr, b_h)
                ub, ubb = acc(dr["w_upb"], cb, MC, lambda k: big[:, MC + k, :], [b_ob] * MC)
                s1, s1b = tmp.next(); s2, s2b = tmp.next()
                fw.op("act", lambda h, s1=s1, ga=ga: h.activation(out=s1[:], in_=ga[:, 0:TT], func=AF.Sigmoid), reads=[gab], writes=[s1b])
                fw.op("act", lambda h, s2=s2, gb=gb: h.activation(out=s2[:], in_=gb[:, 0:TT], func=AF.Sigmoid), reads=[gbb], writes=[s2b])
                fw.op("dve", lambda h, s1=s1, ua=ua: h.tensor_tensor(out=s1[:], in0=s1[:], in1=ua[:, 0:TT], op=ALU.mult), reads=[s1b, uab], writes=[s1b])
                fw.op("dve", lambda h, s2=s2, ub=ub: h.tensor_tensor(out=s2[:], in0=s2[:], in1=ub[:, 0:TT], op=ALU.mult), reads=[s2b, ubb], writes=[s2b])
                fw.op("dve", lambda h, s1=s1, s2=s2, cb=cb: h.tensor_tensor(out=big[:, 2 * MC + cb, :], in0=s1[:], in1=s2[:], op=ALU.add), reads=[s1b, s2b], writes=[b_mx[cb]])
            for cb in range(KC):
                bk, bb = acc(dr["w_out"], cb, KC, lambda k: big[:, 2 * MC + k, :], b_mx)
                fw.op("dve", lambda h, cb=cb, bk=bk: h.tensor_tensor(out=xT[:, cb, :], in0=xT[:, cb, :], in1=bk[:, 0:TT], op=ALU.add), reads=[bb, b_x[cb]], writes=[b_x[cb]])
            if "dbg1" in dr:
                for k in range(KC):
                    fw.dma("sp", "y", lambda h, k=k, ts=ts: h.dma_start(out=dr["dbg1"][k * 128:(k + 1) * 128, ts], in_=xT[:, k, :]), reads=[b_x[k]], writes=[b_y])
                    fw.dma("sp", "y", lambda h, k=k, ts=ts: h.dma_start(out=dr["dbg0"][k * 128:(k + 1) * 128, ts], in_=big[:, 2 * MC + k, :]), reads=[b_mx[k]], writes=[b_y])
            norm_to_h(1)
            for h0 in range(0, HC, HH):
                for j in range(HH):
                    bk, bb = acc(dr["w_mi"], h0 + j, KC, hr, b_h)
                    r, rb = tmp.next()
                    fw.op("act", lambda h, r=r, bk=bk: h.activation(out=r[:], in_=bk[:, 0:TT], func=AF.Relu), reads=[bb], writes=[rb])
                    fw.op("dve", lambda h, r=r, j=j: h.tensor_tensor(out=big[:, j, :], in0=r[:], in1=r[:], op=ALU.mult), reads=[rb], writes=[b_hd[j]])
                for cb in range(KC):
                    bk, bb = acc(dr["w_mo"], cb, HH, lambda k: big[:, k, :], b_hd, k0=h0)
                    fw.op("dve", lambda h, cb=cb, bk=bk: h.tensor_tensor(out=xT[:, cb, :], in0=xT[:, cb, :], in1=bk[:, 0:TT], op=ALU.add), reads=[bb, b_x[cb]], writes=[b_x[cb]])
            if "dbg2" in dr:
                for k in range(KC):
                    fw.dma("sp", "y", lambda h, k=k, ts=ts: h.dma_start(out=dr["dbg2"][k * 128:(k + 1) * 128, ts], in_=xT[:, k, :]), reads=[b_x[k]], writes=[b_y])
            norm_to_h(2)
            for cb in range(KC):
                g_, gb_ = acc(dr["w_pg"], cb, KC, hr, b_h)
                p_, pb_ = acc(dr["w_pp"], cb, PC, lambda k: pTb[:, k, :], [b_p] * PC)
                s1, s1b = tmp.next()
                y, yb = yt.next()
                fw.op("act", lambda h, s1=s1, g_=g_: h.activation(out=s1[:], in_=g_[:, 0:TT], func=AF.Sigmoid), reads=[gb_], writes=[s1b])
                fw.op("dve", lambda h, s1=s1, p_=p_: h.tensor_tensor(out=s1[:], in0=s1[:], in1=p_[:, 0:TT], op=ALU.mult), reads=[s1b, pb_], writes=[s1b])
                fw.op("dve", lambda h, s1=s1, y=y, cb=cb: h.tensor_tensor(out=y[:], in0=s1[:], in1=xT[:, cb, :], op=ALU.add), reads=[s1b, b_x[cb]], writes=[yb])
                fw.dma("sp", "y", lambda h, y=y, cb=cb, ts=ts: h.dma_start(out=dr["yT"][cb * 128:(cb + 1) * 128, ts], in_=y[:]), reads=[yb], writes=[b_y])
        fw.flush(barrier=True)
    return b_y


def dram_decls(nc, cfg, debug_feed_mix=False, debug_out=False):
    D, S, TO, KC, HID, MW, PLE = cfg.D, cfg.S, cfg.TO, cfg.KC, cfg.HID, cfg.MW, cfg.PLE
    NCP, NSB = S // 16, S // 64
    ext = lambda n, s, d=F32: nc.dram_tensor(n, list(s), d, kind="ExternalInput").ap()
    scr = lambda n, s, d=BF16: nc.dram_tensor(n, list(s), d, kind="Internal").ap()
    dr = {"xTpad": ext("xTpad", [D, S]), "gmix_col": ext("gmix_col", [128, KC]), "pT": ext("pT", [PLE, TO]), "gcols": ext("gcols", [128, 3, KC])}
    for n_ in ("w_kc", "w_vc", "w_ks", "w_kw"):
        dr[n_] = ext(n_, [4, 128, KC, 128])
    for n_ in ("w_q", "w_gq", "w_gk", "w_gv", "w_gz"):
        dr[n_] = ext(n_, [16, 128, KC, 128])
    dr["w_v_t"] = ext("w_v_t", [128, KC, 1024]); dr["w_gate_t"] = ext("w_gate_t", [128, KC, 48])
    dr["w_bd"] = ext("w_bd", [128, KC, 32])
    dr["kgain_cols"] = ext("kgain_cols", [128, 3]); dr["kcg_bc"] = ext("kcg_bc", [32, 128]); dr["qgain_col"] = ext("qgain_col", [128, 1])
    dr["w1k"] = ext("w1k", [128, 32, 512]); dr["w1v"] = ext("w1v", [128, 32, 512]); dr["w2k"] = ext("w2k", [128, 4, 128]); dr["w2v"] = ext("w2v", [128, 4, 128])
    dr["pek"] = ext("pek", [128, 32]); dr["pev"] = ext("pev", [128, 32])
    for n_, s_ in (("cosk", [128, S]), ("sink", [128, S]), ("cosc", [NCP, 64]), ("sinc", [NCP, 64]), ("cthr_row", [128, NCP]), ("cthr_col", [128, max(NCP // 128, 1)]),
                   ("tq_row", [128, TO]), ("tq_col", [128, TO // 128]), ("j_row", [128, NSB]), ("curp_col", [128, TO // 128]), ("validj_row", [128, NSB]), ("e0_row", [128, NSB]),
                   ("kvalid_col", [128, S // 128]), ("ebig", [NSB, S]), ("caus", [128, 4, 512]), ("band", [128, 8, 512]), ("prot", [128, 128]), ("ident128", [128, 128]),
                   ("gcst", [64, 5, 64]), ("cw", [128, 16, 3, 4]), ("adt", [64, 2, 16]), ("ogain", [64, 128])):
        dr[n_] = ext(n_, s_)
    dr["KS"] = scr("KS", [4, 128, S]); dr["KW"] = scr("KW", [4, 128, S])
    dr["VS"] = scr("VS", [4, 128, S // 128, 132]); dr["VW"] = scr("VW", [4, 128, S // 128, 132])
    dr["KC"] = scr("KC", [4, 128, NCP]); dr["VC"] = scr("VC", [4, 128, max(NCP // 128, 1), 132])
    for n_ in ("w_ga", "w_gb", "w_out", "w_pg"):
        dr[n_] = ext(n_, [KC, 128, KC, 128])
    dr["w_upa"] = ext("w_upa", [KC, 128, MW // 128, 128]); dr["w_upb"] = ext("w_upb", [KC, 128, MW // 128, 128])
    dr["w_mi"] = ext("w_mi", [HID // 128, 128, KC, 128]); dr["w_mo"] = ext("w_mo", [KC, 128, HID // 128, 128]); dr["w_pp"] = ext("w_pp", [KC, 128, PLE // 128, 128])
    dr["yT"] = nc.dram_tensor("yT", [D, TO], F32, kind="ExternalOutput").ap()
    if debug_feed_mix:
        dr["oaT"] = ext("oaT", [MW, TO], BF16); dr["obT"] = ext("obT", [MW, TO], BF16)
    else:
        mk = (lambda n, s_: nc.dram_tensor(n, list(s_), BF16, kind="ExternalOutput").ap()) if debug_out else scr
        dr["oaT"] = mk("oaT", [MW, TO]); dr["obT"] = mk("obT", [MW, TO])
        if debug_out:
            dr["dbg0"] = nc.dram_tensor("dbg0", [D, TO], BF16, kind="ExternalOutput").ap()
            dr["dbg1"] = nc.dram_tensor("dbg1", [D, TO], F32, kind="ExternalOutput").ap()
            dr["dbg2"] = nc.dram_tensor("dbg2", [D, TO], F32, kind="ExternalOutput").ap()
    return dr


def build_program(cfg, debug_feed_mix=False, debug_out=False):
    nc = bass.Bass("TRN2", target_bir_lowering=False)
    dr = dram_decls(nc, cfg, debug_feed_mix, debug_out)
    with ExitStack() as es:
        fw = FW(nc, es)
        if not debug_feed_mix:
            dr["_b_scr"] = emit_nsa_k(nc, fw, cfg, dr)
            emit_nsa_q(nc, fw, cfg, dr)
            emit_gdn(nc, fw, cfg, dr)
        emit_tail(nc, fw, cfg, dr)
    return nc


def tail_inputs(cfg, inp, c):
    D, TO, G = cfg.D, cfg.TO, cfg.G
    b, g = c // G, c % G
    t0 = g * TO
    off = 13392
    w_in = inp["w_in"][0]
    return {
        "pT": np.ascontiguousarray(inp["p"][0, b, t0:t0 + TO].T),
        "gcols": np.ascontiguousarray(np.stack([col_vec(inp["g_mix"][0]), col_vec(inp["g_mlp"][0]), col_vec(inp["g_ple"][0])], axis=1)),
        "w_ga": blk_w(w_in[:, off:off + D]), "w_gb": blk_w(w_in[:, off + D:off + 2 * D]),
        "w_upa": blk_w(inp["w_up_nsa"][0]), "w_upb": blk_w(inp["w_up_gdn"][0]), "w_out": blk_w(inp["w_out"][0]),
        "w_mi": blk_w(inp["w_mlp_in"][0]), "w_mo": blk_w(inp["w_mlp_out"][0]),
        "w_pg": blk_w(inp["w_ple_gate"][0]), "w_pp": blk_w(inp["w_ple_proj"][0]),
    }


def core_inputs(cfg, inp, c):
    m = {}
    m.update(nsa_inputs(cfg, inp, c))
    m.update(gdn_inputs(cfg, inp, c))
    m.update(tail_inputs(cfg, inp, c))
    return m


def percore_inputs(cfg, inp, c):
    b, g = c // cfg.G, c % cfg.G
    n = (g + 1) * cfg.TO
    xp = np.zeros((cfg.D, cfg.S), np.float32)
    xp[:, cfg.S - n:] = inp["x"][b, :n].T
    m = {"xTpad": xp, "pT": np.ascontiguousarray(inp["p"][0, b, g * cfg.TO:(g + 1) * cfg.TO].T)}
    m.update(nsa_tables(cfg, g))
    return m


_PROG = {}


def run_cfg(cfg, inputs):
    key = (cfg.D, cfg.S, cfg.NB, cfg.HID)
    if key not in _PROG:
        _PROG[key] = build_program(cfg)
    nc = _PROG[key]
    ncores = cfg.NB * cfg.G
    inp = {k_: np.asarray(v) for k_, v in inputs.items()}
    base = core_inputs(cfg, inp, 0)
    maps = [base]
    for c in range(1, ncores):
        m = dict(base)
        m.update(percore_inputs(cfg, inp, c))
        maps.append(m)
    res = run_bass_kernel_spmd(nc, maps, core_ids=list(range(ncores)))
    out = np.empty((cfg.NB, cfg.S, cfg.D), np.float32)
    for c in range(ncores):
        b, g_ = c // cfg.G, c % cfg.G
        out[b, g_ * cfg.TO:(g_ + 1) * cfg.TO] = np.asarray(res.results[c]["yT"]).T
    return out


def kernel(**inputs):
    return run_cfg(Cfg(), inputs)
```

```python
from contextlib import ExitStack
import numpy as np
import concourse.bass as bass
import concourse.mybir as mybir
from concourse.bass_utils import run_bass_kernel_spmd

F32 = mybir.dt.float32
BF16 = mybir.dt.bfloat16
ALU = mybir.AluOpType
AF = mybir.ActivationFunctionType
EPS = 1e-6


class Cfg:
    def __init__(s, D=4096, S=8192, NB=2, HID=16384, PLE=256, G=4):
        s.D, s.S, s.NB, s.HID, s.PLE, s.G = D, S, NB, HID, PLE, G
        s.TO = S // G
        s.KC = D // 128
        s.TT = 512
        s.MW = 2048


class Lane:
    def __init__(self, name, sem, step):
        self.name, self.sem, self.step, self.count = name, sem, step, 0


class Buf:
    __slots__ = ("name", "w", "r", "lane")

    def __init__(self, name=""):
        self.name, self.w, self.r, self.lane = name, None, {}, None


class Eng:
    def __init__(self, name, lane):
        self.name, self.lane, self.waited, self.prog = name, lane, {}, []


class FW:
    ENG = ("pe", "act", "dve", "pool", "sp")

    def __init__(self, nc, es):
        self.nc, self.es = nc, es
        self.eng = {}
        for n in self.ENG:
            sem = es.enter_context(nc.semaphore("s_" + n))
            self.eng[n] = Eng(n, Lane(n, sem, 1))
        self.lanes = {}

    def dma_lane(self, name):
        if name not in self.lanes:
            sem = self.es.enter_context(self.nc.semaphore("d_" + name))
            self.lanes[name] = Lane(name, sem, 16)
        return self.lanes[name]

    def _deps(self, e, reads, writes, pe_acc=False):
        need = {}

        def add(lv):
            if lv is not None and need.get(lv[0], 0) < lv[1]:
                need[lv[0]] = lv[1]
        for b in reads:
            add(b.w)
        for b in writes:
            if not (pe_acc and b.w is not None and b.w[0] is e.lane):
                add(b.w)
            for lane, v in b.r.items():
                add((lane, v))
        waits = []
        for lane, v in need.items():
            if e.waited.get(lane, 0) < v:
                e.waited[lane] = v
                waits.append((lane.sem, v))
        return waits

    @staticmethod
    def _mark(lane, val, reads, writes):
        for b in reads:
            if b.r.get(lane, 0) < val:
                b.r[lane] = val
        for b in writes:
            b.w, b.r = (lane, val), {}

    def op(self, ename, fn, reads=(), writes=(), pe_acc=False):
        e = self.eng[ename]
        waits = self._deps(e, reads, writes, pe_acc)
        e.lane.count += 1
        sem = e.lane.sem

        def run(h, waits=waits, fn=fn, sem=sem):
            for s, v in waits:
                h.wait_ge(s, v)
            fn(h).then_inc(sem, 1)
        e.prog.append(run)
        self._mark(e.lane, e.lane.count, reads, writes)

    def dma(self, qname, lane_name, fn, reads=(), writes=()):
        e = self.eng[qname]
        b0 = writes[0]
        if b0.lane is None:
            self._nl = getattr(self, "_nl", 0) + 1
            b0.lane = self.dma_lane("%s_%d" % (lane_name, self._nl))
        lane = b0.lane
        waits = self._deps(e, reads, writes)
        lane.count += 1
        sem = lane.sem

        def run(h, waits=waits, fn=fn, sem=sem):
            for s, v in waits:
                h.wait_ge(s, v)
            fn(h).then_inc(sem, 16)
        e.prog.append(run)
        self._mark(lane, lane.count * 16, reads, writes)

    def flush(self, barrier=True):
        if barrier:
            finals = [(e.lane.sem, e.lane.count, e.lane) for e in self.eng.values() if e.lane.count]
            finals += [(l.sem, l.count * 16, l) for l in self.lanes.values() if l.count]
            for e in self.eng.values():
                ws = []
                for sem, v, lane in finals:
                    if lane is e.lane:
                        continue
                    if e.waited.get(lane, 0) < v:
                        e.waited[lane] = v
                        ws.append((sem, v))

                def bar(h, ws=ws):
                    for s, v in ws:
                        h.wait_ge(s, v)
                e.prog.append(bar)
        progs = {n: self.eng[n].prog for n in self.ENG}
        for n in self.ENG:
            self.eng[n].prog = []
        with self.nc.Block() as block:
            @block.tensor
            def _(h):
                for f in progs["pe"]:
                    f(h)

            @block.scalar
            def _(h):
                for f in progs["act"]:
                    f(h)

            @block.vector
            def _(h):
                for f in progs["dve"]:
                    f(h)

            @block.gpsimd
            def _(h):
                for f in progs["pool"]:
                    f(h)

            @block.sync
            def _(h):
                for f in progs["sp"]:
                    f(h)


class Ring:
    def __init__(self, es, nc, name, shape, dt, n, psum=False):
        mk = nc.psum_tensor if psum else nc.sbuf_tensor
        self.t = [es.enter_context(mk("%s%d" % (name, i), list(shape), dt)) for i in range(n)]
        self.b = [Buf("%s%d" % (name, i)) for i in range(n)]
        self.i = -1

    def next(self):
        self.i = (self.i + 1) % len(self.t)
        return self.t[self.i], self.b[self.i]


def blk_w(w):
    K, N = w.shape
    return np.ascontiguousarray(w.reshape(K // 128, 128, N // 128, 128).transpose(2, 1, 0, 3))


def col_vec(v):
    return np.ascontiguousarray(v.reshape(-1, 128).T)


SCALE = 128.0 ** -0.5
BIG = 1.0e4
GK = 2.0 * (2.0 / np.pi) ** 0.5


def rope_tab(t):
    inv = (np.float32(10000.0) ** (-np.arange(64, dtype=np.float32) / np.float32(64))).astype(np.float32)
    ang = t.astype(np.float32)[:, None] * inv[None, :]
    return np.cos(ang).astype(np.float32), np.sin(ang).astype(np.float32)


def nsa_tables(cfg, g):
    S, TO = cfg.S, cfg.TO
    PAD = S - (g + 1) * TO
    idx = np.arange(S)
    t = (idx - PAD).astype(np.float32)
    c, s = rope_tab(t)
    cosk = np.ascontiguousarray(np.concatenate([c, c], 1).T)
    sink = np.ascontiguousarray(np.concatenate([s, s], 1).T)
    NCP = S // 16
    n_true = np.arange(NCP) - PAD // 16
    cend = (16 * n_true + 31).astype(np.float32)
    cc, sc = rope_tab(cend)
    thr = np.where(n_true >= 0, cend, 1e9).astype(np.float32)
    q_idx = np.arange(S - TO, S)
    tq = (q_idx - PAD).astype(np.float32)
    NSB = S // 64
    j0 = PAD // 64
    jp = np.arange(NSB)
    bc = lambda v: np.ascontiguousarray(np.broadcast_to(v[None, :].astype(np.float32), (128, v.shape[0])))
    colf = lambda v: np.ascontiguousarray(v.astype(np.float32).reshape(-1, 128).T)
    return {
        "cosk": cosk, "sink": sink, "cosc": cc, "sinc": sc,
        "cthr_row": bc(thr), "cthr_col": colf(thr),
        "tq_row": bc(tq), "tq_col": colf(tq),
        "j_row": bc(jp), "curp_col": colf(q_idx // 64), "validj_row": bc((jp >= j0)), "e0_row": bc((jp == j0)),
        "kvalid_col": colf((idx >= PAD)),
    }


def nsa_consts(cfg):
    S = cfg.S
    NSB = S // 64
    Eb = (np.arange(S)[None, :] // 64 == np.arange(NSB)[:, None]).astype(np.float32)
    p = np.arange(128)[:, None, None]
    rel = np.arange(8)[None, :, None]
    q = np.arange(512)[None, None, :]
    caus = ((128 * rel[:, :4] + p) <= q).astype(np.float32)
    d = q - (128 * (rel - 4) + p)
    band = ((d >= 0) & (d < 512)).astype(np.float32)
    prot = np.zeros((128, 128), np.float32)
    for m in range(64):
        prot[m + 64, m] = -1.0
        prot[m, m + 64] = 1.0
    return {"ebig": Eb, "caus": np.ascontiguousarray(caus), "band": np.ascontiguousarray(band), "prot": prot,
            "ident128": np.eye(128, dtype=np.float32)}


class Common:
    def __init__(self, nc, fw, cfg, dr, es, pfx, with_bkb=True, nw=3, nx=3):
        self.nc, self.fw, self.cfg, self.dr = nc, fw, cfg, dr
        KC = cfg.KC
        sb = lambda n, s, d: es.enter_context(nc.sbuf_tensor(pfx + n, list(s), d))
        self.sb = sb
        self.hT = sb("h", [128, KC, 512], BF16); self.b_h = [Buf() for _ in range(KC)]
        self.gcol = sb("g", [128, KC], F32); self.b_g = Buf()
        self.ones = sb("ones", [128, 128], BF16); self.b_c = Buf()
        self.epsc = sb("eps", [128, 1], F32)
        self.rstd = sb("rstd", [128, 512], F32); self.b_rstd = Buf()
        self.xr = Ring(es, nc, pfx + "xr", [128, 512], F32, nx)
        self.sq = Ring(es, nc, pfx + "sq", [128, 512], BF16, 2)
        self.wring = Ring(es, nc, pfx + "w", [128, KC, 128], BF16, nw)
        self.bank = Ring(es, nc, pfx + "bk", [128, 512], F32, 6, psum=True)
        self.bkb = Ring(es, nc, pfx + "bkb", [128, 1024], BF16, 2, psum=True) if with_bkb else None
        fw.dma("sp", "cst", lambda h: h.dma_start(out=self.gcol[:], in_=dr["gmix_col"]), writes=[self.b_g])
        fw.op("dve", lambda h: h.memset(self.ones[:], 1.0), writes=[self.b_c])
        fw.op("dve", lambda h: h.memset(self.epsc[:], EPS), writes=[self.b_g])

    def make_h(self, col0):
        fw, KC, D = self.fw, self.cfg.KC, self.cfg.D
        ts = slice(col0, col0 + 512)
        bk, bb = self.bank.next()
        for k in range(KC):
            x, xb = self.xr.next()
            fw.dma("sp", "x", lambda h, k=k, x=x, ts=ts: h.dma_start(out=x[:], in_=self.dr["xTpad"][k * 128:(k + 1) * 128, ts]), writes=[xb])
            st, sbf = self.sq.next()
            fw.op("act", lambda h, x=x, st=st: h.activation(out=st[:], in_=x[:], func=AF.Square), reads=[xb], writes=[sbf])
            fw.op("pe", lambda h, k=k, st=st, bk=bk: h.matmul(bk[:, :], lhsT=self.ones[:], rhs=st[:], start=(k == 0), stop=(k == KC - 1)),
                  reads=[self.b_c, sbf], writes=[bb], pe_acc=(k > 0))
        fw.op("act", lambda h, bk=bk: h.activation(out=self.rstd[:], in_=bk[:, :], func=AF.Sqrt, bias=self.epsc[:], scale=1.0 / D), reads=[bb, self.b_g], writes=[self.b_rstd])
        fw.op("dve", lambda h: h.reciprocal(out=self.rstd[:], in_=self.rstd[:]), reads=[self.b_rstd], writes=[self.b_rstd])
        for k in range(KC):
            x, xb = self.xr.next()
            fw.dma("sp", "x", lambda h, k=k, x=x, ts=ts: h.dma_start(out=x[:], in_=self.dr["xTpad"][k * 128:(k + 1) * 128, ts]), writes=[xb])
            fw.op("dve", lambda h, k=k, x=x: h.scalar_tensor_tensor(out=self.hT[:, k, :], in0=x[:], scalar=self.gcol[:, k:k + 1], in1=self.rstd[:], op0=ALU.mult, op1=ALU.mult),
                  reads=[xb, self.b_g, self.b_rstd], writes=[self.b_h[k]])

    def proj_f(self, wd, cb):
        fw, KC = self.fw, self.cfg.KC
        wt, wb = self.wring.next()
        fw.dma("pool", "w", lambda h: h.dma_start(out=wt[:], in_=wd[cb]), writes=[wb])
        bk, bb = self.bank.next()
        for k in range(KC):
            fw.op("pe", lambda h, k=k: h.matmul(bk[:, :], lhsT=wt[:, k, :], rhs=self.hT[:, k, :], start=(k == 0), stop=(k == KC - 1)),
                  reads=[wb, self.b_h[k]], writes=[bb], pe_acc=(k > 0))
        return bk, bb

    def mm(self, out, lhsT, rhs, rd, wr, start=True, stop=True, acc=False):
        self.fw.op("pe", lambda h: h.matmul(out, lhsT=lhsT, rhs=rhs, start=start, stop=stop), reads=rd, writes=[wr], pe_acc=acc)

    def tr(self, out, in_, ident, rd, wr):
        self.fw.op("pe", lambda h: h.transpose(out, in_, ident), reads=rd, writes=[wr])

    def rsq(self, ss, ssb, n, scale):
        fw = self.fw
        fw.op("act", lambda h: h.activation(out=ss, in_=ss, func=AF.Sqrt, bias=self.epsc[0:n, :], scale=scale), reads=[ssb, self.b_g], writes=[ssb])
        fw.op("dve", lambda h: h.reciprocal(out=ss, in_=ss), reads=[ssb], writes=[ssb])

    def norm_rope_f(self, bk, bb, gain_col, cosT, sinT, b_tab, prot, out, outb, scratch):
        fw = self.fw
        f32r, b16r = scratch
        xq, xqb = f32r.next()
        fw.op("act", lambda h: h.activation(out=xq[:], in_=bk[:, :], func=AF.Copy), reads=[bb], writes=[xqb])
        s2, s2b = b16r.next()
        fw.op("act", lambda h: h.activation(out=s2[:], in_=bk[:, :], func=AF.Square), reads=[bb], writes=[s2b])
        b2, b2b = self.bank.next()
        self.mm(b2[:, :], self.ones[:], s2[:], [self.b_c, s2b], b2b)
        rn, rnb = f32r.next()
        fw.op("act", lambda h: h.activation(out=rn[:], in_=b2[:, :], func=AF.Sqrt, bias=self.epsc[:], scale=1.0 / 128), reads=[b2b, self.b_g], writes=[rnb])
        fw.op("dve", lambda h: h.reciprocal(out=rn[:], in_=rn[:]), reads=[rnb], writes=[rnb])
        fw.op("dve", lambda h: h.scalar_tensor_tensor(out=xq[:], in0=xq[:], scalar=gain_col, in1=rn[:], op0=ALU.mult, op1=ALU.mult), reads=[xqb, rnb, self.b_c], writes=[xqb])
        xb16, xb16b = b16r.next()
        fw.op("act", lambda h: h.activation(out=xb16[:], in_=xq[:], func=AF.Copy), reads=[xqb], writes=[xb16b])
        b3, b3b = self.bank.next()
        self.mm(b3[:, :], prot, xb16[:], [self.b_c, xb16b], b3b)
        fw.op("dve", lambda h: h.tensor_tensor(out=xq[:], in0=xq[:], in1=cosT, op=ALU.mult), reads=[xqb, b_tab], writes=[xqb])
        fw.op("dve", lambda h: h.tensor_tensor(out=rn[:], in0=b3[:, :], in1=sinT, op=ALU.mult), reads=[b3b, b_tab], writes=[rnb])
        fw.op("dve", lambda h: h.tensor_tensor(out=out, in0=xq[:], in1=rn[:], op=ALU.add), reads=[xqb, rnb], writes=[outb])


def emit_nsa_k(nc, fw, cfg, dr):
    D, KC, S = cfg.D, cfg.KC, cfg.S
    NBLK = S // 512
    b_scr = Buf()
    with ExitStack() as es:
        cm = Common(nc, fw, cfg, dr, es, "n1_", nw=2, nx=2)
        sb = cm.sb
        gains = sb("gains", [128, 3], F32)
        kcg = sb("kcg", [32, 128], F32)
        prot = sb("prot", [128, 128], BF16)
        idb = sb("idb", [128, 128], BF16)
        wvr = Ring(es, nc, "n1_wv", [128, KC, 512], BF16, 1)
        w1 = {"k": sb("w1k", [128, 32, 512], BF16), "v": sb("w1v", [128, 32, 512], BF16)}
        w2 = {"k": sb("w2k", [128, 4, 128], BF16), "v": sb("w2v", [128, 4, 128], BF16)}
        cbrow = {"k": sb("cbrk", [32, 512], F32), "v": sb("cbrv", [32, 512], F32)}
        perep = {"k": sb("perk", [128, 32, 32], BF16), "v": sb("perv", [128, 32, 32], BF16)}
        b_w1 = Buf()
        tabs = Ring(es, nc, "n1_tab", [128, 2, 512], F32, 1)
        ctab = Ring(es, nc, "n1_ctab", [32, 2, 64], F32, 2)
        f32r = Ring(es, nc, "n1_f32", [128, 512], F32, 3)
        b16r = Ring(es, nc, "n1_b16", [128, 512], BF16, 3)
        outr = Ring(es, nc, "n1_out", [128, 512], BF16, 2)
        vout = Ring(es, nc, "n1_vout", [128, 8, 132], BF16, 4)
        cT = {(gq, kv): sb("cT%d%s" % (gq, kv), [128, 16 + 512], BF16) for gq in range(4) for kv in "kv"}
        b_cT = {k: Buf() for k in cT}
        hid = Ring(es, nc, "n1_hid", [128, 4, 32], BF16, 2)
        g32 = Ring(es, nc, "n1_g32", [32, 512], F32, 2)
        gb32 = Ring(es, nc, "n1_gb32", [32, 512], BF16, 2)
        t32 = Ring(es, nc, "n1_t32", [32, 128], F32, 4)
        tb32 = Ring(es, nc, "n1_tb32", [32, 132], BF16, 3)
        c32 = Ring(es, nc, "n1_c32", [32, 2], F32, 4)
        kco = Ring(es, nc, "n1_kco", [128, 32], BF16, 2)
        zc = sb("zc", [128, 16], BF16)
        zv = sb("zv", [1, 132], BF16)

        fw.dma("sp", "cst", lambda h: h.dma_start(out=gains[:], in_=dr["kgain_cols"]), writes=[cm.b_c])
        fw.dma("sp", "cst", lambda h: h.dma_start(out=kcg[:], in_=dr["kcg_bc"]), writes=[cm.b_c])
        fw.dma("pool", "w", lambda h: h.dma_start(out=prot[:], in_=dr["prot"]), writes=[cm.b_c])
        fw.dma("pool", "w", lambda h: h.dma_start(out=idb[:], in_=dr["ident128"]), writes=[cm.b_c])
        for kv in "kv":
            for l0 in range(0, 32, 4):
                fw.dma("pool", "w", lambda h, kv=kv, l0=l0: h.dma_start(out=w1[kv][:, l0:l0 + 4, :], in_=dr["w1" + kv][:, l0:l0 + 4, :]), writes=[b_w1])
            fw.dma("pool", "w", lambda h, kv=kv: h.dma_start(out=w2[kv][:], in_=dr["w2" + kv]), writes=[b_w1])
            fw.dma("pool", "w", lambda h, kv=kv: h.dma_start(out=perep[kv][:], in_=dr["perep" + kv]), writes=[b_w1])
        for k in cT:
            fw.op("dve", lambda h, k=k: h.memset(cT[k][:, 0:16], 0.0), writes=[b_cT[k]])
        fw.op("dve", lambda h: h.memset(zc[:], 0.0), writes=[cm.b_c])
        fw.op("dve", lambda h: h.memset(zv[:], 0.0), writes=[cm.b_c])
        for i_ in range(2):
            fw.op("dve", lambda h, i_=i_: h.memset(ctab.t[i_][:], 0.0), writes=[ctab.b[i_]])
        for i_ in range(4):
            fw.op("dve", lambda h, i_=i_: h.memset(vout.t[i_][:, :, 128:132], 1.0), writes=[vout.b[i_]])
        for i_ in range(3):
            fw.op("dve", lambda h, i_=i_: h.memset(tb32.t[i_][:, 128:132], 1.0), writes=[tb32.b[i_]])
        for kv in "kv":
            bk, bb = cm.bank.next()
            for l in range(32):
                cm.mm(bk[0:32, :], perep[kv][:, l, :], w1[kv][:, l, :], [b_w1], bb, start=(l == 0), stop=(l == 31), acc=(l > 0))
            fw.op("act", lambda h, kv=kv, bk=bk: h.activation(out=cbrow[kv][:], in_=bk[0:32, :], func=AF.Copy), reads=[bb], writes=[b_w1])
        for gq in range(4):
            fw.dma("sp", "scr", lambda h, gq=gq: h.dma_start(out=dr["KC"][gq][:, S // 16 - 16:S // 16], in_=zc[:]), reads=[cm.b_c], writes=[b_scr])
            fw.dma("sp", "scr", lambda h, gq=gq: h.dma_start(out=dr["VC"][gq][127:128, (S // 16 - 1) // 128, :], in_=zv[0:1, :]), reads=[cm.b_c], writes=[b_scr])

        for blk in range(NBLK):
            c0 = blk * 512
            cm.make_h(c0)
            tb, tbb = tabs.next()
            fw.dma("sp", "tab", lambda h, tb=tb, c0=c0: h.dma_start(out=tb[:, 0, :], in_=dr["cosk"][:, c0:c0 + 512]), writes=[tbb])
            fw.dma("sp", "tab", lambda h, tb=tb, c0=c0: h.dma_start(out=tb[:, 1, :], in_=dr["sink"][:, c0:c0 + 512]), writes=[tbb])
            ct, ctb = ctab.next()
            n0 = c0 // 16 - 1
            j_lo = 1 if blk == 0 else 0
            fw.dma("sp", "tab", lambda h, ct=ct, n0=n0, j_lo=j_lo: h.dma_start(out=ct[j_lo:32, 0, :], in_=dr["cosc"][n0 + j_lo:n0 + 32, :]), writes=[ctb])
            fw.dma("sp", "tab", lambda h, ct=ct, n0=n0, j_lo=j_lo: h.dma_start(out=ct[j_lo:32, 1, :], in_=dr["sinc"][n0 + j_lo:n0 + 32, :]), writes=[ctb])
            vos = [vout.next() for _ in range(4)]
            for half in range(2):
                wv, b_wv = wvr.next()
                fw.dma("pool", "w", lambda h, wv=wv, half=half: h.dma_start(out=wv[:], in_=dr["w_v_t"][:, :, half * 512:(half + 1) * 512]), writes=[b_wv])
                for tl in range(4):
                    vo, vob = vos[tl]
                    bk, bb = cm.bank.next()
                    for k in range(KC):
                        cm.mm(bk[:, :], cm.hT[:, k, tl * 128:(tl + 1) * 128], wv[:, k, :], [cm.b_h[k], b_wv], bb, start=(k == 0), stop=(k == KC - 1), acc=(k > 0))
                    fw.op("act", lambda h, vo=vo, bk=bk, half=half: h.activation(out=vo[:, half * 4:(half + 1) * 4, 0:128], in_=bk[:, :].rearrange("p (g c) -> p g c", c=128), func=AF.Copy), reads=[bb], writes=[vob])
            for tl in range(4):
                vo, vob = vos[tl]
                tile_ = c0 // 128 + tl
                for g4 in range(4):
                    fw.dma("sp", "scr", lambda h, vo=vo, g4=g4, tile_=tile_: h.dma_start(out=dr["VS"][g4][:, tile_, :], in_=vo[:, g4, :]), reads=[vob], writes=[b_scr])
                    fw.dma("sp", "scr", lambda h, vo=vo, g4=g4, tile_=tile_: h.dma_start(out=dr["VW"][g4][:, tile_, :], in_=vo[:, 4 + g4, :]), reads=[vob], writes=[b_scr])
            for gq in range(4):
                for nm, gi, dst in (("w_ks", 0, "KS"), ("w_kw", 1, "KW")):
                    bk, bb = cm.proj_f(dr[nm], gq)
                    o, ob = outr.next()
                    cm.norm_rope_f(bk, bb, gains[:, gi:gi + 1], tb[:, 0, :], tb[:, 1, :], tbb, prot[:], o[:], ob, (f32r, b16r))
                    fw.dma("sp", "scr", lambda h, o=o, dst=dst, gq=gq, c0=c0: h.dma_start(out=dr[dst][gq][:, c0:c0 + 512], in_=o[:]), reads=[ob], writes=[b_scr])
                for kv, nm in (("k", "w_kc"), ("v", "w_vc")):
                    bk, bb = cm.proj_f(dr[nm], gq)
                    t_, tb_ = cT[(gq, kv)], b_cT[(gq, kv)]
                    fw.op("act", lambda h, t_=t_, bk=bk: h.activation(out=t_[:, 16:16 + 512], in_=bk[:, :], func=AF.Copy), reads=[bb], writes=[tb_])
                    hd_, hdb = hid.next()
                    b2, b2b = cm.bank.next()
                    for l in range(32):
                        cm.mm(b2[0:32, :], t_[:, l:l + 16 * 31 + 1:16], w1[kv][:, l, :], [b_w1, tb_], b2b, start=(l == 0), stop=(l == 31), acc=(l > 0))
                    x_, xb_ = g32.next(); y_, yb_ = g32.next()
                    fw.op("dve", lambda h, x_=x_, b2=b2, kv=kv: h.tensor_tensor(out=x_[:], in0=b2[0:32, :], in1=cbrow[kv][:], op=ALU.add), reads=[b2b, b_w1], writes=[xb_])
                    fw.op("dve", lambda h, x_=x_, y_=y_: h.tensor_tensor(out=y_[:], in0=x_[:], in1=x_[:], op=ALU.mult), reads=[xb_], writes=[yb_])
                    fw.op("dve", lambda h, y_=y_: h.tensor_scalar(out=y_[:], in0=y_[:], scalar1=0.044715, scalar2=1.0, op0=ALU.mult, op1=ALU.add), reads=[yb_], writes=[yb_])
                    fw.op("dve", lambda h, x_=x_, y_=y_: h.tensor_tensor(out=y_[:], in0=y_[:], in1=x_[:], op=ALU.mult), reads=[xb_, yb_], writes=[yb_])
                    fw.op("act", lambda h, y_=y_: h.activation(out=y_[:], in_=y_[:], func=AF.Sigmoid, scale=float(GK)), reads=[yb_], writes=[yb_])
                    hb16, hb16b = gb32.next()
                    fw.op("dve", lambda h, x_=x_, y_=y_, hb16=hb16: h.tensor_tensor(out=hb16[:], in0=x_[:], in1=y_[:], op=ALU.mult), reads=[xb_, yb_], writes=[hb16b])
                    pbh, pbhb = cm.bkb.next()
                    for hb in range(4):
                        cm.tr(pbh[:, hb * 32:(hb + 1) * 32], hb16[:, hb * 128:(hb + 1) * 128], idb[0:32, 0:32], [hb16b, cm.b_c], pbhb)
                    fw.op("act", lambda h, hd_=hd_, pbh=pbh: h.activation(out=hd_[:, :, :], in_=pbh[:, 0:128].rearrange("p (a n) -> p a n", a=4), func=AF.Copy), reads=[pbhb], writes=[hdb])
                    fw.op("dve", lambda h, t_=t_: h.tensor_copy(out=t_[:, 0:16], in_=t_[:, 512:528]), reads=[tb_], writes=[tb_])
                    b3, b3b = cm.bank.next()
                    for hb in range(4):
                        cm.mm(b3[0:32, 0:128], hd_[:, hb, :], w2[kv][:, hb, :], [hdb, b_w1], b3b, start=(hb == 0), stop=(hb == 3), acc=(hb > 0))
                    if kv == "v":
                        vc, vcb = tb32.next()
                        fw.op("act", lambda h, vc=vc, b3=b3: h.activation(out=vc[:, 0:128], in_=b3[0:32, 0:128], func=AF.Copy), reads=[b3b], writes=[vcb])
                        r0 = n0 + j_lo
                        while r0 < n0 + 32:
                            r1 = min(n0 + 32, (r0 // 128 + 1) * 128)
                            fw.dma("sp", "scr", lambda h, vc=vc, gq=gq, n0=n0, r0=r0, r1=r1: h.dma_start(out=dr["VC"][gq][r0 % 128:r0 % 128 + (r1 - r0), r0 // 128, :], in_=vc[r0 - n0:r1 - n0, :]), reads=[vcb], writes=[b_scr])
                            r0 = r1
                    else:
                        cs_, csb = c32.next()
                        jk, jkb = tb32.next()
                        fw.op("act", lambda h, jk=jk, b3=b3, cs_=cs_: h.activation(out=jk[:, 0:128], in_=b3[0:32, 0:128], func=AF.Square, accum_out=cs_[:, 0:1]), reads=[b3b], writes=[jkb, csb])
                        cm.rsq(cs_[:, 0:1], csb, 32, 1.0 / 128)
                        kn, knb = t32.next()
                        fw.op("dve", lambda h, kn=kn, b3=b3, cs_=cs_: h.scalar_tensor_tensor(out=kn[:], in0=b3[0:32, 0:128], scalar=cs_[:, 0:1], in1=kcg[:], op0=ALU.mult, op1=ALU.mult), reads=[b3b, csb, cm.b_c], writes=[knb])
                        r1, r1b = t32.next(); r2, r2b = t32.next()
                        fw.op("dve", lambda h, r1=r1, kn=kn, ct=ct: h.tensor_tensor(out=r1[:, 0:64], in0=kn[:, 0:64], in1=ct[:, 0, :], op=ALU.mult), reads=[knb, ctb], writes=[r1b])
                        fw.op("dve", lambda h, r1=r1, kn=kn, ct=ct: h.tensor_tensor(out=r1[:, 64:128], in0=kn[:, 64:128], in1=ct[:, 0, :], op=ALU.mult), reads=[knb, ctb], writes=[r1b])
                        fw.op("dve", lambda h, r2=r2, kn=kn, ct=ct: h.tensor_tensor(out=r2[:, 0:64], in0=kn[:, 64:128], in1=ct[:, 1, :], op=ALU.mult), reads=[knb, ctb], writes=[r2b])
                        fw.op("dve", lambda h, r2=r2, kn=kn, ct=ct: h.tensor_tensor(out=r2[:, 64:128], in0=kn[:, 0:64], in1=ct[:, 1, :], op=ALU.mult), reads=[knb, ctb], writes=[r2b])
                        kr, krb = tb32.next()
                        fw.op("dve", lambda h, kr=kr, r1=r1, r2=r2: h.tensor_tensor(out=kr[:, 0:64], in0=r1[:, 0:64], in1=r2[:, 0:64], op=ALU.subtract), reads=[r1b, r2b], writes=[krb])
                        fw.op("dve", lambda h, kr=kr, r1=r1, r2=r2: h.tensor_tensor(out=kr[:, 64:128], in0=r1[:, 64:128], in1=r2[:, 64:128], op=ALU.add), reads=[r1b, r2b], writes=[krb])
                        pb, pbb = cm.bkb.next()
                        cm.tr(pb[:, 0:32], kr[:, 0:128], idb[0:32, 0:32], [krb, cm.b_c], pbb)
                        ko, kob = kco.next()
                        fw.op("act", lambda h, ko=ko, pb=pb: h.activation(out=ko[:], in_=pb[:, 0:32], func=AF.Copy), reads=[pbb], writes=[kob])
                        fw.dma("sp", "scr", lambda h, ko=ko, gq=gq, n0=n0, j_lo=j_lo: h.dma_start(out=dr["KC"][gq][:, n0 + j_lo:n0 + 32], in_=ko[:, j_lo:32]), reads=[kob], writes=[b_scr])
        fw.flush(barrier=True)
    return b_scr


def emit_nsa_q(nc, fw, cfg, dr):
    D, KC, S, TO = cfg.D, cfg.KC, cfg.S, cfg.TO
    NOWN = TO // 512
    NCP, NSB, NKT = S // 16, S // 64, S // 128
    NCT = max(NCP // 128, 1)
    CW = min(NCP, 512)
    b_out = Buf()
    with ExitStack() as es:
        cm = Common(nc, fw, cfg, dr, es, "n2_", with_bkb=False, nw=2)
        sb = cm.sb
        qg = sb("qg", [128, 1], F32)
        prot = sb("prot", [128, 128], BF16)
        idb = sb("idb", [128, 128], BF16)
        idf = sb("idf", [128, 128], F32)
        wgt = sb("wgt", [128, KC, 48], BF16); b_wgt = Buf()
        ebig = sb("ebig", [NSB, S], BF16)
        caus = sb("caus", [128, 4, 512], BF16)
        band = sb("band", [128, 8, 512], BF16)
        tq_row = sb("tqr", [128, TO], F32)
        colt = sb("colt", [128, 3, max(TO // 128, NKT, NCT)], F32)
        kval = sb("kval", [128, NKT], F32)
        cthc = sb("cthc", [128, NCT], F32)
        rowt = sb("rowt", [128, 4, NSB], F32)
        cthr = sb("cthr", [128, NCP], F32)
        tabs = Ring(es, nc, "n2_tab", [128, 2, 512], F32, 1)
        f32r = Ring(es, nc, "n2_f32", [128, 512], F32, 4)
        b16r = Ring(es, nc, "n2_b16", [128, 512], BF16, 4)
        qT = [sb("qT%d" % i, [128, 512], BF16) for i in range(4)]; b_qT = [Buf() for _ in range(4)]
        gate = sb("gate", [128, 4, 48], F32); b_gate = Buf()
        KSs = sb("KSs", [128, S], BF16); b_KS = Buf()
        VSs = sb("VSs", [128, NKT, 132], BF16); b_VS = Buf()
        KWs = sb("KWs", [128, 1024], BF16); b_KW = Buf()
        VWs = sb("VWs", [128, 8, 132], BF16); b_VW = Buf()
        KCs = sb("KCs", [128, NCP], BF16); b_KC = Buf()
        VCs = sb("VCs", [128, NCT, 132], BF16); b_VC = Buf()
        mrow = sb("mrow", [128, 4, CW], F32); b_mrow = Buf()
        mcT = sb("mcT", [128, NCT, 512], BF16); b_mcT = Buf()
        mW = sb("mW", [128, 8, 512], BF16); b_mW = Buf()
        p4 = sb("p4", [128, NCP + 4], F32); b_p4 = Buf()
        selT = sb("selT", [NSB, 512], BF16); b_selT = Buf()
        mS = Ring(es, nc, "n2_mS", [128, 512], BF16, 3)
        pT = Ring(es, nc, "n2_pT", [128, 512], BF16, 4)
        pr = Ring(es, nc, "n2_pr", [128, CW], F32, 3)
        sc = Ring(es, nc, "n2_sc", [128, 4, NSB], F32, 2)
        c8 = Ring(es, nc, "n2_c8", [128, 16], F32, 6)
        oacc = sb("oacc", [128, 4, 128], F32); b_oacc = Buf()
        ob16 = Ring(es, nc, "n2_ob16", [128, 128], BF16, 2)
        oT = Ring(es, nc, "n2_oT", [128, 512], BF16, 2)
        accb = Ring(es, nc, "n2_acc", [128, 512], F32, 2, psum=True)

        ld = lambda dst, src, q="sp", lane="cst", wr=cm.b_c: fw.dma(q, lane, lambda h: h.dma_start(out=dst, in_=src), writes=[wr])
        ld(qg[:], dr["qgain_col"]); ld(tq_row[:], dr["tq_row"]); ld(colt[:, 0, 0:TO // 128], dr["tq_col"]); ld(colt[:, 1, 0:TO // 128], dr["curp_col"])
        ld(kval[:], dr["kvalid_col"]); ld(cthc[:], dr["cthr_col"]); ld(cthr[:], dr["cthr_row"]); ld(idf[:], dr["ident128"])
        ld(rowt[:, 0, :], dr["j_row"]); ld(rowt[:, 1, :], dr["validj_row"]); ld(rowt[:, 2, :], dr["e0_row"])
        for dst, nm in ((prot[:], "prot"), (idb[:], "ident128"), (caus[:], "caus"), (band[:], "band")):
            ld(dst, dr[nm], q="pool", lane="w")
        for e0 in range(0, S, 2048):
            ld(ebig[:, e0:e0 + 2048], dr["ebig"][:, e0:e0 + 2048], q="pool", lane="w")
        ld(wgt[:], dr["w_gate_t"], q="pool", lane="w", wr=b_wgt)
        fw.op("dve", lambda h: h.memset(p4[:], 0.0), writes=[b_p4])

        for ob_ in range(NOWN):
            blk = S // 512 - NOWN + ob_
            c0 = blk * 512
            cm.make_h(c0)
            tb, tbb = tabs.next()
            fw.dma("sp", "tab", lambda h, tb=tb, c0=c0: h.dma_start(out=tb[:, 0, :], in_=dr["cosk"][:, c0:c0 + 512]), writes=[tbb])
            fw.dma("sp", "tab", lambda h, tb=tb, c0=c0: h.dma_start(out=tb[:, 1, :], in_=dr["sink"][:, c0:c0 + 512]), writes=[tbb])
            qs = slice(ob_ * 512, ob_ * 512 + 512)
            nkt = (c0 + 512) // 128
            nct = min(NCT, (c0 + 512) // 16 // 128 + 1)
            for tl in range(4):
                bk, bb = cm.bank.next()
                for k in range(KC):
                    cm.mm(bk[:, 0:48], cm.hT[:, k, tl * 128:(tl + 1) * 128], wgt[:, k, :], [cm.b_h[k], b_wgt], bb, start=(k == 0), stop=(k == KC - 1), acc=(k > 0))
                fw.op("act", lambda h, bk=bk, tl=tl: h.activation(out=gate[:, tl, :], in_=bk[:, 0:48], func=AF.Sigmoid), reads=[bb], writes=[b_gate])
            for qt in range(4):
                fw.op("dve", lambda h, qt=qt, ob_=ob_: h.tensor_scalar(out=mrow[:, qt, :], in0=cthr[:, 0:CW], scalar1=colt[:, 0, ob_ * 4 + qt:ob_ * 4 + qt + 1], scalar2=None, op0=ALU.is_le),
                      reads=[cm.b_c], writes=[b_mrow])
            for ct in range(nct):
                fw.op("dve", lambda h, ct=ct, qs=qs: h.tensor_scalar(out=mcT[:, ct, :], in0=tq_row[:, qs], scalar1=cthc[:, ct:ct + 1], scalar2=None, op0=ALU.is_ge),
                      reads=[cm.b_c], writes=[b_mcT])
            for wt in range(8):
                kt = c0 // 128 - 4 + wt
                fw.op("dve", lambda h, wt=wt, kt=kt: h.tensor_scalar(out=mW[:, wt, :], in0=band[:, wt, :], scalar1=kval[:, kt:kt + 1], scalar2=None, op0=ALU.mult),
                      reads=[cm.b_c], writes=[b_mW])
            for gq in range(4):
                fw.dma("sp", "kv", lambda h, gq=gq, nkt=nkt: h.dma_start(out=KSs[:, 0:nkt * 128], in_=dr["KS"][gq][:, 0:nkt * 128]), reads=[dr["_b_scr"]], writes=[b_KS])
                fw.dma("sp", "kv", lambda h, gq=gq, nkt=nkt: h.dma_start(out=VSs[:, 0:nkt, :], in_=dr["VS"][gq][:, 0:nkt, :]), reads=[dr["_b_scr"]], writes=[b_VS])
                fw.dma("sp", "kv", lambda h, gq=gq, c0=c0: h.dma_start(out=KWs[:], in_=dr["KW"][gq][:, c0 - 512:c0 + 512]), reads=[dr["_b_scr"]], writes=[b_KW])
                fw.dma("sp", "kv", lambda h, gq=gq, c0=c0: h.dma_start(out=VWs[:, :, :], in_=dr["VW"][gq][:, c0 // 128 - 4:c0 // 128 + 4, :]), reads=[dr["_b_scr"]], writes=[b_VW])
                fw.dma("sp", "kv", lambda h, gq=gq: h.dma_start(out=KCs[:], in_=dr["KC"][gq]), reads=[dr["_b_scr"]], writes=[b_KC])
                fw.dma("sp", "kv", lambda h, gq=gq: h.dma_start(out=VCs[:, :, :], in_=dr["VC"][gq]), reads=[dr["_b_scr"]], writes=[b_VC])
                for hl in range(4):
                    bk, bb = cm.proj_f(dr["w_q"], gq * 4 + hl)
                    cm.norm_rope_f(bk, bb, qg[:, 0:1], tb[:, 0, :], tb[:, 1, :], tbb, prot[:], qT[hl][:], b_qT[hl], (f32r, b16r))
                for qt in range(4):
                    for hl in range(4):
                        bk, bb = cm.bank.next()
                        cm.mm(bk[:, 0:CW], qT[hl][:, qt * 128:(qt + 1) * 128], KCs[:, 0:CW], [b_qT[hl], b_KC], bb)
                        p_, pb_ = pr.next()
                        fw.op("act", lambda h, p_=p_, bk=bk: h.activation(out=p_[:], in_=bk[:, 0:CW], func=AF.Exp, scale=SCALE), reads=[bb], writes=[pb_])
                        c_, cb_ = c8.next()
                        fw.op("dve", lambda h, p_=p_, qt=qt, c_=c_: h.scalar_tensor_tensor(out=p_[:], in0=p_[:], scalar=1.0, in1=mrow[:, qt, :], op0=ALU.mult, op1=ALU.mult, accum_out=c_[:, 0:1]),
                              reads=[pb_, b_mrow], writes=[pb_, cb_])
                        fw.op("dve", lambda h, c_=c_: h.tensor_scalar(out=c_[:, 0:1], in0=c_[:, 0:1], scalar1=1e-30, scalar2=None, op0=ALU.max), reads=[cb_], writes=[cb_])
                        fw.op("dve", lambda h, c_=c_: h.reciprocal(out=c_[:, 0:1], in_=c_[:, 0:1]), reads=[cb_], writes=[cb_])
                        if hl == 0:
                            fw.op("dve", lambda h, p_=p_, c_=c_: h.tensor_scalar(out=p4[:, 4:4 + CW], in0=p_[:], scalar1=c_[:, 0:1], scalar2=None, op0=ALU.mult), reads=[pb_, cb_], writes=[b_p4])
                        else:
                            fw.op("dve", lambda h, p_=p_, c_=c_: h.scalar_tensor_tensor(out=p4[:, 4:4 + CW], in0=p_[:], scalar=c_[:, 0:1], in1=p4[:, 4:4 + CW], op0=ALU.mult, op1=ALU.add), reads=[pb_, cb_, b_p4], writes=[b_p4])
                    s_, sb_ = sc.next()
                    NJ = CW // 4
                    fw.op("dve", lambda h, s_=s_: h.tensor_reduce(out=s_[:, 0, 0:NJ], in_=p4[:, 4:4 + CW].rearrange("p (j r) -> p j r", r=4), axis=mybir.AxisListType.X, op=ALU.add), reads=[b_p4], writes=[sb_])
                    fw.op("dve", lambda h, s_=s_: h.tensor_tensor(out=s_[:, 0, 0:NJ], in0=s_[:, 0, 0:NJ], in1=p4[:, 0:CW].rearrange("p (j r) -> p j r", r=4)[:, :, 3], op=ALU.add), reads=[b_p4, sb_], writes=[sb_])
                    col = colt[:, 1, ob_ * 4 + qt:ob_ * 4 + qt + 1]
                    fw.op("dve", lambda h, s_=s_, col=col: h.scalar_tensor_tensor(out=s_[:, 1, :], in0=rowt[:, 0, :], scalar=col, in1=rowt[:, 1, :], op0=ALU.is_le, op1=ALU.mult), reads=[cm.b_c], writes=[sb_])
                    fw.op("dve", lambda h, s_=s_, col=col: h.scalar_tensor_tensor(out=s_[:, 2, :], in0=rowt[:, 0, :], scalar=col, in1=rowt[:, 2, :], op0=ALU.is_equal, op1=ALU.add), reads=[cm.b_c], writes=[sb_])
                    fw.op("dve", lambda h, s_=s_, col=col: h.tensor_scalar(out=s_[:, 3, :], in0=rowt[:, 0, :], scalar1=1.0, scalar2=col, op0=ALU.add, op1=ALU.is_equal), reads=[cm.b_c], writes=[sb_])
                    fw.op("dve", lambda h, s_=s_: h.tensor_tensor(out=s_[:, 2, :], in0=s_[:, 2, :], in1=s_[:, 3, :], op=ALU.add), reads=[sb_], writes=[sb_])
                    fw.op("dve", lambda h, s_=s_: h.scalar_tensor_tensor(out=s_[:, 0, :], in0=s_[:, 0, :], scalar=1.0, in1=s_[:, 1, :], op0=ALU.add, op1=ALU.mult), reads=[sb_], writes=[sb_])
                    fw.op("dve", lambda h, s_=s_: h.scalar_tensor_tensor(out=s_[:, 0, :], in0=s_[:, 2, :], scalar=BIG, in1=s_[:, 0, :], op0=ALU.mult, op1=ALU.add), reads=[sb_], writes=[sb_])
                    m8, m8b = c8.next()
                    fw.op("dve", lambda h, s_=s_, m8=m8: h.max(out=m8[:, 0:8], in_=s_[:, 0, :]), reads=[sb_], writes=[m8b])
                    fw.op("dve", lambda h, s_=s_, m8=m8: h.match_replace(out=s_[:, 3, :], in_to_replace=m8[:, 0:8], in_values=s_[:, 0, :], imm_value=-1.0), reads=[sb_, m8b], writes=[sb_])
                    fw.op("dve", lambda h, s_=s_, m8=m8: h.max(out=m8[:, 8:16], in_=s_[:, 3, :]), reads=[sb_], writes=[m8b])
                    fw.op("dve", lambda h, s_=s_, m8=m8: h.scalar_tensor_tensor(out=s_[:, 3, :], in0=s_[:, 0, :], scalar=m8[:, 15:16], in1=s_[:, 1, :], op0=ALU.is_ge, op1=ALU.mult), reads=[sb_, m8b], writes=[sb_])
                    bt, btb = cm.bank.next()
                    cm.tr(bt[0:NSB, 0:128], s_[:, 3, :], idf[:, :], [sb_, cm.b_c], btb)
                    fw.op("act", lambda h, bt=bt, qt=qt: h.activation(out=selT[:, qt * 128:(qt + 1) * 128], in_=bt[0:NSB, 0:128], func=AF.Copy), reads=[btb], writes=[b_selT])
                for hl in range(4):
                    hd = gq * 4 + hl
                    for br in range(3):
                        npc = min(128, NCP)
                        if br == 0:
                            tiles = [(KCs[:, ct * 128:ct * 128 + npc], VCs[0:npc, ct, 0:129], mcT[0:npc, ct, :], b_KC, b_VC, [b_mcT], npc) for ct in range(nct)]
                        elif br == 2:
                            tiles = [(KWs[:, wt * 128:(wt + 1) * 128], VWs[:, wt, 0:129], mW[:, wt, :], b_KW, b_VW, [b_mW], 128) for wt in range(8)]
                        else:
                            tiles = [(KSs[:, kt * 128:(kt + 1) * 128], VSs[:, kt, 0:129], None, b_KS, b_VS, [], 128) for kt in range(nkt)]
                        acA, acAb = accb.next(); acB, acBb = accb.next()
                        accs = [(acA, acAb, 0), (acA, acAb, 132), (acB, acBb, 0), (acB, acBb, 132)]
                        def stage_s(ti):
                            kap, vap, msk, kb_, vb_, mb_l, np_ = tiles[ti]
                            if br == 1:
                                bm, bmb = cm.bank.next()
                                cm.mm(bm[:, :], ebig[:, ti * 128:(ti + 1) * 128], selT[:, :], [cm.b_c, b_selT], bmb)
                                m_, mb_ = mS.next()
                                rel = ti - c0 // 128
                                if rel >= 0:
                                    fw.op("dve", lambda h, m_=m_, bm=bm, rel=rel: h.tensor_tensor(out=m_[:], in0=bm[:, :], in1=caus[:, rel, :], op=ALU.mult), reads=[bmb, cm.b_c], writes=[mb_])
                                else:
                                    fw.op("act", lambda h, m_=m_, bm=bm: h.activation(out=m_[:], in_=bm[:, :], func=AF.Copy), reads=[bmb], writes=[mb_])
                                msk, mb_l = m_[:], [mb_]
                            bs, bsb = cm.bank.next()
                            cm.mm(bs[0:np_, :], kap, qT[hl][:], [kb_, b_qT[hl]], bsb)
                            p_, pb_ = pT.next()
                            fw.op("act", lambda h, p_=p_, bs=bs, np_=np_: h.activation(out=p_[0:np_, :], in_=bs[0:np_, :], func=AF.Exp, scale=SCALE), reads=[bsb], writes=[pb_])
                            fw.op("dve", lambda h, p_=p_, msk=msk, np_=np_: h.tensor_tensor(out=p_[0:np_, :], in0=p_[0:np_, :], in1=msk, op=ALU.mult), reads=[pb_] + mb_l, writes=[pb_])
                            return p_, pb_, vap, vb_, np_

                        pend = stage_s(0)
                        for ti in range(len(tiles)):
                            nxt = stage_s(ti + 1) if ti + 1 < len(tiles) else None
                            p_, pb_, vap, vb_, np_ = pend
                            for qt in range(4):
                                a, ab, off = accs[qt]
                                cm.mm(a[:, off:off + 129], p_[0:np_, qt * 128:(qt + 1) * 128], vap, [pb_, vb_], ab,
                                      start=(ti == 0 and qt in (0, 2)), stop=(ti == len(tiles) - 1), acc=(ti > 0 or qt in (1, 3)))
                            pend = nxt
                        for qt in range(4):
                            a, ab, off = accs[qt]
                            c_, cb_ = c8.next()
                            fw.op("dve", lambda h, c_=c_, a=a, off=off: h.tensor_scalar(out=c_[:, 0:1], in0=a[:, off + 128:off + 129], scalar1=1e-30, scalar2=None, op0=ALU.max), reads=[ab], writes=[cb_])
                            fw.op("dve", lambda h, c_=c_: h.reciprocal(out=c_[:, 0:1], in_=c_[:, 0:1]), reads=[cb_], writes=[cb_])
                            gi = hd * 3 + br
                            fw.op("dve", lambda h, c_=c_, qt=qt, gi=gi: h.tensor_tensor(out=c_[:, 0:1], in0=c_[:, 0:1], in1=gate[:, qt, gi:gi + 1], op=ALU.mult), reads=[cb_, b_gate], writes=[cb_])
                            if br == 0:
                                fw.op("dve", lambda h, c_=c_, a=a, off=off, qt=qt: h.tensor_scalar(out=oacc[:, qt, :], in0=a[:, off:off + 128], scalar1=c_[:, 0:1], scalar2=None, op0=ALU.mult), reads=[ab, cb_], writes=[b_oacc])
                            else:
                                fw.op("dve", lambda h, c_=c_, a=a, off=off, qt=qt: h.scalar_tensor_tensor(out=oacc[:, qt, :], in0=a[:, off:off + 128], scalar=c_[:, 0:1], in1=oacc[:, qt, :], op0=ALU.mult, op1=ALU.add), reads=[ab, cb_, b_oacc], writes=[b_oacc])
                    o_, ob2 = oT.next()
                    for qt in range(4):
                        bt, btb = cm.bank.next()
                        cm.tr(bt[:, 0:128], oacc[:, qt, :], idf[:, :], [b_oacc, cm.b_c], btb)
                        fw.op("act", lambda h, o_=o_, bt=bt, qt=qt: h.activation(out=o_[:, qt * 128:(qt + 1) * 128], in_=bt[:, 0:128], func=AF.Copy), reads=[btb], writes=[ob2])
                    fw.dma("sp", "oa", lambda h, o_=o_, hd=hd, qs=qs: h.dma_start(out=dr["oaT"][hd * 128:(hd + 1) * 128, qs], in_=o_[:]), reads=[ob2], writes=[b_out])
        fw.flush(barrier=True)
    return b_out


def nsa_inputs(cfg, inp, c):
    D, S, TO, G = cfg.D, cfg.S, cfg.TO, cfg.G
    b, g = c // G, c % G
    n = (g + 1) * TO
    xp = np.zeros((D, S), np.float32)
    xp[:, S - n:] = inp["x"][b, :n].T
    w_in = inp["w_in"][0]
    kv = lambda i: w_in[:, 2048 + i * 512:2048 + (i + 1) * 512]
    tmaj = lambda w: np.ascontiguousarray(w.reshape(D // 128, 128, w.shape[1]).transpose(1, 0, 2))
    rep = lambda v, n_: np.ascontiguousarray(np.broadcast_to(v[None, :], (n_, v.shape[0])))
    z = np.zeros(128, np.float32)
    m = {
        "xTpad": xp, "gmix_col": col_vec(inp["g_mix"][0]),
        "w_kc": blk_w(kv(0)), "w_vc": blk_w(kv(1)), "w_ks": blk_w(kv(2)), "w_kw": blk_w(kv(4)),
        "w_v_t": tmaj(np.concatenate([kv(3), kv(5)], axis=1)),
        "w_q": blk_w(w_in[:, 0:2048]), "w_gate_t": tmaj(w_in[:, 5120:5168]),
        "kgain_cols": np.ascontiguousarray(np.stack([inp["nsa_ks_gain"][0], inp["nsa_kw_gain"][0], z], axis=1)),
        "kcg_bc": rep(inp["nsa_kc_gain"][0], 32), "qgain_col": np.ascontiguousarray(inp["nsa_q_gain"][0][:, None]),
        "w1k": np.ascontiguousarray(inp["cmp_wk1"][0].transpose(1, 0, 2)), "w1v": np.ascontiguousarray(inp["cmp_wv1"][0].transpose(1, 0, 2)),
        "w2k": np.ascontiguousarray(inp["cmp_wk2"][0].reshape(4, 128, 128).transpose(1, 0, 2)),
        "w2v": np.ascontiguousarray(inp["cmp_wv2"][0].reshape(4, 128, 128).transpose(1, 0, 2)),
        "perepk": np.ascontiguousarray(np.broadcast_to(inp["cmp_pe_k"][0].T[:, :, None], (128, 32, 32))),
        "perepv": np.ascontiguousarray(np.broadcast_to(inp["cmp_pe_v"][0].T[:, :, None], (128, 32, 32))),
    }
    m.update(nsa_tables(cfg, g))
    m.update(nsa_consts(cfg))
    return m


TDT = BF16


def gdn_consts():
    i = np.arange(64)
    ident = np.eye(64, dtype=np.float32)
    MU = (i[:, None] <= i[None, :]).astype(np.float32)
    MsU = (i[:, None] < i[None, :]).astype(np.float32)
    MsL = (i[:, None] > i[None, :]).astype(np.float32)
    ones = np.ones((64, 64), np.float32)
    return np.ascontiguousarray(np.stack([ident, MU, MsU, MsL, ones], axis=1))


def emit_gdn(nc, fw, cfg, dr, heads=None):
    D, KC, S, TO = cfg.D, cfg.KC, cfg.S, cfg.TO
    TB = 512
    NBLK, NOWN = S // TB, TO // TB
    NH = 16
    with ExitStack() as es:
        cm = Common(nc, fw, cfg, dr, es, "g_", nw=2)
        sb = lambda n, s, d: es.enter_context(nc.sbuf_tensor(n, list(s), d))
        hT, b_h, b_g, epsc = cm.hT, cm.b_h, cm.b_g, cm.epsc
        cst = sb("g_cst", [64, 5, 64], F32); b_cst = Buf()
        ones32 = sb("g_ones32", [64, 128], F32)
        mk2 = sb("g_mk2", [64, 128], F32)
        idb = sb("g_idb", [128, 128], BF16)
        idt = sb("g_idt", [64, 64], TDT)
        cw = sb("g_cw", [128, NH, 3, 4], F32)
        bc16 = sb("g_bc16", [64, 3, NH], F32)
        ogain = sb("g_og", [64, 128], F32)
        wbd = sb("g_wbd", [128, KC, 32], BF16); b_wbd = Buf()
        Sf = sb("g_Sf", [128, NH, 128], F32); Sb = sb("g_Sb", [128, NH, 128], BF16)
        b_Sf = [Buf() for _ in range(NH)]; b_Sb = [Buf() for _ in range(NH)]
        halo = sb("g_halo", [128, NH, 3, 3], F32); b_halo = [[Buf() for _ in range(3)] for _ in range(NH)]
        NCH = 8
        zsr = Ring(es, nc, "g_zs", [128, TB], BF16, NCH)
        cin = Ring(es, nc, "g_cin", [128, 3 + TB], F32, 3)
        cac = Ring(es, nc, "g_cac", [128, TB], F32, 2)
        csl = {i: Ring(es, nc, "g_cs%d" % i, [128, TB], BF16, NCH) for i in range(3)}
        gat = Ring(es, nc, "g_gat", [64, 8, NH], F32, 8)
        gst = Ring(es, nc, "g_gst", [64, 3, NH], F32, 8)
        etot = Ring(es, nc, "g_etot", [128, NH], F32, 8)
        RES = [{"t64": Ring(es, nc, "g_t64_%d" % i_, [64, 128], F32, 5), "tb64": Ring(es, nc, "g_tb64_%d" % i_, [64, 128], BF16, 12),
                "tt": Ring(es, nc, "g_tt_%d" % i_, [64, 64], TDT, 8), "t128": Ring(es, nc, "g_t128_%d" % i_, [128, 64], BF16, 6),
                "f128": Ring(es, nc, "g_f128_%d" % i_, [128, 64], F32, 2), "col": Ring(es, nc, "g_col_%d" % i_, [64, 4], F32, 8)} for i_ in range(NCH)]
        obr = Ring(es, nc, "g_ob", [128, TB], BF16, NCH)
        bank, bkb = cm.bank, cm.bkb
        b_out = Buf()

        fw.dma("sp", "cst", lambda h: h.dma_start(out=cst[:], in_=dr["gcst"]), writes=[b_cst])
        fw.dma("sp", "cst", lambda h: h.dma_start(out=cw[:], in_=dr["cw"]), writes=[b_cst])
        fw.dma("sp", "cst", lambda h: h.dma_start(out=bc16[:, 0:2, :], in_=dr["adt"]), writes=[b_cst])
        fw.dma("sp", "cst", lambda h: h.dma_start(out=ogain[:], in_=dr["ogain"]), writes=[b_cst])
        fw.dma("pool", "w", lambda h: h.dma_start(out=wbd[:], in_=dr["w_bd"]), writes=[b_wbd])
        fw.dma("pool", "w", lambda h: h.dma_start(out=idb[:], in_=dr["ident128"]), writes=[b_cst])
        fw.op("dve", lambda h: h.memset(ones32[:], 1.0), writes=[b_cst])
        fw.op("dve", lambda h: h.memset(Sf[:], 0.0), writes=b_Sf)
        fw.op("dve", lambda h: h.memset(Sb[:], 0.0), writes=b_Sb)
        fw.op("dve", lambda h: h.memset(halo[:], 0.0), writes=[b for r in b_halo for b in r])
        fw.op("dve", lambda h: h.tensor_copy(out=mk2[:, 0:64], in_=cst[:, 1, :]), reads=[b_cst], writes=[b_cst])
        fw.op("dve", lambda h: h.tensor_copy(out=mk2[:, 64:128], in_=cst[:, 2, :]), reads=[b_cst], writes=[b_cst])
        fw.op("dve", lambda h: h.tensor_copy(out=idt[:], in_=cst[:, 0, :]), reads=[b_cst], writes=[b_cst])
        fw.op("act", lambda h: h.activation(out=bc16[:, 2, :], in_=bc16[:, 0, :], func=AF.Exp), reads=[b_cst], writes=[b_cst])
        fw.op("dve", lambda h: h.tensor_scalar(out=bc16[:, 2, :], in0=bc16[:, 2, :], scalar1=-1.0, scalar2=None, op0=ALU.mult), reads=[b_cst], writes=[b_cst])

        proj_f = cm.proj_f

        def mm(out, lhsT, rhs, rd, wr, start=True, stop=True, acc=False):
            fw.op("pe", lambda h: h.matmul(out, lhsT=lhsT, rhs=rhs, start=start, stop=stop), reads=rd, writes=[wr], pe_acc=acc)

        def tr(out, in_, ident, rd, wr):
            fw.op("pe", lambda h: h.transpose(out, in_, ident), reads=rd + [b_cst], writes=[wr])

        def rsq(ss, ssb, n, scale):
            fw.op("act", lambda h: h.activation(out=ss, in_=ss, func=AF.Sqrt, bias=epsc[0:n, :], scale=scale), reads=[ssb, b_g], writes=[ssb])
            fw.op("dve", lambda h: h.reciprocal(out=ss, in_=ss), reads=[ssb], writes=[ssb])

        for blk in range(NBLK):
            own = blk >= NBLK - NOWN
            cm.make_h(blk * TB)
            G = []
            for c in range(8):
                cs = slice(c * 64, c * 64 + 64)
                bk, bb = bank.next()
                for k in range(KC):
                    mm(bk[0:64, 0:32], hT[:, k, cs], wbd[:, k, :], [b_h[k], b_wbd], bb, start=(k == 0), stop=(k == KC - 1), acc=(k > 0))
                ga, gab = gat.next()
                fw.op("act", lambda h, ga=ga, bk=bk: h.activation(out=ga[:, 0, :], in_=bk[0:64, 0:16], func=AF.Exp, scale=-1.0), reads=[bb], writes=[gab])
                fw.op("act", lambda h, ga=ga: h.activation(out=ga[:, 0, :], in_=ga[:, 0, :], func=AF.Ln, bias=1.0), reads=[gab], writes=[gab])
                fw.op("act", lambda h, ga=ga: h.activation(out=ga[:, 2, :], in_=ga[:, 0, :], func=AF.Exp, scale=-1.0), reads=[gab], writes=[gab])
                fw.op("dve", lambda h, ga=ga: h.tensor_scalar(out=ga[:, 1, :], in0=ga[:, 0, :], scalar1=-1.0, scalar2=None, op0=ALU.mult), reads=[gab], writes=[gab])
                fw.op("dve", lambda h, ga=ga, bk=bk: h.tensor_tensor(out=ga[:, 3, :], in0=bk[0:64, 16:32], in1=bc16[:, 1, :], op=ALU.add), reads=[bb, b_cst], writes=[gab])
                fw.op("act", lambda h, ga=ga: h.activation(out=ga[:, 3, :], in_=ga[:, 3, :], func=AF.Exp), reads=[gab], writes=[gab])
                fw.op("act", lambda h, ga=ga: h.activation(out=ga[:, 3, :], in_=ga[:, 3, :], func=AF.Ln, bias=1.0), reads=[gab], writes=[gab])
                fw.op("dve", lambda h, ga=ga: h.tensor_tensor(out=ga[:, 3, :], in0=ga[:, 3, :], in1=bc16[:, 2, :], op=ALU.mult), reads=[gab, b_cst], writes=[gab])
                b2, b2b = bank.next()
                mm(b2[0:64, 0:16], cst[:, 1, :], ga[:, 3, :], [b_cst, gab], b2b)
                mm(b2[0:64, 16:32], cst[:, 3, :], ga[:, 3, :], [b_cst, gab], b2b, acc=True)
                mm(b2[:, 32:48], ones32[:, :], ga[:, 3, :], [b_cst, gab], b2b, acc=True)
                gs, gsb = gst.next()
                et, etb = etot.next()
                fw.op("act", lambda h, gs=gs, b2=b2: h.activation(out=gs[:, 0, :], in_=b2[0:64, 0:16], func=AF.Copy), reads=[b2b], writes=[gsb])
                fw.op("act", lambda h, gs=gs, b2=b2: h.activation(out=gs[:, 1, :], in_=b2[0:64, 0:16], func=AF.Exp), reads=[b2b], writes=[gsb])
                fw.op("act", lambda h, gs=gs, b2=b2: h.activation(out=gs[:, 2, :], in_=b2[0:64, 16:32], func=AF.Exp), reads=[b2b], writes=[gsb])
                fw.op("act", lambda h, et=et, b2=b2: h.activation(out=et[:], in_=b2[:, 32:48], func=AF.Exp), reads=[b2b], writes=[etb])
                fw.op("dve", lambda h, ga=ga, gs=gs: h.tensor_tensor(out=ga[:, 4, :], in0=ga[:, 2, :], in1=gs[:, 1, :], op=ALU.mult), reads=[gab, gsb], writes=[gab])
                G.append((ga, gab, gs, gsb, et, etb))
            for hg in range(16 // NCH):
                if heads is not None and not any(hg * NCH + q in heads for q in range(NCH)):
                    continue
                CS = {}
                OB = {}
                for hl in range(NCH):
                    hd = hg * NCH + hl
                    if heads is not None and hd not in heads:
                        continue
                    cs_t = CS.setdefault(hd, {})
                    for i, wn in ((1, "w_gk"), (2, "w_gv"), (0, "w_gq")):
                        if i == 0 and not (own or blk == NBLK - NOWN - 1):
                            continue
                        bk, bb = proj_f(dr[wn], hd)
                        ci, cib = cin.next()
                        fw.op("act", lambda h, ci=ci, bk=bk: h.activation(out=ci[:, 3:3 + TB], in_=bk[:, :], func=AF.Copy), reads=[bb], writes=[cib])
                        fw.op("dve", lambda h, ci=ci, i=i, hd=hd: h.tensor_copy(out=ci[:, 0:3], in_=halo[:, hd, i, :]), reads=[b_halo[hd][i]], writes=[cib])
                        fw.op("dve", lambda h, ci=ci, i=i, hd=hd: h.tensor_copy(out=halo[:, hd, i, :], in_=ci[:, TB:TB + 3]), reads=[cib], writes=[b_halo[hd][i]])
                        if i == 0 and not own:
                            continue
                        ac, acb = cac.next()
                        fw.op("dve", lambda h, ci=ci, ac=ac, i=i, hd=hd: h.tensor_scalar(out=ac[:], in0=ci[:, 0:TB], scalar1=cw[:, hd, i, 0:1], scalar2=None, op0=ALU.mult), reads=[cib, b_cst], writes=[acb])
                        for j in (1, 2, 3):
                            fw.op("dve", lambda h, ci=ci, ac=ac, i=i, hd=hd, j=j: h.scalar_tensor_tensor(out=ac[:], in0=ci[:, j:j + TB], scalar=cw[:, hd, i, j:j + 1], in1=ac[:], op0=ALU.mult, op1=ALU.add),
                                  reads=[cib, b_cst, acb], writes=[acb])
                        st, stb = csl[i].next()
                        fw.op("act", lambda h, st=st, ac=ac: h.activation(out=st[:], in_=ac[:], func=AF.Silu), reads=[acb], writes=[stb])
                        cs_t[i] = (st, stb)
                    OB[hd] = (obr.next() if own else (None, None))
                    if own:
                        bk, bb = proj_f(dr["w_gz"], hd)
                        zs, zsb = zsr.next()
                        fw.op("act", lambda h, zs=zs, bk=bk: h.activation(out=zs[:], in_=bk[:, :], func=AF.Silu), reads=[bb], writes=[zsb])
                        CS[hd]["z"] = (zs, zsb)

                def chain(hd, hl, c, R):
                    cs_t = CS[hd]
                    ob, obb = OB[hd]
                    cs = slice(c * 64, c * 64 + 64)
                    ga, gab, gs, gsb, et, etb = G[c][:6]
                    kst, kstb = cs_t[1]
                    pb, pbb = bkb.next()
                    tr(pb[0:64, 0:128], kst[:, cs], idb[:, :], [kstb], pbb)
                    cl, clb = R["col"].next()
                    junk, jb = R["tb64"].next()
                    fw.op("act", lambda h, junk=junk, pb=pb, cl=cl: h.activation(out=junk[:], in_=pb[0:64, 0:128], func=AF.Square, accum_out=cl[:, 0:1]), reads=[pbb], writes=[jb, clb])
                    rsq(cl[:, 0:1], clb, 64, 1.0)
                    kn, knb = R["tb64"].next()
                    fw.op("dve", lambda h, kn=kn, pb=pb, cl=cl: h.tensor_scalar(out=kn[:], in0=pb[0:64, 0:128], scalar1=cl[:, 0:1], scalar2=None, op0=ALU.mult), reads=[pbb, clb], writes=[knb])
                    yield
                    pb2, pb2b = bkb.next()
                    tr(pb2[:, 0:64], kn[:], idb[0:64, 0:64], [knb], pb2b)
                    kT, kTb = R["t128"].next()
                    fw.op("act", lambda h, kT=kT, pb2=pb2: h.activation(out=kT[:], in_=pb2[:, 0:64], func=AF.Copy), reads=[pb2b], writes=[kTb])
                    yield
                    vst, vstb = cs_t[2]
                    pb3, pb3b = bkb.next()
                    tr(pb3[0:64, 0:128], vst[:, cs], idb[:, :], [vstb], pb3b)
                    vb, vbb = R["tb64"].next()
                    fw.op("dve", lambda h, vb=vb, pb3=pb3, ga=ga, hd=hd: h.tensor_scalar(out=vb[:], in0=pb3[0:64, 0:128], scalar1=ga[:, 2, hd:hd + 1], scalar2=None, op0=ALU.mult), reads=[pb3b, gab], writes=[vbb])
                    yield
                    r12, r12b = R["t64"].next()
                    fw.op("dve", lambda h, r12=r12, ga=ga, hd=hd: h.tensor_scalar(out=r12[:, 0:64], in0=cst[:, 1, :], scalar1=ga[:, 3, hd:hd + 1], scalar2=None, op0=ALU.mult), reads=[b_cst, gab], writes=[r12b])
                    fw.op("dve", lambda h, r12=r12, ga=ga, hd=hd: h.scalar_tensor_tensor(out=r12[:, 64:128], in0=cst[:, 0, :], scalar=ga[:, 1, hd:hd + 1], in1=r12[:, 0:64], op0=ALU.mult, op1=ALU.add), reads=[b_cst, gab, r12b], writes=[r12b])
                    pg, pgb = bank.next()
                    mm(pg[:, 0:64], ones32[:, :], r12[:, 0:64], [b_cst, r12b], pgb)
                    mm(pg[0:64, 64:128], ones32[:, 0:64], r12[:, 64:128], [b_cst, r12b], pgb, acc=True)
                    if own:
                        eg, egb = R["f128"].next()
                        fw.op("act", lambda h, eg=eg, pg=pg: h.activation(out=eg[:], in_=pg[:, 0:64], func=AF.Exp), reads=[pgb], writes=[egb])
                    dm, dmb = R["t64"].next()
                    fw.op("dve", lambda h, dm=dm, pg=pg, gs=gs, hd=hd: h.tensor_scalar(out=dm[:], in0=pg[0:64, 0:128], scalar1=gs[:, 0, hd:hd + 1], scalar2=0.0, op0=ALU.subtract, op1=ALU.min), reads=[pgb, gsb], writes=[dmb])
                    yield
                    fw.op("act", lambda h, dm=dm: h.activation(out=dm[:], in_=dm[:], func=AF.Exp), reads=[dmb], writes=[dmb])
                    fw.op("dve", lambda h, dm=dm: h.tensor_tensor(out=dm[:], in0=dm[:], in1=mk2[:], op=ALU.mult), reads=[dmb, b_cst], writes=[dmb])
                    pk, pkb = bank.next()
                    mm(pk[0:64, 0:64], kT[:, :], kT[:, :], [kTb], pkb)
                    X, Xb = R["tt"].next()
                    fw.op("dve", lambda h, X=X, pk=pk, dm=dm: h.tensor_tensor(out=X[:], in0=pk[0:64, 0:64], in1=dm[:, 64:128], op=ALU.mult), reads=[pkb, dmb], writes=[Xb])
                    yield
                    pa, pab = bkb.next()
                    tr(pa[0:64, 0:64], X[:], idb[0:64, 0:64], [Xb], pab)
                    Y, Yb = R["tt"].next()
                    fw.op("act", lambda h, Y=Y, pa=pa: h.activation(out=Y[:], in_=pa[0:64, 0:64], func=AF.Copy), reads=[pab], writes=[Yb])
                    P, Pb = R["tt"].next()
                    fw.op("dve", lambda h, P=P, X=X: h.tensor_tensor(out=P[:], in0=idt[:], in1=X[:], op=ALU.subtract), reads=[Xb, b_cst], writes=[Pb])
                    yield
                    for lvl in range(5):
                        py, pyb = bank.next()
                        mm(py[0:64, 0:64], X[:], Y[:], [Xb, Yb], pyb)
                        if lvl < 4:
                            mm(py[0:64, 64:128], Y[:], X[:], [Xb, Yb], pyb, acc=True)
                        Y2, Y2b = R["tt"].next()
                        fw.op("act", lambda h, Y2=Y2, py=py: h.activation(out=Y2[:], in_=py[0:64, 0:64], func=AF.Copy), reads=[pyb], writes=[Y2b])
                        if lvl < 4:
                            X2, X2b = R["tt"].next()
                            fw.op("act", lambda h, X2=X2, py=py: h.activation(out=X2[:], in_=py[0:64, 64:128], func=AF.Copy), reads=[pyb], writes=[X2b])
                            yield
                        pp, ppb = bank.next()
                        mm(pp[0:64, 0:64], Y2[:], P[:], [Y2b, Pb], ppb)
                        P2, P2b = R["tt"].next()
                        fw.op("dve", lambda h, P2=P2, P=P, pp=pp: h.tensor_tensor(out=P2[:], in0=P[:], in1=pp[0:64, 0:64], op=ALU.add), reads=[Pb, ppb], writes=[P2b])
                        yield
                        P, Pb = P2, P2b
                        Y, Yb = Y2, Y2b
                        if lvl < 4:
                            X, Xb = X2, X2b
                    if TDT != BF16:
                        Pm, Pmb = R["tb64"].next()
                        fw.op("act", lambda h, Pm=Pm, P=P: h.activation(out=Pm[:, 0:64], in_=P[:], func=AF.Copy), reads=[Pb], writes=[Pmb])
                        Pt = Pm[:, 0:64]; Ptb = Pmb
                    else:
                        Pt = P[:]; Ptb = Pb
                    pu, pub = bank.next()
                    mm(pu[0:64, 0:128], Pt, vb[:], [Ptb, vbb], pub)
                    u, ub = R["t64"].next()
                    fw.op("act", lambda h, u=u, pu=pu: h.activation(out=u[:], in_=pu[0:64, 0:128], func=AF.Copy), reads=[pub], writes=[ub])
                    yield
                    kbg, kbgb = R["tb64"].next()
                    fw.op("dve", lambda h, kbg=kbg, kn=kn, ga=ga, hd=hd: h.tensor_scalar(out=kbg[:], in0=kn[:], scalar1=ga[:, 4, hd:hd + 1], scalar2=None, op0=ALU.mult), reads=[knb, gab], writes=[kbgb])
                    pw, pwb = bank.next()
                    mm(pw[:, 0:64], kbg[:], Pt, [kbgb, Ptb], pwb)
                    wT, wTb = R["t128"].next()
                    fw.op("act", lambda h, wT=wT, pw=pw: h.activation(out=wT[:], in_=pw[:, 0:64], func=AF.Copy), reads=[pwb], writes=[wTb])
                    yield
                    kg, kgb = R["tb64"].next()
                    fw.op("dve", lambda h, kg=kg, kn=kn, gs=gs, hd=hd: h.tensor_scalar(out=kg[:], in0=kn[:], scalar1=gs[:, 2, hd:hd + 1], scalar2=None, op0=ALU.mult), reads=[knb, gsb], writes=[kgb])
                    pws, pwsb = bank.next()
                    mm(pws[0:64, 0:128], wT[:], Sb[:, hd, :], [wTb, b_Sb[hd]], pwsb)
                    vn, vnb = R["tb64"].next()
                    fw.op("dve", lambda h, vn=vn, u=u, pws=pws: h.tensor_tensor(out=vn[:], in0=u[:], in1=pws[0:64, 0:128], op=ALU.subtract), reads=[ub, pwsb], writes=[vnb])
                    yield
                    if own:
                        qst, qstb = cs_t[0]
                        pq, pqb = bkb.next()
                        tr(pq[0:64, 0:128], qst[:, cs], idb[:, :], [qstb], pqb)
                        cq, cqb = R["col"].next()
                        junk2, j2b = R["tb64"].next()
                        fw.op("act", lambda h, junk2=junk2, pq=pq, cq=cq: h.activation(out=junk2[:], in_=pq[0:64, 0:128], func=AF.Square, accum_out=cq[:, 0:1]), reads=[pqb], writes=[j2b, cqb])
                        rsq(cq[:, 0:1], cqb, 64, 1.0)
                        qn, qnb = R["tb64"].next()
                        fw.op("dve", lambda h, qn=qn, pq=pq, cq=cq: h.tensor_scalar(out=qn[:], in0=pq[0:64, 0:128], scalar1=cq[:, 0:1], scalar2=128.0 ** -0.5, op0=ALU.mult, op1=ALU.mult), reads=[pqb, cqb], writes=[qnb])
                        yield
                        pq2, pq2b = bkb.next()
                        tr(pq2[:, 0:64], qn[:], idb[0:64, 0:64], [qnb], pq2b)
                        qT, qTb = R["t128"].next()
                        fw.op("act", lambda h, qT=qT, pq2=pq2: h.activation(out=qT[:], in_=pq2[:, 0:64], func=AF.Copy), reads=[pq2b], writes=[qTb])
                        qg, qgb = R["t128"].next()
                        fw.op("dve", lambda h, qg=qg, qT=qT, eg=eg: h.tensor_tensor(out=qg[:], in0=qT[:], in1=eg[:], op=ALU.mult), reads=[qTb, egb], writes=[qgb])
                        yield
                        pat, patb = bank.next()
                        mm(pat[0:64, 0:64], kT[:, :], qT[:, :], [kTb, qTb], patb)
                        at, atb = R["tb64"].next()
                        fw.op("dve", lambda h, at=at, pat=pat, dm=dm: h.tensor_tensor(out=at[:, 0:64], in0=pat[0:64, 0:64], in1=dm[:, 0:64], op=ALU.mult), reads=[patb, dmb], writes=[atb])
                        yield
                        po, pob = bank.next()
                        mm(po[0:64, 0:128], qg[:], Sb[:, hd, :], [qgb, b_Sb[hd]], pob, start=True, stop=False)
                        mm(po[0:64, 0:128], at[:, 0:64], vn[:], [atb, vnb], pob, start=False, stop=True, acc=True)
                        co, cob = R["col"].next()
                        junk3, j3b = R["tb64"].next()
                        fw.op("act", lambda h, junk3=junk3, po=po, co=co: h.activation(out=junk3[:], in_=po[0:64, 0:128], func=AF.Square, accum_out=co[:, 0:1]), reads=[pob], writes=[j3b, cob])
                        rsq(co[:, 0:1], cob, 64, 1.0 / 128)
                        on, onb = R["t64"].next()
                        fw.op("dve", lambda h, on=on, po=po, co=co: h.scalar_tensor_tensor(out=on[:], in0=po[0:64, 0:128], scalar=co[:, 0:1], in1=ogain[:], op0=ALU.mult, op1=ALU.mult), reads=[pob, cob, b_cst], writes=[onb])
                        yield
                        zs, zsb = cs_t["z"]
                        pz, pzb = bkb.next()
                        tr(pz[0:64, 0:128], zs[:, cs], idb[:, :], [zsb], pzb)
                        obt, obtb = R["tb64"].next()
                        fw.op("dve", lambda h, obt=obt, on=on, pz=pz: h.tensor_tensor(out=obt[:], in0=on[:], in1=pz[0:64, 0:128], op=ALU.mult), reads=[onb, pzb], writes=[obtb])
                        pot, potb = bkb.next()
                        tr(pot[:, 0:64], obt[:], idb[0:64, 0:64], [obtb], potb)
                        fw.op("act", lambda h, ob=ob, pot=pot, cs=cs: h.activation(out=ob[:, cs], in_=pot[:, 0:64], func=AF.Copy), reads=[potb], writes=[obb])
                        yield
                    ps, psb = bank.next()
                    mm(ps[:, 0:128], kg[:], vn[:], [kgb, vnb], psb)
                    fw.op("dve", lambda h, hd=hd, et=et, ps=ps: h.scalar_tensor_tensor(out=Sf[:, hd, :], in0=Sf[:, hd, :], scalar=et[:, hd:hd + 1], in1=ps[:, 0:128], op0=ALU.mult, op1=ALU.add),
                          reads=[b_Sf[hd], etb, psb], writes=[b_Sf[hd]])
                    fw.op("act", lambda h, hd=hd: h.activation(out=Sb[:, hd, :], in_=Sf[:, hd, :], func=AF.Copy), reads=[b_Sf[hd]], writes=[b_Sb[hd]])

                hds = [hg * NCH + hl for hl in range(NCH) if heads is None or hg * NCH + hl in heads]
                for c in range(8):
                    cs = slice(c * 64, c * 64 + 64)
                    gens = [chain(hd, hd % 4, c, RES[i_]) for i_, hd in enumerate(hds)]
                    while gens:
                        nxt = []
                        for g_ in gens:
                            try:
                                next(g_)
                                nxt.append(g_)
                            except StopIteration:
                                pass
                        gens = nxt
                if own:
                    oblk = blk - (NBLK - NOWN)
                    for hd in hds:
                        ob, obb = OB[hd]
                        fw.dma("sp", "ob", lambda h, ob=ob, hd=hd, oblk=oblk: h.dma_start(out=dr["obT"][hd * 128:(hd + 1) * 128, oblk * TB:(oblk + 1) * TB], in_=ob[:]), reads=[obb], writes=[b_out])
        fw.flush(barrier=True)
    return b_out


def gdn_inputs(cfg, inp, c):
    D, S, TO, G = cfg.D, cfg.S, cfg.TO, cfg.G
    b, g = c // G, c % G
    n = (g + 1) * TO
    xp = np.zeros((D, S), np.float32)
    xp[:, S - n:] = inp["x"][b, :n].T
    w_in = inp["w_in"][0]
    o = 5168
    cwt = inp["gdn_conv_w"][0].reshape(4, 3, 16, 128)
    rep = lambda v, n_: np.ascontiguousarray(np.broadcast_to(v[None, :], (n_, v.shape[0])))
    return {
        "xTpad": xp, "gmix_col": col_vec(inp["g_mix"][0]), "gcst": gdn_consts(),
        "ident128": np.eye(128, dtype=np.float32),
        "cw": np.ascontiguousarray(cwt.transpose(3, 2, 1, 0)),
        "adt": np.ascontiguousarray(np.stack([rep(inp["gdn_a_log"][0], 64), rep(inp["gdn_dt_bias"][0], 64)], axis=1)),
        "ogain": rep(inp["gdn_o_gain"][0], 64),
        "w_gq": blk_w(w_in[:, o:o + 2048]), "w_gk": blk_w(w_in[:, o + 2048:o + 4096]), "w_gv": blk_w(w_in[:, o + 4096:o + 6144]),
        "w_gz": blk_w(w_in[:, 11312:11312 + 2048]),
        "w_bd": np.ascontiguousarray(w_in[:, 13360:13392].reshape(D // 128, 128, 32).transpose(1, 0, 2)),
    }


def emit_tail(nc, fw, cfg, dr):
    D, KC, TT, TO, HID, MW = cfg.D, cfg.KC, cfg.TT, cfg.TO, cfg.HID, cfg.MW
    MC, PC, HC = MW // 128, cfg.PLE // 128, HID // 128
    HH = min(HC, 32)
    with ExitStack() as es:
        sb = lambda n, s, d: es.enter_context(nc.sbuf_tensor(n, list(s), d))
        xT = sb("t_x", [128, KC, TT], F32); b_x = [Buf("x%d" % k) for k in range(KC)]
        hT = sb("t_h", [128, KC, TT], BF16); b_h = [Buf("h%d" % k) for k in range(KC)]
        big = sb("t_big", [128, max(2 * MC + KC, HH), TT], BF16)
        b_oa, b_ob = Buf("oa"), Buf("ob")
        b_mx = [Buf("mx%d" % k) for k in range(KC)]
        b_hd = [Buf("hd%d" % k) for k in range(HH)]
        pTb = sb("t_p", [128, PC, TT], BF16); b_p = Buf("p")
        gcol = sb("t_g", [128, 3, KC], F32); b_g = Buf("g")
        ones = sb("t_ones", [128, 128], BF16); b_ones = Buf("ones")
        epsc = sb("t_eps", [128, 1], F32)
        rstd = sb("t_rstd", [128, TT], F32); b_rstd = Buf("rstd")
        wring = Ring(es, nc, "t_w", [128, max(KC, HH), 128], BF16, 3)
        sq = Ring(es, nc, "t_sq", [128, TT], BF16, 2)
        tmp = Ring(es, nc, "t_tmp", [128, TT], F32, 4)
        yt = Ring(es, nc, "t_y", [128, TT], F32, 2)
        bank = Ring(es, nc, "t_bk", [128, 512], F32, 8, psum=True)
        b_y = Buf("y")

        fw.dma("sp", "cst", lambda h: h.dma_start(out=gcol[:], in_=dr["gcols"]), writes=[b_g])
        fw.op("dve", lambda h: h.memset(ones[:], 1.0), writes=[b_ones])
        fw.op("dve", lambda h: h.memset(epsc[:], EPS), writes=[b_g])

        def load_w(wd, cb, nk, k0=0):
            wt, wb = wring.next()
            fw.dma("pool", "w", lambda h: h.dma_start(out=wt[:, 0:nk, :], in_=wd[cb, :, k0:k0 + nk, :]), writes=[wb])
            return wt, wb

        def acc(wd, cb, nk, rhs, rbufs, k0=0):
            wt, wb = load_w(wd, cb, nk, k0)
            bk, bb = bank.next()
            for k in range(nk):
                fw.op("pe", lambda h, k=k: h.matmul(bk[:, 0:TT], lhsT=wt[:, k, :], rhs=rhs(k), start=(k == 0), stop=(k == nk - 1)),
                      reads=[wb, rbufs[k]], writes=[bb], pe_acc=(k > 0))
            return bk, bb

        def norm_to_h(gi):
            bk, bb = bank.next()
            for k in range(KC):
                st, sbuf_ = sq.next()
                fw.op("act", lambda h, k=k, st=st: h.activation(out=st[:], in_=xT[:, k, :], func=AF.Square), reads=[b_x[k]], writes=[sbuf_])
                fw.op("pe", lambda h, k=k, st=st: h.matmul(bk[:, 0:TT], lhsT=ones[:], rhs=st[:], start=(k == 0), stop=(k == KC - 1)),
                      reads=[b_ones, sbuf_], writes=[bb], pe_acc=(k > 0))
            fw.op("act", lambda h: h.activation(out=rstd[:], in_=bk[:, 0:TT], func=AF.Sqrt, bias=epsc[:], scale=1.0 / D), reads=[bb, b_g], writes=[b_rstd])
            fw.op("dve", lambda h: h.reciprocal(out=rstd[:], in_=rstd[:]), reads=[b_rstd], writes=[b_rstd])
            for k in range(KC):
                fw.op("dve", lambda h, k=k: h.scalar_tensor_tensor(out=hT[:, k, :], in0=xT[:, k, :], scalar=gcol[:, gi, k:k + 1], in1=rstd[:], op0=ALU.mult, op1=ALU.mult),
                      reads=[b_x[k], b_g, b_rstd], writes=[b_h[k]])

        oaT, obT, mxT = big[:, 0:MC, :], big[:, MC:2 * MC, :], big[:, 2 * MC:2 * MC + KC, :]
        for blk in range(TO // TT):
            ts = slice(blk * TT, (blk + 1) * TT)
            for k in range(KC):
                fw.dma("sp", "x", lambda h, k=k, blk=blk: h.dma_start(out=xT[:, k, :], in_=dr["xTpad"][k * 128:(k + 1) * 128, cfg.S - cfg.TO + blk * TT:cfg.S - cfg.TO + (blk + 1) * TT]), writes=[b_x[k]])
            fw.dma("pool", "mix", lambda h, ts=ts: h.dma_start(out=oaT, in_=dr["oaT"].rearrange("(c p) t -> p c t", p=128)[:, :, ts]), writes=[b_oa])
            fw.dma("pool", "mix", lambda h, ts=ts: h.dma_start(out=obT, in_=dr["obT"].rearrange("(c p) t -> p c t", p=128)[:, :, ts]), writes=[b_ob])
            fw.dma("pool", "mix", lambda h, ts=ts: h.dma_start(out=pTb[:], in_=dr["pT"].rearrange("(c p) t -> p c t", p=128)[:, :, ts]), writes=[b_p])
            norm_to_h(0)
            hr = lambda k: hT[:, k, :]
            for cb in range(KC):
                ga, gab = acc(dr["w_ga"], cb, KC, hr, b_h)
                ua, uab = acc(dr["w_upa"], cb, MC, lambda k: big[:, k, :], [b_oa] * MC)
                gb, gbb = acc(dr["w_gb"], cb, KC, hr, b_h)
                ub, ubb = acc(dr["w_upb"], cb, MC, lambda k: big[:, MC + k, :], [b_ob] * MC)
                s1, s1b = tmp.next(); s2, s2b = tmp.next()
                fw.op("act", lambda h, s1=s1, ga=ga: h.activation(out=s1[:], in_=ga[:, 0:TT], func=AF.Sigmoid), reads=[gab], writes=[s1b])
                fw.op("act", lambda h, s2=s2, gb=gb: h.activation(out=s2[:], in_=gb[:, 0:TT], func=AF.Sigmoid), reads=[gbb], writes=[s2b])
                fw.op("dve", lambda h, s1=s1, ua=ua: h.tensor_tensor(out=s1[:], in0=s1[:], in1=ua[:, 0:TT], op=ALU.mult), reads=[s1b, uab], writes=[s1b])
                fw.op("dve", lambda h, s2=s2, ub=ub: h.tensor_tensor(out=s2[:], in0=s2[:], in1=ub[:, 0:TT], op=ALU.mult), reads=[s2b, ubb], writes=[s2b])
                fw.op("dve", lambda h, s1=s1, s2=s2, cb=cb: h.tensor_tensor(out=big[:, 2 * MC + cb, :], in0=s1[:], in1=s2[:], op=ALU.add), reads=[s1b, s2b], writes=[b_mx[cb]])
            for cb in range(KC):
                bk, bb = acc(dr["w_out"], cb, KC, lambda k: big[:, 2 * MC + k, :], b_mx)
                fw.op("dve", lambda h, cb=cb, bk=bk: h.tensor_tensor(out=xT[:, cb, :], in0=xT[:, cb, :], in1=bk[:, 0:TT], op=ALU.add), reads=[bb, b_x[cb]], writes=[b_x[cb]])
            if "dbg1" in dr:
                for k in range(KC):
                    fw.dma("sp", "y", lambda h, k=k, ts=ts: h.dma_start(out=dr["dbg1"][k * 128:(k + 1) * 128, ts], in_=xT[:, k, :]), reads=[b_x[k]], writes=[b_y])
                    fw.dma("sp", "y", lambda h, k=k, ts=ts: h.dma_start(out=dr["dbg0"][k * 128:(k + 1) * 128, ts], in_=big[:, 2 * MC + k, :]), reads=[b_mx[k]], writes=[b_y])
            norm_to_h(1)
            for h0 in range(0, HC, HH):
                for j in range(HH):
                    bk, bb = acc(dr["w_mi"], h0 + j, KC, hr, b_h)
                    r, rb = tmp.next()
                    fw.op("act", lambda h, r=r, bk=bk: h.activation(out=r[:], in_=bk[:, 0:TT], func=AF.Relu), reads=[bb], writes=[rb])
                    fw.op("dve", lambda h, r=r, j=j: h.tensor_tensor(out=big[:, j, :], in0=r[:], in1=r[:], op=ALU.mult), reads=[rb], writes=[b_hd[j]])
                for cb in range(KC):
                    bk, bb = acc(dr["w_mo"], cb, HH, lambda k: big[:, k, :], b_hd, k0=h0)
                    fw.op("dve", lambda h, cb=cb, bk=bk: h.tensor_tensor(out=xT[:, cb, :], in0=xT[:, cb, :], in1=bk[:, 0:TT], op=ALU.add), reads=[bb, b_x[cb]], writes=[b_x[cb]])
            if "dbg2" in dr:
                for k in range(KC):
                    fw.dma("sp", "y", lambda h, k=k, ts=ts: h.dma_start(out=dr["dbg2"][k * 128:(k + 1) * 128, ts], in_=xT[:, k, :]), reads=[b_x[k]], writes=[b_y])
            norm_to_h(2)
            for cb in range(KC):
                g_, gb_ = acc(dr["w_pg"], cb, KC, hr, b_h)
                p_, pb_ = acc(dr["w_pp"], cb, PC, lambda k: pTb[:, k, :], [b_p] * PC)
                s1, s1b = tmp.next()
                y, yb = yt.next()
                fw.op("act", lambda h, s1=s1, g_=g_: h.activation(out=s1[:], in_=g_[:, 0:TT], func=AF.Sigmoid), reads=[gb_], writes=[s1b])
                fw.op("dve", lambda h, s1=s1, p_=p_: h.tensor_tensor(out=s1[:], in0=s1[:], in1=p_[:, 0:TT], op=ALU.mult), reads=[s1b, pb_], writes=[s1b])
                fw.op("dve", lambda h, s1=s1, y=y, cb=cb: h.tensor_tensor(out=y[:], in0=s1[:], in1=xT[:, cb, :], op=ALU.add), reads=[s1b, b_x[cb]], writes=[yb])
                fw.dma("sp", "y", lambda h, y=y, cb=cb, ts=ts: h.dma_start(out=dr["yT"][cb * 128:(cb + 1) * 128, ts], in_=y[:]), reads=[yb], writes=[b_y])
        fw.flush(barrier=True)
    return b_y


def dram_decls(nc, cfg, debug_feed_mix=False, debug_out=False):
    D, S, TO, KC, HID, MW, PLE = cfg.D, cfg.S, cfg.TO, cfg.KC, cfg.HID, cfg.MW, cfg.PLE
    NCP, NSB = S // 16, S // 64
    ext = lambda n, s, d=F32: nc.dram_tensor(n, list(s), d, kind="ExternalInput").ap()
    scr = lambda n, s, d=BF16: nc.dram_tensor(n, list(s), d, kind="Internal").ap()
    dr = {"xTpad": ext("xTpad", [D, S]), "gmix_col": ext("gmix_col", [128, KC]), "pT": ext("pT", [PLE, TO]), "gcols": ext("gcols", [128, 3, KC])}
    for n_ in ("w_kc", "w_vc", "w_ks", "w_kw"):
        dr[n_] = ext(n_, [4, 128, KC, 128])
    for n_ in ("w_q", "w_gq", "w_gk", "w_gv", "w_gz"):
        dr[n_] = ext(n_, [16, 128, KC, 128])
    dr["w_v_t"] = ext("w_v_t", [128, KC, 1024]); dr["w_gate_t"] = ext("w_gate_t", [128, KC, 48])
    dr["w_bd"] = ext("w_bd", [128, KC, 32])
    dr["kgain_cols"] = ext("kgain_cols", [128, 3]); dr["kcg_bc"] = ext("kcg_bc", [32, 128]); dr["qgain_col"] = ext("qgain_col", [128, 1])
    dr["w1k"] = ext("w1k", [128, 32, 512]); dr["w1v"] = ext("w1v", [128, 32, 512]); dr["w2k"] = ext("w2k", [128, 4, 128]); dr["w2v"] = ext("w2v", [128, 4, 128])
    dr["perepk"] = ext("perepk", [128, 32, 32]); dr["perepv"] = ext("perepv", [128, 32, 32])
    for n_, s_ in (("cosk", [128, S]), ("sink", [128, S]), ("cosc", [NCP, 64]), ("sinc", [NCP, 64]), ("cthr_row", [128, NCP]), ("cthr_col", [128, max(NCP // 128, 1)]),
                   ("tq_row", [128, TO]), ("tq_col", [128, TO // 128]), ("j_row", [128, NSB]), ("curp_col", [128, TO // 128]), ("validj_row", [128, NSB]), ("e0_row", [128, NSB]),
                   ("kvalid_col", [128, S // 128]), ("ebig", [NSB, S]), ("caus", [128, 4, 512]), ("band", [128, 8, 512]), ("prot", [128, 128]), ("ident128", [128, 128]),
                   ("gcst", [64, 5, 64]), ("cw", [128, 16, 3, 4]), ("adt", [64, 2, 16]), ("ogain", [64, 128])):
        dr[n_] = ext(n_, s_)
    dr["KS"] = scr("KS", [4, 128, S]); dr["KW"] = scr("KW", [4, 128, S])
    dr["VS"] = scr("VS", [4, 128, S // 128, 132]); dr["VW"] = scr("VW", [4, 128, S // 128, 132])
    dr["KC"] = scr("KC", [4, 128, NCP]); dr["VC"] = scr("VC", [4, 128, max(NCP // 128, 1), 132])
    for n_ in ("w_ga", "w_gb", "w_out", "w_pg"):
        dr[n_] = ext(n_, [KC, 128, KC, 128])
    dr["w_upa"] = ext("w_upa", [KC, 128, MW // 128, 128]); dr["w_upb"] = ext("w_upb", [KC, 128, MW // 128, 128])
    dr["w_mi"] = ext("w_mi", [HID // 128, 128, KC, 128]); dr["w_mo"] = ext("w_mo", [KC, 128, HID // 128, 128]); dr["w_pp"] = ext("w_pp", [KC, 128, PLE // 128, 128])
    dr["yT"] = nc.dram_tensor("yT", [D, TO], F32, kind="ExternalOutput").ap()
    if debug_feed_mix:
        dr["oaT"] = ext("oaT", [MW, TO], BF16); dr["obT"] = ext("obT", [MW, TO], BF16)
    else:
        mk = (lambda n, s_: nc.dram_tensor(n, list(s_), BF16, kind="ExternalOutput").ap()) if debug_out else scr
        dr["oaT"] = mk("oaT", [MW, TO]); dr["obT"] = mk("obT", [MW, TO])
        if debug_out:
            dr["dbg0"] = nc.dram_tensor("dbg0", [D, TO], BF16, kind="ExternalOutput").ap()
            dr["dbg1"] = nc.dram_tensor("dbg1", [D, TO], F32, kind="ExternalOutput").ap()
            dr["dbg2"] = nc.dram_tensor("dbg2", [D, TO], F32, kind="ExternalOutput").ap()
    return dr


def build_program(cfg, debug_feed_mix=False, debug_out=False):
    nc = bass.Bass("TRN2", target_bir_lowering=False)
    dr = dram_decls(nc, cfg, debug_feed_mix, debug_out)
    with ExitStack() as es:
        fw = FW(nc, es)
        if not debug_feed_mix:
            dr["_b_scr"] = emit_nsa_k(nc, fw, cfg, dr)
            emit_nsa_q(nc, fw, cfg, dr)
            emit_gdn(nc, fw, cfg, dr)
        emit_tail(nc, fw, cfg, dr)
    return nc


def tail_inputs(cfg, inp, c):
    D, TO, G = cfg.D, cfg.TO, cfg.G
    b, g = c // G, c % G
    t0 = g * TO
    off = 13392
    w_in = inp["w_in"][0]
    return {
        "pT": np.ascontiguousarray(inp["p"][0, b, t0:t0 + TO].T),
        "gcols": np.ascontiguousarray(np.stack([col_vec(inp["g_mix"][0]), col_vec(inp["g_mlp"][0]), col_vec(inp["g_ple"][0])], axis=1)),
        "w_ga": blk_w(w_in[:, off:off + D]), "w_gb": blk_w(w_in[:, off + D:off + 2 * D]),
        "w_upa": blk_w(inp["w_up_nsa"][0]), "w_upb": blk_w(inp["w_up_gdn"][0]), "w_out": blk_w(inp["w_out"][0]),
        "w_mi": blk_w(inp["w_mlp_in"][0]), "w_mo": blk_w(inp["w_mlp_out"][0]),
        "w_pg": blk_w(inp["w_ple_gate"][0]), "w_pp": blk_w(inp["w_ple_proj"][0]),
    }


def core_inputs(cfg, inp, c):
    m = {}
    m.update(nsa_inputs(cfg, inp, c))
    m.update(gdn_inputs(cfg, inp, c))
    m.update(tail_inputs(cfg, inp, c))
    return m


def percore_inputs(cfg, inp, c):
    b, g = c // cfg.G, c % cfg.G
    n = (g + 1) * cfg.TO
    xp = np.zeros((cfg.D, cfg.S), np.float32)
    xp[:, cfg.S - n:] = inp["x"][b, :n].T
    m = {"xTpad": xp, "pT": np.ascontiguousarray(inp["p"][0, b, g * cfg.TO:(g + 1) * cfg.TO].T)}
    m.update(nsa_tables(cfg, g))
    return m


_PROG = {}


def run_cfg(cfg, inputs):
    key = (cfg.D, cfg.S, cfg.NB, cfg.HID)
    if key not in _PROG:
        _PROG[key] = build_program(cfg)
    nc = _PROG[key]
    ncores = cfg.NB * cfg.G
    inp = {k_: np.asarray(v) for k_, v in inputs.items()}
    base = core_inputs(cfg, inp, 0)
    maps = [base]
    for c in range(1, ncores):
        m = dict(base)
        m.update(percore_inputs(cfg, inp, c))
        maps.append(m)
    res = run_bass_kernel_spmd(nc, maps, core_ids=list(range(ncores)))
    out = np.empty((cfg.NB, cfg.S, cfg.D), np.float32)
    for c in range(ncores):
        b, g_ = c // cfg.G, c % cfg.G
        out[b, g_ * cfg.TO:(g_ + 1) * cfg.TO] = np.asarray(res.results[c]["yT"]).T
    return out


def kernel(**inputs):
    return run_cfg(Cfg(), inputs)
```

```python
from contextlib import ExitStack
import numpy as np
import concourse.bass as bass
import concourse.mybir as mybir
from concourse.bass_utils import run_bass_kernel_spmd

F32 = mybir.dt.float32
BF16 = mybir.dt.bfloat16
ALU = mybir.AluOpType
AF = mybir.ActivationFunctionType
EPS = 1e-6


class Cfg:
    def __init__(s, D=4096, S=8192, NB=2, HID=16384, PLE=256, G=4):
        s.D, s.S, s.NB, s.HID, s.PLE, s.G = D, S, NB, HID, PLE, G
        s.TO = S // G
        s.KC = D // 128
        s.TT = 512
        s.MW = 2048


class Lane:
    def __init__(self, name, sem, step):
        self.name, self.sem, self.step, self.count = name, sem, step, 0


class Buf:
    __slots__ = ("name", "w", "r", "lane")

    def __init__(self, name=""):
        self.name, self.w, self.r, self.lane = name, None, {}, None


class Eng:
    def __init__(self, name, lane):
        self.name, self.lane, self.waited, self.prog = name, lane, {}, []


class FW:
    ENG = ("pe", "act", "dve", "pool", "sp")

    def __init__(self, nc, es):
        self.nc, self.es = nc, es
        self.eng = {}
        for n in self.ENG:
            sem = es.enter_context(nc.semaphore("s_" + n))
            self.eng[n] = Eng(n, Lane(n, sem, 1))
        self.lanes = {}

    def dma_lane(self, name):
        if name not in self.lanes:
            sem = self.es.enter_context(self.nc.semaphore("d_" + name))
            self.lanes[name] = Lane(name, sem, 16)
        return self.lanes[name]

    def _deps(self, e, reads, writes, pe_acc=False):
        need = {}

        def add(lv):
            if lv is not None and need.get(lv[0], 0) < lv[1]:
                need[lv[0]] = lv[1]
        for b in reads:
            add(b.w)
        for b in writes:
            if not (pe_acc and b.w is not None and b.w[0] is e.lane):
                add(b.w)
            for lane, v in b.r.items():
                add((lane, v))
        waits = []
        for lane, v in need.items():
            if e.waited.get(lane, 0) < v:
                e.waited[lane] = v
                waits.append((lane.sem, v))
        return waits

    @staticmethod
    def _mark(lane, val, reads, writes):
        for b in reads:
            if b.r.get(lane, 0) < val:
                b.r[lane] = val
        for b in writes:
            b.w, b.r = (lane, val), {}

    def op(self, ename, fn, reads=(), writes=(), pe_acc=False):
        e = self.eng[ename]
        waits = self._deps(e, reads, writes, pe_acc)
        e.lane.count += 1
        sem = e.lane.sem

        def run(h, waits=waits, fn=fn, sem=sem):
            for s, v in waits:
                h.wait_ge(s, v)
            fn(h).then_inc(sem, 1)
        e.prog.append(run)
        self._mark(e.lane, e.lane.count, reads, writes)

    def dma(self, qname, lane_name, fn, reads=(), writes=()):
        e = self.eng[qname]
        b0 = writes[0]
        if b0.lane is None:
            self._nl = getattr(self, "_nl", 0) + 1
            b0.lane = self.dma_lane("%s_%d" % (lane_name, self._nl))
        lane = b0.lane
        waits = self._deps(e, reads, writes)
        lane.count += 1
        sem = lane.sem

        def run(h, waits=waits, fn=fn, sem=sem):
            for s, v in waits:
                h.wait_ge(s, v)
            fn(h).then_inc(sem, 16)
        e.prog.append(run)
        self._mark(lane, lane.count * 16, reads, writes)

    def flush(self, barrier=True):
        if barrier:
            finals = [(e.lane.sem, e.lane.count, e.lane) for e in self.eng.values() if e.lane.count]
            finals += [(l.sem, l.count * 16, l) for l in self.lanes.values() if l.count]
            for e in self.eng.values():
                ws = []
                for sem, v, lane in finals:
                    if lane is e.lane:
                        continue
                    if e.waited.get(lane, 0) < v:
                        e.waited[lane] = v
                        ws.append((sem, v))

                def bar(h, ws=ws):
                    for s, v in ws:
                        h.wait_ge(s, v)
                e.prog.append(bar)
        progs = {n: self.eng[n].prog for n in self.ENG}
        for n in self.ENG:
            self.eng[n].prog = []
        with self.nc.Block() as block:
            @block.tensor
            def _(h):
                for f in progs["pe"]:
                    f(h)

            @block.scalar
            def _(h):
                for f in progs["act"]:
                    f(h)

            @block.vector
            def _(h):
                for f in progs["dve"]:
                    f(h)

            @block.gpsimd
            def _(h):
                for f in progs["pool"]:
                    f(h)

            @block.sync
            def _(h):
                for f in progs["sp"]:
                    f(h)


class Ring:
    def __init__(self, es, nc, name, shape, dt, n, psum=False):
        mk = nc.psum_tensor if psum else nc.sbuf_tensor
        self.t = [es.enter_context(mk("%s%d" % (name, i), list(shape), dt)) for i in range(n)]
        self.b = [Buf("%s%d" % (name, i)) for i in range(n)]
        self.i = -1

    def next(self):
        self.i = (self.i + 1) % len(self.t)
        return self.t[self.i], self.b[self.i]


def blk_w(w):
    K, N = w.shape
    return np.ascontiguousarray(w.reshape(K // 128, 128, N // 128, 128).transpose(2, 1, 0, 3))


def col_vec(v):
    return np.ascontiguousarray(v.reshape(-1, 128).T)


SCALE = 128.0 ** -0.5
BIG = 1.0e4
GK = 2.0 * (2.0 / np.pi) ** 0.5


def rope_tab(t):
    inv = (np.float32(10000.0) ** (-np.arange(64, dtype=np.float32) / np.float32(64))).astype(np.float32)
    ang = t.astype(np.float32)[:, None] * inv[None, :]
    return np.cos(ang).astype(np.float32), np.sin(ang).astype(np.float32)


def nsa_tables(cfg, g):
    S, TO = cfg.S, cfg.TO
    PAD = S - (g + 1) * TO
    idx = np.arange(S)
    t = (idx - PAD).astype(np.float32)
    c, s = rope_tab(t)
    cosk = np.ascontiguousarray(np.concatenate([c, c], 1).T)
    sink = np.ascontiguousarray(np.concatenate([s, s], 1).T)
    NCP = S // 16
    n_true = np.arange(NCP) - PAD // 16
    cend = (16 * n_true + 31).astype(np.float32)
    cc, sc = rope_tab(cend)
    thr = np.where(n_true >= 0, cend, 1e9).astype(np.float32)
    q_idx = np.arange(S - TO, S)
    tq = (q_idx - PAD).astype(np.float32)
    NSB = S // 64
    j0 = PAD // 64
    jp = np.arange(NSB)
    bc = lambda v: np.ascontiguousarray(np.broadcast_to(v[None, :].astype(np.float32), (128, v.shape[0])))
    colf = lambda v: np.ascontiguousarray(v.astype(np.float32).reshape(-1, 128).T)
    return {
        "cosk": cosk, "sink": sink, "cosc": cc, "sinc": sc,
        "cthr_row": bc(thr), "cthr_col": colf(thr),
        "tq_row": bc(tq), "tq_col": colf(tq),
        "j_row": bc(jp), "curp_col": colf(q_idx // 64), "validj_row": bc((jp >= j0)), "e0_row": bc((jp == j0)),
        "kvalid_col": colf((idx >= PAD)),
    }


def nsa_consts(cfg):
    S = cfg.S
    NSB = S // 64
    Eb = (np.arange(S)[None, :] // 64 == np.arange(NSB)[:, None]).astype(np.float32)
    p = np.arange(128)[:, None, None]
    rel = np.arange(8)[None, :, None]
    q = np.arange(512)[None, None, :]
    caus = ((128 * rel[:, :4] + p) <= q).astype(np.float32)
    d = q - (128 * (rel - 4) + p)
    band = ((d >= 0) & (d < 512)).astype(np.float32)
    prot = np.zeros((128, 128), np.float32)
    for m in range(64):
        prot[m + 64, m] = -1.0
        prot[m, m + 64] = 1.0
    return {"ebig": Eb, "caus": np.ascontiguousarray(caus), "band": np.ascontiguousarray(band), "prot": prot,
            "ident128": np.eye(128, dtype=np.float32)}


class Common:
    def __init__(self, nc, fw, cfg, dr, es, pfx, with_bkb=True, nw=3):
        self.nc, self.fw, self.cfg, self.dr = nc, fw, cfg, dr
        KC = cfg.KC
        sb = lambda n, s, d: es.enter_context(nc.sbuf_tensor(pfx + n, list(s), d))
        self.sb = sb
        self.hT = sb("h", [128, KC, 512], BF16); self.b_h = [Buf() for _ in range(KC)]
        self.gcol = sb("g", [128, KC], F32); self.b_g = Buf()
        self.ones = sb("ones", [128, 128], BF16); self.b_c = Buf()
        self.epsc = sb("eps", [128, 1], F32)
        self.rstd = sb("rstd", [128, 512], F32); self.b_rstd = Buf()
        self.xr = Ring(es, nc, pfx + "xr", [128, 512], F32, 3)
        self.sq = Ring(es, nc, pfx + "sq", [128, 512], BF16, 2)
        self.wring = Ring(es, nc, pfx + "w", [128, KC, 128], BF16, nw)
        self.bank = Ring(es, nc, pfx + "bk", [128, 512], F32, 6, psum=True)
        self.bkb = Ring(es, nc, pfx + "bkb", [128, 1024], BF16, 2, psum=True) if with_bkb else None
        fw.dma("sp", "cst", lambda h: h.dma_start(out=self.gcol[:], in_=dr["gmix_col"]), writes=[self.b_g])
        fw.op("dve", lambda h: h.memset(self.ones[:], 1.0), writes=[self.b_c])
        fw.op("dve", lambda h: h.memset(self.epsc[:], EPS), writes=[self.b_g])

    def make_h(self, col0):
        fw, KC, D = self.fw, self.cfg.KC, self.cfg.D
        ts = slice(col0, col0 + 512)
        bk, bb = self.bank.next()
        for k in range(KC):
            x, xb = self.xr.next()
            fw.dma("sp", "x", lambda h, k=k, x=x, ts=ts: h.dma_start(out=x[:], in_=self.dr["xTpad"][k * 128:(k + 1) * 128, ts]), writes=[xb])
            st, sbf = self.sq.next()
            fw.op("act", lambda h, x=x, st=st: h.activation(out=st[:], in_=x[:], func=AF.Square), reads=[xb], writes=[sbf])
            fw.op("pe", lambda h, k=k, st=st, bk=bk: h.matmul(bk[:, :], lhsT=self.ones[:], rhs=st[:], start=(k == 0), stop=(k == KC - 1)),
                  reads=[self.b_c, sbf], writes=[bb], pe_acc=(k > 0))
        fw.op("act", lambda h, bk=bk: h.activation(out=self.rstd[:], in_=bk[:, :], func=AF.Sqrt, bias=self.epsc[:], scale=1.0 / D), reads=[bb, self.b_g], writes=[self.b_rstd])
        fw.op("dve", lambda h: h.reciprocal(out=self.rstd[:], in_=self.rstd[:]), reads=[self.b_rstd], writes=[self.b_rstd])
        for k in range(KC):
            x, xb = self.xr.next()
            fw.dma("sp", "x", lambda h, k=k, x=x, ts=ts: h.dma_start(out=x[:], in_=self.dr["xTpad"][k * 128:(k + 1) * 128, ts]), writes=[xb])
            fw.op("dve", lambda h, k=k, x=x: h.scalar_tensor_tensor(out=self.hT[:, k, :], in0=x[:], scalar=self.gcol[:, k:k + 1], in1=self.rstd[:], op0=ALU.mult, op1=ALU.mult),
                  reads=[xb, self.b_g, self.b_rstd], writes=[self.b_h[k]])

    def proj_f(self, wd, cb):
        fw, KC = self.fw, self.cfg.KC
        wt, wb = self.wring.next()
        fw.dma("pool", "w", lambda h: h.dma_start(out=wt[:], in_=wd[cb]), writes=[wb])
        bk, bb = self.bank.next()
        for k in range(KC):
            fw.op("pe", lambda h, k=k: h.matmul(bk[:, :], lhsT=wt[:, k, :], rhs=self.hT[:, k, :], start=(k == 0), stop=(k == KC - 1)),
                  reads=[wb, self.b_h[k]], writes=[bb], pe_acc=(k > 0))
        return bk, bb

    def mm(self, out, lhsT, rhs, rd, wr, start=True, stop=True, acc=False):
        self.fw.op("pe", lambda h: h.matmul(out, lhsT=lhsT, rhs=rhs, start=start, stop=stop), reads=rd, writes=[wr], pe_acc=acc)

    def tr(self, out, in_, ident, rd, wr):
        self.fw.op("pe", lambda h: h.transpose(out, in_, ident), reads=rd, writes=[wr])

    def rsq(self, ss, ssb, n, scale):
        fw = self.fw
        fw.op("act", lambda h: h.activation(out=ss, in_=ss, func=AF.Sqrt, bias=self.epsc[0:n, :], scale=scale), reads=[ssb, self.b_g], writes=[ssb])
        fw.op("dve", lambda h: h.reciprocal(out=ss, in_=ss), reads=[ssb], writes=[ssb])

    def norm_rope_f(self, bk, bb, gain_col, cosT, sinT, b_tab, prot, out, outb, scratch):
        fw = self.fw
        f32r, b16r = scratch
        xq, xqb = f32r.next()
        fw.op("act", lambda h: h.activation(out=xq[:], in_=bk[:, :], func=AF.Copy), reads=[bb], writes=[xqb])
        s2, s2b = b16r.next()
        fw.op("act", lambda h: h.activation(out=s2[:], in_=bk[:, :], func=AF.Square), reads=[bb], writes=[s2b])
        b2, b2b = self.bank.next()
        self.mm(b2[:, :], self.ones[:], s2[:], [self.b_c, s2b], b2b)
        rn, rnb = f32r.next()
        fw.op("act", lambda h: h.activation(out=rn[:], in_=b2[:, :], func=AF.Sqrt, bias=self.epsc[:], scale=1.0 / 128), reads=[b2b, self.b_g], writes=[rnb])
        fw.op("dve", lambda h: h.reciprocal(out=rn[:], in_=rn[:]), reads=[rnb], writes=[rnb])
        fw.op("dve", lambda h: h.scalar_tensor_tensor(out=xq[:], in0=xq[:], scalar=gain_col, in1=rn[:], op0=ALU.mult, op1=ALU.mult), reads=[xqb, rnb, self.b_c], writes=[xqb])
        xb16, xb16b = b16r.next()
        fw.op("act", lambda h: h.activation(out=xb16[:], in_=xq[:], func=AF.Copy), reads=[xqb], writes=[xb16b])
        b3, b3b = self.bank.next()
        self.mm(b3[:, :], prot, xb16[:], [self.b_c, xb16b], b3b)
        fw.op("dve", lambda h: h.tensor_tensor(out=xq[:], in0=xq[:], in1=cosT, op=ALU.mult), reads=[xqb, b_tab], writes=[xqb])
        fw.op("dve", lambda h: h.tensor_tensor(out=rn[:], in0=b3[:, :], in1=sinT, op=ALU.mult), reads=[b3b, b_tab], writes=[rnb])
        fw.op("dve", lambda h: h.tensor_tensor(out=out, in0=xq[:], in1=rn[:], op=ALU.add), reads=[xqb, rnb], writes=[outb])


def emit_nsa_k(nc, fw, cfg, dr):
    D, KC, S = cfg.D, cfg.KC, cfg.S
    NBLK = S // 512
    b_scr = Buf()
    with ExitStack() as es:
        cm = Common(nc, fw, cfg, dr, es, "n1_", nw=2)
        sb = cm.sb
        gains = sb("gains", [128, 3], F32)
        kcg = sb("kcg", [32, 128], F32)
        prot = sb("prot", [128, 128], BF16)
        idb = sb("idb", [128, 128], BF16)
        wvr = Ring(es, nc, "n1_wv", [128, KC, 512], BF16, 1)
        w1 = {"k": sb("w1k", [128, 32, 512], BF16), "v": sb("w1v", [128, 32, 512], BF16)}
        w2 = {"k": sb("w2k", [128, 4, 128], BF16), "v": sb("w2v", [128, 4, 128], BF16)}
        peT = {"k": sb("pek", [128, 32], BF16), "v": sb("pev", [128, 32], BF16)}
        cbias = {"k": sb("cbk", [128, 4], F32), "v": sb("cbv", [128, 4], F32)}
        b_w1 = Buf()
        tabs = Ring(es, nc, "n1_tab", [128, 2, 512], F32, 2)
        ctab = Ring(es, nc, "n1_ctab", [32, 2, 64], F32, 2)
        f32r = Ring(es, nc, "n1_f32", [128, 512], F32, 4)
        b16r = Ring(es, nc, "n1_b16", [128, 512], BF16, 4)
        outr = Ring(es, nc, "n1_out", [128, 512], BF16, 3)
        vout = Ring(es, nc, "n1_vout", [128, 8, 132], BF16, 4)
        cT = {(gq, kv): sb("cT%d%s" % (gq, kv), [128, 16 + 512], BF16) for gq in range(4) for kv in "kv"}
        b_cT = {k: Buf() for k in cT}
        hid = Ring(es, nc, "n1_hid", [128, 4, 32], BF16, 2)
        s32 = Ring(es, nc, "n1_s32", [128, 32], F32, 6)
        t32 = Ring(es, nc, "n1_t32", [32, 128], F32, 4)
        tb32 = Ring(es, nc, "n1_tb32", [32, 132], BF16, 3)
        c32 = Ring(es, nc, "n1_c32", [32, 2], F32, 4)
        kco = Ring(es, nc, "n1_kco", [128, 32], BF16, 2)
        zc = sb("zc", [128, 16], BF16)
        zv = sb("zv", [1, 132], BF16)

        fw.dma("sp", "cst", lambda h: h.dma_start(out=gains[:], in_=dr["kgain_cols"]), writes=[cm.b_c])
        fw.dma("sp", "cst", lambda h: h.dma_start(out=kcg[:], in_=dr["kcg_bc"]), writes=[cm.b_c])
        fw.dma("pool", "w", lambda h: h.dma_start(out=prot[:], in_=dr["prot"]), writes=[cm.b_c])
        fw.dma("pool", "w", lambda h: h.dma_start(out=idb[:], in_=dr["ident128"]), writes=[cm.b_c])
        for kv in "kv":
            for l0 in range(0, 32, 4):
                fw.dma("pool", "w", lambda h, kv=kv, l0=l0: h.dma_start(out=w1[kv][:, l0:l0 + 4, :], in_=dr["w1" + kv][:, l0:l0 + 4, :]), writes=[b_w1])
            fw.dma("pool", "w", lambda h, kv=kv: h.dma_start(out=w2[kv][:], in_=dr["w2" + kv]), writes=[b_w1])
            fw.dma("pool", "w", lambda h, kv=kv: h.dma_start(out=peT[kv][:], in_=dr["pe" + kv]), writes=[b_w1])
        for k in cT:
            fw.op("dve", lambda h, k=k: h.memset(cT[k][:, 0:16], 0.0), writes=[b_cT[k]])
        fw.op("dve", lambda h: h.memset(zc[:], 0.0), writes=[cm.b_c])
        fw.op("dve", lambda h: h.memset(zv[:], 0.0), writes=[cm.b_c])
        for i_ in range(2):
            fw.op("dve", lambda h, i_=i_: h.memset(ctab.t[i_][:], 0.0), writes=[ctab.b[i_]])
        for i_ in range(4):
            fw.op("dve", lambda h, i_=i_: h.memset(vout.t[i_][:, :, 128:132], 1.0), writes=[vout.b[i_]])
        for i_ in range(3):
            fw.op("dve", lambda h, i_=i_: h.memset(tb32.t[i_][:, 128:132], 1.0), writes=[tb32.b[i_]])
        for kv in "kv":
            bk, bb = cm.bank.next()
            for hb in range(4):
                for l in range(32):
                    cm.mm(bk[:, hb:hb + 1], w1[kv][:, l, hb * 128:(hb + 1) * 128], peT[kv][:, l:l + 1], [b_w1], bb, start=(l == 0), stop=(l == 31), acc=not (hb == 0 and l == 0))
            fw.op("act", lambda h, kv=kv, bk=bk: h.activation(out=cbias[kv][:], in_=bk[:, 0:4], func=AF.Copy), reads=[bb], writes=[b_w1])
        for gq in range(4):
            fw.dma("sp", "scr", lambda h, gq=gq: h.dma_start(out=dr["KC"][gq][:, S // 16 - 16:S // 16], in_=zc[:]), reads=[cm.b_c], writes=[b_scr])
            fw.dma("sp", "scr", lambda h, gq=gq: h.dma_start(out=dr["VC"][gq][127:128, (S // 16 - 1) // 128, :], in_=zv[0:1, :]), reads=[cm.b_c], writes=[b_scr])

        for blk in range(NBLK):
            c0 = blk * 512
            cm.make_h(c0)
            tb, tbb = tabs.next()
            fw.dma("sp", "tab", lambda h, tb=tb, c0=c0: h.dma_start(out=tb[:, 0, :], in_=dr["cosk"][:, c0:c0 + 512]), writes=[tbb])
            fw.dma("sp", "tab", lambda h, tb=tb, c0=c0: h.dma_start(out=tb[:, 1, :], in_=dr["sink"][:, c0:c0 + 512]), writes=[tbb])
            ct, ctb = ctab.next()
            n0 = c0 // 16 - 1
            j_lo = 1 if blk == 0 else 0
            fw.dma("sp", "tab", lambda h, ct=ct, n0=n0, j_lo=j_lo: h.dma_start(out=ct[j_lo:32, 0, :], in_=dr["cosc"][n0 + j_lo:n0 + 32, :]), writes=[ctb])
            fw.dma("sp", "tab", lambda h, ct=ct, n0=n0, j_lo=j_lo: h.dma_start(out=ct[j_lo:32, 1, :], in_=dr["sinc"][n0 + j_lo:n0 + 32, :]), writes=[ctb])
            need_w = blk >= NBLK - (cfg.TO // 512) - 1
            vos = [vout.next() for _ in range(4)]
            for half in range(2 if need_w else 1):
                wv, b_wv = wvr.next()
                fw.dma("pool", "w", lambda h, wv=wv, half=half: h.dma_start(out=wv[:], in_=dr["w_v_t"][:, :, half * 512:(half + 1) * 512]), writes=[b_wv])
                for tl in range(4):
                    vo, vob = vos[tl]
                    bk, bb = cm.bank.next()
                    for k in range(KC):
                        cm.mm(bk[:, :], cm.hT[:, k, tl * 128:(tl + 1) * 128], wv[:, k, :], [cm.b_h[k], b_wv], bb, start=(k == 0), stop=(k == KC - 1), acc=(k > 0))
                    fw.op("act", lambda h, vo=vo, bk=bk, half=half: h.activation(out=vo[:, half * 4:(half + 1) * 4, 0:128], in_=bk[:, :].rearrange("p (g c) -> p g c", c=128), func=AF.Copy), reads=[bb], writes=[vob])
            for tl in range(4):
                vo, vob = vos[tl]
                tile_ = c0 // 128 + tl
                for g4 in range(4):
                    fw.dma("sp", "scr", lambda h, vo=vo, g4=g4, tile_=tile_: h.dma_start(out=dr["VS"][g4][:, tile_, :], in_=vo[:, g4, :]), reads=[vob], writes=[b_scr])
                    if need_w:
                        fw.dma("sp", "scr", lambda h, vo=vo, g4=g4, tile_=tile_: h.dma_start(out=dr["VW"][g4][:, tile_, :], in_=vo[:, 4 + g4, :]), reads=[vob], writes=[b_scr])
            for gq in range(4):
                for nm, gi, dst in (("w_ks", 0, "KS"), ("w_kw", 1, "KW")):
                    if dst == "KW" and not need_w:
                        continue
                    bk, bb = cm.proj_f(dr[nm], gq)
                    o, ob = outr.next()
                    cm.norm_rope_f(bk, bb, gains[:, gi:gi + 1], tb[:, 0, :], tb[:, 1, :], tbb, prot[:], o[:], ob, (f32r, b16r))
                    fw.dma("sp", "scr", lambda h, o=o, dst=dst, gq=gq, c0=c0: h.dma_start(out=dr[dst][gq][:, c0:c0 + 512], in_=o[:]), reads=[ob], writes=[b_scr])
                for kv, nm in (("k", "w_kc"), ("v", "w_vc")):
                    bk, bb = cm.proj_f(dr[nm], gq)
                    t_, tb_ = cT[(gq, kv)], b_cT[(gq, kv)]
                    fw.op("act", lambda h, t_=t_, bk=bk: h.activation(out=t_[:, 16:16 + 512], in_=bk[:, :], func=AF.Copy), reads=[bb], writes=[tb_])
                    hd_, hdb = hid.next()
                    for hb in range(4):
                        b2, b2b = cm.bank.next()
                        for l in range(32):
                            cm.mm(b2[:, 0:32], w1[kv][:, l, hb * 128:(hb + 1) * 128], t_[:, l:l + 16 * 31 + 1:16], [b_w1, tb_], b2b, start=(l == 0), stop=(l == 31), acc=(l > 0))
                        x_, xb_ = s32.next(); y_, yb_ = s32.next()
                        fw.op("act", lambda h, x_=x_, b2=b2, kv=kv, hb=hb: h.activation(out=x_[:], in_=b2[:, 0:32], func=AF.Identity, bias=cbias[kv][:, hb:hb + 1]), reads=[b2b, b_w1], writes=[xb_])
                        fw.op("dve", lambda h, x_=x_, y_=y_: h.tensor_tensor(out=y_[:], in0=x_[:], in1=x_[:], op=ALU.mult), reads=[xb_], writes=[yb_])
                        fw.op("dve", lambda h, y_=y_: h.tensor_scalar(out=y_[:], in0=y_[:], scalar1=0.044715, scalar2=1.0, op0=ALU.mult, op1=ALU.add), reads=[yb_], writes=[yb_])
                        fw.op("dve", lambda h, x_=x_, y_=y_: h.tensor_tensor(out=y_[:], in0=y_[:], in1=x_[:], op=ALU.mult), reads=[xb_, yb_], writes=[yb_])
                        fw.op("act", lambda h, y_=y_: h.activation(out=y_[:], in_=y_[:], func=AF.Sigmoid, scale=float(GK)), reads=[yb_], writes=[yb_])
                        fw.op("dve", lambda h, x_=x_, y_=y_, hd_=hd_, hb=hb: h.tensor_tensor(out=hd_[:, hb, :], in0=x_[:], in1=y_[:], op=ALU.mult), reads=[xb_, yb_], writes=[hdb])
                    fw.op("dve", lambda h, t_=t_: h.tensor_copy(out=t_[:, 0:16], in_=t_[:, 512:528]), reads=[tb_], writes=[tb_])
                    b3, b3b = cm.bank.next()
                    for hb in range(4):
                        cm.mm(b3[0:32, 0:128], hd_[:, hb, :], w2[kv][:, hb, :], [hdb, b_w1], b3b, start=(hb == 0), stop=(hb == 3), acc=(hb > 0))
                    if kv == "v":
                        vc, vcb = tb32.next()
                        fw.op("act", lambda h, vc=vc, b3=b3: h.activation(out=vc[:, 0:128], in_=b3[0:32, 0:128], func=AF.Copy), reads=[b3b], writes=[vcb])
                        r0 = n0 + j_lo
                        while r0 < n0 + 32:
                            r1 = min(n0 + 32, (r0 // 128 + 1) * 128)
                            fw.dma("sp", "scr", lambda h, vc=vc, gq=gq, n0=n0, r0=r0, r1=r1: h.dma_start(out=dr["VC"][gq][r0 % 128:r0 % 128 + (r1 - r0), r0 // 128, :], in_=vc[r0 - n0:r1 - n0, :]), reads=[vcb], writes=[b_scr])
                            r0 = r1
                    else:
                        cs_, csb = c32.next()
                        jk, jkb = tb32.next()
                        fw.op("act", lambda h, jk=jk, b3=b3, cs_=cs_: h.activation(out=jk[:, 0:128], in_=b3[0:32, 0:128], func=AF.Square, accum_out=cs_[:, 0:1]), reads=[b3b], writes=[jkb, csb])
                        cm.rsq(cs_[:, 0:1], csb, 32, 1.0 / 128)
                        kn, knb = t32.next()
                        fw.op("dve", lambda h, kn=kn, b3=b3, cs_=cs_: h.scalar_tensor_tensor(out=kn[:], in0=b3[0:32, 0:128], scalar=cs_[:, 0:1], in1=kcg[:], op0=ALU.mult, op1=ALU.mult), reads=[b3b, csb, cm.b_c], writes=[knb])
                        r1, r1b = t32.next(); r2, r2b = t32.next()
                        fw.op("dve", lambda h, r1=r1, kn=kn, ct=ct: h.tensor_tensor(out=r1[:, 0:64], in0=kn[:, 0:64], in1=ct[:, 0, :], op=ALU.mult), reads=[knb, ctb], writes=[r1b])
                        fw.op("dve", lambda h, r1=r1, kn=kn, ct=ct: h.tensor_tensor(out=r1[:, 64:128], in0=kn[:, 64:128], in1=ct[:, 0, :], op=ALU.mult), reads=[knb, ctb], writes=[r1b])
                        fw.op("dve", lambda h, r2=r2, kn=kn, ct=ct: h.tensor_tensor(out=r2[:, 0:64], in0=kn[:, 64:128], in1=ct[:, 1, :], op=ALU.mult), reads=[knb, ctb], writes=[r2b])
                        fw.op("dve", lambda h, r2=r2, kn=kn, ct=ct: h.tensor_tensor(out=r2[:, 64:128], in0=kn[:, 0:64], in1=ct[:, 1, :], op=ALU.mult), reads=[knb, ctb], writes=[r2b])
                        kr, krb = tb32.next()
                        fw.op("dve", lambda h, kr=kr, r1=r1, r2=r2: h.tensor_tensor(out=kr[:, 0:64], in0=r1[:, 0:64], in1=r2[:, 0:64], op=ALU.subtract), reads=[r1b, r2b], writes=[krb])
                        fw.op("dve", lambda h, kr=kr, r1=r1, r2=r2: h.tensor_tensor(out=kr[:, 64:128], in0=r1[:, 64:128], in1=r2[:, 64:128], op=ALU.add), reads=[r1b, r2b], writes=[krb])
                        pb, pbb = cm.bkb.next()
                        cm.tr(pb[:, 0:32], kr[:, 0:128], idb[0:32, 0:32], [krb, cm.b_c], pbb)
                        ko, kob = kco.next()
                        fw.op("act", lambda h, ko=ko, pb=pb: h.activation(out=ko[:], in_=pb[:, 0:32], func=AF.Copy), reads=[pbb], writes=[kob])
                        fw.dma("sp", "scr", lambda h, ko=ko, gq=gq, n0=n0, j_lo=j_lo: h.dma_start(out=dr["KC"][gq][:, n0 + j_lo:n0 + 32], in_=ko[:, j_lo:32]), reads=[kob], writes=[b_scr])
        fw.flush(barrier=True)
    return b_scr


def emit_nsa_q(nc, fw, cfg, dr):
    D, KC, S, TO = cfg.D, cfg.KC, cfg.S, cfg.TO
    NOWN = TO // 512
    NCP, NSB, NKT = S // 16, S // 64, S // 128
    NCT = max(NCP // 128, 1)
    CW = min(NCP, 512)
    b_out = Buf()
    with ExitStack() as es:
        cm = Common(nc, fw, cfg, dr, es, "n2_", with_bkb=False, nw=2)
        sb = cm.sb
        qg = sb("qg", [128, 1], F32)
        prot = sb("prot", [128, 128], BF16)
        idb = sb("idb", [128, 128], BF16)
        idf = sb("idf", [128, 128], F32)
        wgt = sb("wgt", [128, KC, 48], BF16); b_wgt = Buf()
        ebig = sb("ebig", [NSB, S], BF16)
        caus = sb("caus", [128, 4, 512], BF16)
        band = sb("band", [128, 8, 512], BF16)
        tq_row = sb("tqr", [128, TO], F32)
        colt = sb("colt", [128, 3, max(TO // 128, NKT, NCT)], F32)
        kval = sb("kval", [128, NKT], F32)
        cthc = sb("cthc", [128, NCT], F32)
        rowt = sb("rowt", [128, 4, NSB], F32)
        cthr = sb("cthr", [128, NCP], F32)
        tabs = Ring(es, nc, "n2_tab", [128, 2, 512], F32, 1)
        f32r = Ring(es, nc, "n2_f32", [128, 512], F32, 4)
        b16r = Ring(es, nc, "n2_b16", [128, 512], BF16, 4)
        qT = [sb("qT%d" % i, [128, 512], BF16) for i in range(4)]; b_qT = [Buf() for _ in range(4)]
        gate = sb("gate", [128, 4, 48], F32); b_gate = Buf()
        KSs = sb("KSs", [128, S], BF16); b_KS = Buf()
        VSs = sb("VSs", [128, NKT, 132], BF16); b_VS = Buf()
        KWs = sb("KWs", [128, 1024], BF16); b_KW = Buf()
        VWs = sb("VWs", [128, 8, 132], BF16); b_VW = Buf()
        KCs = sb("KCs", [128, NCP], BF16); b_KC = Buf()
        VCs = sb("VCs", [128, NCT, 132], BF16); b_VC = Buf()
        mrow = sb("mrow", [128, 4, CW], F32); b_mrow = Buf()
        mcT = sb("mcT", [128, NCT, 512], BF16); b_mcT = Buf()
        mW = sb("mW", [128, 8, 512], BF16); b_mW = Buf()
        p4 = sb("p4", [128, NCP + 4], F32); b_p4 = Buf()
        selT = sb("selT", [NSB, 512], BF16); b_selT = Buf()
        mS = Ring(es, nc, "n2_mS", [128, 512], BF16, 3)
        pT = Ring(es, nc, "n2_pT", [128, 512], BF16, 4)
        pr = Ring(es, nc, "n2_pr", [128, CW], F32, 3)
        sc = Ring(es, nc, "n2_sc", [128, 4, NSB], F32, 2)
        c8 = Ring(es, nc, "n2_c8", [128, 16], F32, 6)
        oacc = sb("oacc", [128, 4, 128], F32); b_oacc = Buf()
        ob16 = Ring(es, nc, "n2_ob16", [128, 128], BF16, 2)
        oT = Ring(es, nc, "n2_oT", [128, 512], BF16, 2)
        accb = Ring(es, nc, "n2_acc", [128, 512], F32, 2, psum=True)

        ld = lambda dst, src, q="sp", lane="cst", wr=cm.b_c: fw.dma(q, lane, lambda h: h.dma_start(out=dst, in_=src), writes=[wr])
        ld(qg[:], dr["qgain_col"]); ld(tq_row[:], dr["tq_row"]); ld(colt[:, 0, 0:TO // 128], dr["tq_col"]); ld(colt[:, 1, 0:TO // 128], dr["curp_col"])
        ld(kval[:], dr["kvalid_col"]); ld(cthc[:], dr["cthr_col"]); ld(cthr[:], dr["cthr_row"]); ld(idf[:], dr["ident128"])
        ld(rowt[:, 0, :], dr["j_row"]); ld(rowt[:, 1, :], dr["validj_row"]); ld(rowt[:, 2, :], dr["e0_row"])
        for dst, nm in ((prot[:], "prot"), (idb[:], "ident128"), (caus[:], "caus"), (band[:], "band")):
            ld(dst, dr[nm], q="pool", lane="w")
        for e0 in range(0, S, 2048):
            ld(ebig[:, e0:e0 + 2048], dr["ebig"][:, e0:e0 + 2048], q="pool", lane="w")
        ld(wgt[:], dr["w_gate_t"], q="pool", lane="w", wr=b_wgt)
        fw.op("dve", lambda h: h.memset(p4[:], 0.0), writes=[b_p4])

        for ob_ in range(NOWN):
            blk = S // 512 - NOWN + ob_
            c0 = blk * 512
            cm.make_h(c0)
            tb, tbb = tabs.next()
            fw.dma("sp", "tab", lambda h, tb=tb, c0=c0: h.dma_start(out=tb[:, 0, :], in_=dr["cosk"][:, c0:c0 + 512]), writes=[tbb])
            fw.dma("sp", "tab", lambda h, tb=tb, c0=c0: h.dma_start(out=tb[:, 1, :], in_=dr["sink"][:, c0:c0 + 512]), writes=[tbb])
            qs = slice(ob_ * 512, ob_ * 512 + 512)
            nkt = (c0 + 512) // 128
            nct = min(NCT, (c0 + 512) // 16 // 128 + 1)
            for tl in range(4):
                bk, bb = cm.bank.next()
                for k in range(KC):
                    cm.mm(bk[:, 0:48], cm.hT[:, k, tl * 128:(tl + 1) * 128], wgt[:, k, :], [cm.b_h[k], b_wgt], bb, start=(k == 0), stop=(k == KC - 1), acc=(k > 0))
                fw.op("act", lambda h, bk=bk, tl=tl: h.activation(out=gate[:, tl, :], in_=bk[:, 0:48], func=AF.Sigmoid), reads=[bb], writes=[b_gate])
            for qt in range(4):
                fw.op("dve", lambda h, qt=qt, ob_=ob_: h.tensor_scalar(out=mrow[:, qt, :], in0=cthr[:, 0:CW], scalar1=colt[:, 0, ob_ * 4 + qt:ob_ * 4 + qt + 1], scalar2=None, op0=ALU.is_le),
                      reads=[cm.b_c], writes=[b_mrow])
            for ct in range(nct):
                fw.op("dve", lambda h, ct=ct, qs=qs: h.tensor_scalar(out=mcT[:, ct, :], in0=tq_row[:, qs], scalar1=cthc[:, ct:ct + 1], scalar2=None, op0=ALU.is_ge),
                      reads=[cm.b_c], writes=[b_mcT])
            for wt in range(8):
                kt = c0 // 128 - 4 + wt
                fw.op("dve", lambda h, wt=wt, kt=kt: h.tensor_scalar(out=mW[:, wt, :], in0=band[:, wt, :], scalar1=kval[:, kt:kt + 1], scalar2=None, op0=ALU.mult),
                      reads=[cm.b_c], writes=[b_mW])
            for gq in range(4):
                fw.dma("sp", "kv", lambda h, gq=gq, nkt=nkt: h.dma_start(out=KSs[:, 0:nkt * 128], in_=dr["KS"][gq][:, 0:nkt * 128]), reads=[dr["_b_scr"]], writes=[b_KS])
                fw.dma("sp", "kv", lambda h, gq=gq, nkt=nkt: h.dma_start(out=VSs[:, 0:nkt, :], in_=dr["VS"][gq][:, 0:nkt, :]), reads=[dr["_b_scr"]], writes=[b_VS])
                fw.dma("sp", "kv", lambda h, gq=gq, c0=c0: h.dma_start(out=KWs[:], in_=dr["KW"][gq][:, c0 - 512:c0 + 512]), reads=[dr["_b_scr"]], writes=[b_KW])
                fw.dma("sp", "kv", lambda h, gq=gq, c0=c0: h.dma_start(out=VWs[:, :, :], in_=dr["VW"][gq][:, c0 // 128 - 4:c0 // 128 + 4, :]), reads=[dr["_b_scr"]], writes=[b_VW])
                fw.dma("sp", "kv", lambda h, gq=gq: h.dma_start(out=KCs[:], in_=dr["KC"][gq]), reads=[dr["_b_scr"]], writes=[b_KC])
                fw.dma("sp", "kv", lambda h, gq=gq: h.dma_start(out=VCs[:, :, :], in_=dr["VC"][gq]), reads=[dr["_b_scr"]], writes=[b_VC])
                for hl in range(4):
                    bk, bb = cm.proj_f(dr["w_q"], gq * 4 + hl)
                    cm.norm_rope_f(bk, bb, qg[:, 0:1], tb[:, 0, :], tb[:, 1, :], tbb, prot[:], qT[hl][:], b_qT[hl], (f32r, b16r))
                for qt in range(4):
                    for hl in range(4):
                        bk, bb = cm.bank.next()
                        cm.mm(bk[:, 0:CW], qT[hl][:, qt * 128:(qt + 1) * 128], KCs[:, 0:CW], [b_qT[hl], b_KC], bb)
                        p_, pb_ = pr.next()
                        fw.op("act", lambda h, p_=p_, bk=bk: h.activation(out=p_[:], in_=bk[:, 0:CW], func=AF.Exp, scale=SCALE), reads=[bb], writes=[pb_])
                        c_, cb_ = c8.next()
                        fw.op("dve", lambda h, p_=p_, qt=qt, c_=c_: h.scalar_tensor_tensor(out=p_[:], in0=p_[:], scalar=1.0, in1=mrow[:, qt, :], op0=ALU.mult, op1=ALU.mult, accum_out=c_[:, 0:1]),
                              reads=[pb_, b_mrow], writes=[pb_, cb_])
                        fw.op("dve", lambda h, c_=c_: h.tensor_scalar(out=c_[:, 0:1], in0=c_[:, 0:1], scalar1=1e-30, scalar2=None, op0=ALU.max), reads=[cb_], writes=[cb_])
                        fw.op("dve", lambda h, c_=c_: h.reciprocal(out=c_[:, 0:1], in_=c_[:, 0:1]), reads=[cb_], writes=[cb_])
                        if hl == 0:
                            fw.op("dve", lambda h, p_=p_, c_=c_: h.tensor_scalar(out=p4[:, 4:4 + CW], in0=p_[:], scalar1=c_[:, 0:1], scalar2=None, op0=ALU.mult), reads=[pb_, cb_], writes=[b_p4])
                        else:
                            fw.op("dve", lambda h, p_=p_, c_=c_: h.scalar_tensor_tensor(out=p4[:, 4:4 + CW], in0=p_[:], scalar=c_[:, 0:1], in1=p4[:, 4:4 + CW], op0=ALU.mult, op1=ALU.add), reads=[pb_, cb_, b_p4], writes=[b_p4])
                    s_, sb_ = sc.next()
                    NJ = CW // 4
                    fw.op("dve", lambda h, s_=s_: h.tensor_reduce(out=s_[:, 0, 0:NJ], in_=p4[:, 4:4 + CW].rearrange("p (j r) -> p j r", r=4), axis=mybir.AxisListType.X, op=ALU.add), reads=[b_p4], writes=[sb_])
                    fw.op("dve", lambda h, s_=s_: h.tensor_tensor(out=s_[:, 0, 0:NJ], in0=s_[:, 0, 0:NJ], in1=p4[:, 0:CW].rearrange("p (j r) -> p j r", r=4)[:, :, 3], op=ALU.add), reads=[b_p4, sb_], writes=[sb_])
                    col = colt[:, 1, ob_ * 4 + qt:ob_ * 4 + qt + 1]
                    fw.op("dve", lambda h, s_=s_, col=col: h.scalar_tensor_tensor(out=s_[:, 1, :], in0=rowt[:, 0, :], scalar=col, in1=rowt[:, 1, :], op0=ALU.is_le, op1=ALU.mult), reads=[cm.b_c], writes=[sb_])
                    fw.op("dve", lambda h, s_=s_, col=col: h.scalar_tensor_tensor(out=s_[:, 2, :], in0=rowt[:, 0, :], scalar=col, in1=rowt[:, 2, :], op0=ALU.is_equal, op1=ALU.add), reads=[cm.b_c], writes=[sb_])
                    fw.op("dve", lambda h, s_=s_, col=col: h.tensor_scalar(out=s_[:, 3, :], in0=rowt[:, 0, :], scalar1=1.0, scalar2=col, op0=ALU.add, op1=ALU.is_equal), reads=[cm.b_c], writes=[sb_])
                    fw.op("dve", lambda h, s_=s_: h.tensor_tensor(out=s_[:, 2, :], in0=s_[:, 2, :], in1=s_[:, 3, :], op=ALU.add), reads=[sb_], writes=[sb_])
                    fw.op("dve", lambda h, s_=s_: h.scalar_tensor_tensor(out=s_[:, 0, :], in0=s_[:, 0, :], scalar=1.0, in1=s_[:, 1, :], op0=ALU.add, op1=ALU.mult), reads=[sb_], writes=[sb_])
                    fw.op("dve", lambda h, s_=s_: h.scalar_tensor_tensor(out=s_[:, 0, :], in0=s_[:, 2, :], scalar=BIG, in1=s_[:, 0, :], op0=ALU.mult, op1=ALU.add), reads=[sb_], writes=[sb_])
                    m8, m8b = c8.next()
                    fw.op("dve", lambda h, s_=s_, m8=m8: h.max(out=m8[:, 0:8], in_=s_[:, 0, :]), reads=[sb_], writes=[m8b])
                    fw.op("dve", lambda h, s_=s_, m8=m8: h.match_replace(out=s_[:, 3, :], in_to_replace=m8[:, 0:8], in_values=s_[:, 0, :], imm_value=-1.0), reads=[sb_, m8b], writes=[sb_])
                    fw.op("dve", lambda h, s_=s_, m8=m8: h.max(out=m8[:, 8:16], in_=s_[:, 3, :]), reads=[sb_], writes=[m8b])
                    fw.op("dve", lambda h, s_=s_, m8=m8: h.scalar_tensor_tensor(out=s_[:, 3, :], in0=s_[:, 0, :], scalar=m8[:, 15:16], in1=s_[:, 1, :], op0=ALU.is_ge, op1=ALU.mult), reads=[sb_, m8b], writes=[sb_])
                    bt, btb = cm.bank.next()
                    cm.tr(bt[0:NSB, 0:128], s_[:, 3, :], idf[:, :], [sb_, cm.b_c], btb)
                    fw.op("act", lambda h, bt=bt, qt=qt: h.activation(out=selT[:, qt * 128:(qt + 1) * 128], in_=bt[0:NSB, 0:128], func=AF.Copy), reads=[btb], writes=[b_selT])
                for hl in range(4):
                    hd = gq * 4 + hl
                    for br in range(3):
                        npc = min(128, NCP)
                        if br == 0:
                            tiles = [(KCs[:, ct * 128:ct * 128 + npc], VCs[0:npc, ct, 0:129], mcT[0:npc, ct, :], b_KC, b_VC, [b_mcT], npc) for ct in range(nct)]
                        elif br == 2:
                            tiles = [(KWs[:, wt * 128:(wt + 1) * 128], VWs[:, wt, 0:129], mW[:, wt, :], b_KW, b_VW, [b_mW], 128) for wt in range(8)]
                        else:
                            tiles = [(KSs[:, kt * 128:(kt + 1) * 128], VSs[:, kt, 0:129], None, b_KS, b_VS, [], 128) for kt in range(nkt)]
                        acA, acAb = accb.next(); acB, acBb = accb.next()
                        accs = [(acA, acAb, 0), (acA, acAb, 132), (acB, acBb, 0), (acB, acBb, 132)]
                        def stage_s(ti):
                            kap, vap, msk, kb_, vb_, mb_l, np_ = tiles[ti]
                            if br == 1:
                                bm, bmb = cm.bank.next()
                                cm.mm(bm[:, :], ebig[:, ti * 128:(ti + 1) * 128], selT[:, :], [cm.b_c, b_selT], bmb)
                                m_, mb_ = mS.next()
                                rel = ti - c0 // 128
                                if rel >= 0:
                                    fw.op("dve", lambda h, m_=m_, bm=bm, rel=rel: h.tensor_tensor(out=m_[:], in0=bm[:, :], in1=caus[:, rel, :], op=ALU.mult), reads=[bmb, cm.b_c], writes=[mb_])
                                else:
                                    fw.op("act", lambda h, m_=m_, bm=bm: h.activation(out=m_[:], in_=bm[:, :], func=AF.Copy), reads=[bmb], writes=[mb_])
                                msk, mb_l = m_[:], [mb_]
                            bs, bsb = cm.bank.next()
                            cm.mm(bs[0:np_, :], kap, qT[hl][:], [kb_, b_qT[hl]], bsb)
                            p_, pb_ = pT.next()
                            fw.op("act", lambda h, p_=p_, bs=bs, np_=np_: h.activation(out=p_[0:np_, :], in_=bs[0:np_, :], func=AF.Exp, scale=SCALE), reads=[bsb], writes=[pb_])
                            fw.op("dve", lambda h, p_=p_, msk=msk, np_=np_: h.tensor_tensor(out=p_[0:np_, :], in0=p_[0:np_, :], in1=msk, op=ALU.mult), reads=[pb_] + mb_l, writes=[pb_])
                            return p_, pb_, vap, vb_, np_

                        pend = stage_s(0)
                        for ti in range(len(tiles)):
                            nxt = stage_s(ti + 1) if ti + 1 < len(tiles) else None
                            p_, pb_, vap, vb_, np_ = pend
                            for qt in range(4):
                                a, ab, off = accs[qt]
                                cm.mm(a[:, off:off + 129], p_[0:np_, qt * 128:(qt + 1) * 128], vap, [pb_, vb_], ab,
                                      start=(ti == 0 and qt in (0, 2)), stop=(ti == len(tiles) - 1), acc=(ti > 0 or qt in (1, 3)))
                            pend = nxt
                        for qt in range(4):
                            a, ab, off = accs[qt]
                            c_, cb_ = c8.next()
                            fw.op("dve", lambda h, c_=c_, a=a, off=off: h.tensor_scalar(out=c_[:, 0:1], in0=a[:, off + 128:off + 129], scalar1=1e-30, scalar2=None, op0=ALU.max), reads=[ab], writes=[cb_])
                            fw.op("dve", lambda h, c_=c_: h.reciprocal(out=c_[:, 0:1], in_=c_[:, 0:1]), reads=[cb_], writes=[cb_])
                            gi = hd * 3 + br
                            fw.op("dve", lambda h, c_=c_, qt=qt, gi=gi: h.tensor_tensor(out=c_[:, 0:1], in0=c_[:, 0:1], in1=gate[:, qt, gi:gi + 1], op=ALU.mult), reads=[cb_, b_gate], writes=[cb_])
                            if br == 0:
                                fw.op("dve", lambda h, c_=c_, a=a, off=off, qt=qt: h.tensor_scalar(out=oacc[:, qt, :], in0=a[:, off:off + 128], scalar1=c_[:, 0:1], scalar2=None, op0=ALU.mult), reads=[ab, cb_], writes=[b_oacc])
                            else:
                                fw.op("dve", lambda h, c_=c_, a=a, off=off, qt=qt: h.scalar_tensor_tensor(out=oacc[:, qt, :], in0=a[:, off:off + 128], scalar=c_[:, 0:1], in1=oacc[:, qt, :], op0=ALU.mult, op1=ALU.add), reads=[ab, cb_, b_oacc], writes=[b_oacc])
                    o_, ob2 = oT.next()
                    for qt in range(4):
                        bt, btb = cm.bank.next()
                        cm.tr(bt[:, 0:128], oacc[:, qt, :], idf[:, :], [b_oacc, cm.b_c], btb)
                        fw.op("act", lambda h, o_=o_, bt=bt, qt=qt: h.activation(out=o_[:, qt * 128:(qt + 1) * 128], in_=bt[:, 0:128], func=AF.Copy), reads=[btb], writes=[ob2])
                    fw.dma("sp", "oa", lambda h, o_=o_, hd=hd, qs=qs: h.dma_start(out=dr["oaT"][hd * 128:(hd + 1) * 128, qs], in_=o_[:]), reads=[ob2], writes=[b_out])
        fw.flush(barrier=True)
    return b_out


def nsa_inputs(cfg, inp, c):
    D, S, TO, G = cfg.D, cfg.S, cfg.TO, cfg.G
    b, g = c // G, c % G
    n = (g + 1) * TO
    xp = np.zeros((D, S), np.float32)
    xp[:, S - n:] = inp["x"][b, :n].T
    w_in = inp["w_in"][0]
    kv = lambda i: w_in[:, 2048 + i * 512:2048 + (i + 1) * 512]
    tmaj = lambda w: np.ascontiguousarray(w.reshape(D // 128, 128, w.shape[1]).transpose(1, 0, 2))
    rep = lambda v, n_: np.ascontiguousarray(np.broadcast_to(v[None, :], (n_, v.shape[0])))
    z = np.zeros(128, np.float32)
    m = {
        "xTpad": xp, "gmix_col": col_vec(inp["g_mix"][0]),
        "w_kc": blk_w(kv(0)), "w_vc": blk_w(kv(1)), "w_ks": blk_w(kv(2)), "w_kw": blk_w(kv(4)),
        "w_v_t": tmaj(np.concatenate([kv(3), kv(5)], axis=1)),
        "w_q": blk_w(w_in[:, 0:2048]), "w_gate_t": tmaj(w_in[:, 5120:5168]),
        "kgain_cols": np.ascontiguousarray(np.stack([inp["nsa_ks_gain"][0], inp["nsa_kw_gain"][0], z], axis=1)),
        "kcg_bc": rep(inp["nsa_kc_gain"][0], 32), "qgain_col": np.ascontiguousarray(inp["nsa_q_gain"][0][:, None]),
        "w1k": np.ascontiguousarray(inp["cmp_wk1"][0].transpose(1, 0, 2)), "w1v": np.ascontiguousarray(inp["cmp_wv1"][0].transpose(1, 0, 2)),
        "w2k": np.ascontiguousarray(inp["cmp_wk2"][0].reshape(4, 128, 128).transpose(1, 0, 2)),
        "w2v": np.ascontiguousarray(inp["cmp_wv2"][0].reshape(4, 128, 128).transpose(1, 0, 2)),
        "pek": np.ascontiguousarray(inp["cmp_pe_k"][0].T), "pev": np.ascontiguousarray(inp["cmp_pe_v"][0].T),
    }
    m.update(nsa_tables(cfg, g))
    m.update(nsa_consts(cfg))
    return m


TDT = BF16


def gdn_consts():
    i = np.arange(64)
    ident = np.eye(64, dtype=np.float32)
    MU = (i[:, None] <= i[None, :]).astype(np.float32)
    MsU = (i[:, None] < i[None, :]).astype(np.float32)
    MsL = (i[:, None] > i[None, :]).astype(np.float32)
    ones = np.ones((64, 64), np.float32)
    return np.ascontiguousarray(np.stack([ident, MU, MsU, MsL, ones], axis=1))


def emit_gdn(nc, fw, cfg, dr, heads=None):
    D, KC, S, TO = cfg.D, cfg.KC, cfg.S, cfg.TO
    TB = 512
    NBLK, NOWN = S // TB, TO // TB
    NH = 16
    with ExitStack() as es:
        cm = Common(nc, fw, cfg, dr, es, "g_", nw=2)
        sb = lambda n, s, d: es.enter_context(nc.sbuf_tensor(n, list(s), d))
        hT, b_h, b_g, epsc = cm.hT, cm.b_h, cm.b_g, cm.epsc
        cst = sb("g_cst", [64, 5, 64], F32); b_cst = Buf()
        ones32 = sb("g_ones32", [64, 128], F32)
        mk2 = sb("g_mk2", [64, 128], F32)
        idb = sb("g_idb", [128, 128], BF16)
        idt = sb("g_idt", [64, 64], TDT)
        cw = sb("g_cw", [128, NH, 3, 4], F32)
        bc16 = sb("g_bc16", [64, 3, NH], F32)
        ogain = sb("g_og", [64, 128], F32)
        wbd = sb("g_wbd", [128, KC, 32], BF16); b_wbd = Buf()
        Sf = sb("g_Sf", [128, NH, 128], F32); Sb = sb("g_Sb", [128, NH, 128], BF16)
        b_Sf = [Buf() for _ in range(NH)]; b_Sb = [Buf() for _ in range(NH)]
        halo = sb("g_halo", [128, NH, 3, 3], F32); b_halo = [[Buf() for _ in range(3)] for _ in range(NH)]
        NCH = 8
        zsr = Ring(es, nc, "g_zs", [128, TB], BF16, NCH)
        cin = Ring(es, nc, "g_cin", [128, 3 + TB], F32, 3)
        cac = Ring(es, nc, "g_cac", [128, TB], F32, 2)
        csl = {i: Ring(es, nc, "g_cs%d" % i, [128, TB], BF16, NCH) for i in range(3)}
        gat = Ring(es, nc, "g_gat", [64, 8, NH], F32, 8)
        gst = Ring(es, nc, "g_gst", [64, 3, NH], F32, 8)
        etot = Ring(es, nc, "g_etot", [128, NH], F32, 8)
        RES = [{"t64": Ring(es, nc, "g_t64_%d" % i_, [64, 128], F32, 5), "tb64": Ring(es, nc, "g_tb64_%d" % i_, [64, 128], BF16, 12),
                "tt": Ring(es, nc, "g_tt_%d" % i_, [64, 64], TDT, 8), "t128": Ring(es, nc, "g_t128_%d" % i_, [128, 64], BF16, 6),
                "f128": Ring(es, nc, "g_f128_%d" % i_, [128, 64], F32, 2), "col": Ring(es, nc, "g_col_%d" % i_, [64, 4], F32, 8)} for i_ in range(NCH)]
        obr = Ring(es, nc, "g_ob", [128, TB], BF16, NCH)
        bank, bkb = cm.bank, cm.bkb
        b_out = Buf()

        fw.dma("sp", "cst", lambda h: h.dma_start(out=cst[:], in_=dr["gcst"]), writes=[b_cst])
        fw.dma("sp", "cst", lambda h: h.dma_start(out=cw[:], in_=dr["cw"]), writes=[b_cst])
        fw.dma("sp", "cst", lambda h: h.dma_start(out=bc16[:, 0:2, :], in_=dr["adt"]), writes=[b_cst])
        fw.dma("sp", "cst", lambda h: h.dma_start(out=ogain[:], in_=dr["ogain"]), writes=[b_cst])
        fw.dma("pool", "w", lambda h: h.dma_start(out=wbd[:], in_=dr["w_bd"]), writes=[b_wbd])
        fw.dma("pool", "w", lambda h: h.dma_start(out=idb[:], in_=dr["ident128"]), writes=[b_cst])
        fw.op("dve", lambda h: h.memset(ones32[:], 1.0), writes=[b_cst])
        fw.op("dve", lambda h: h.memset(Sf[:], 0.0), writes=b_Sf)
        fw.op("dve", lambda h: h.memset(Sb[:], 0.0), writes=b_Sb)
        fw.op("dve", lambda h: h.memset(halo[:], 0.0), writes=[b for r in b_halo for b in r])
        fw.op("dve", lambda h: h.tensor_copy(out=mk2[:, 0:64], in_=cst[:, 1, :]), reads=[b_cst], writes=[b_cst])
        fw.op("dve", lambda h: h.tensor_copy(out=mk2[:, 64:128], in_=cst[:, 2, :]), reads=[b_cst], writes=[b_cst])
        fw.op("dve", lambda h: h.tensor_copy(out=idt[:], in_=cst[:, 0, :]), reads=[b_cst], writes=[b_cst])
        fw.op("act", lambda h: h.activation(out=bc16[:, 2, :], in_=bc16[:, 0, :], func=AF.Exp), reads=[b_cst], writes=[b_cst])
        fw.op("dve", lambda h: h.tensor_scalar(out=bc16[:, 2, :], in0=bc16[:, 2, :], scalar1=-1.0, scalar2=None, op0=ALU.mult), reads=[b_cst], writes=[b_cst])

        proj_f = cm.proj_f

        def mm(out, lhsT, rhs, rd, wr, start=True, stop=True, acc=False):
            fw.op("pe", lambda h: h.matmul(out, lhsT=lhsT, rhs=rhs, start=start, stop=stop), reads=rd, writes=[wr], pe_acc=acc)

        def tr(out, in_, ident, rd, wr):
            fw.op("pe", lambda h: h.transpose(out, in_, ident), reads=rd + [b_cst], writes=[wr])

        def rsq(ss, ssb, n, scale):
            fw.op("act", lambda h: h.activation(out=ss, in_=ss, func=AF.Sqrt, bias=epsc[0:n, :], scale=scale), reads=[ssb, b_g], writes=[ssb])
            fw.op("dve", lambda h: h.reciprocal(out=ss, in_=ss), reads=[ssb], writes=[ssb])

        for blk in range(NBLK):
            own = blk >= NBLK - NOWN
            cm.make_h(blk * TB)
            G = []
            for c in range(8):
                cs = slice(c * 64, c * 64 + 64)
                bk, bb = bank.next()
                for k in range(KC):
                    mm(bk[0:64, 0:32], hT[:, k, cs], wbd[:, k, :], [b_h[k], b_wbd], bb, start=(k == 0), stop=(k == KC - 1), acc=(k > 0))
                ga, gab = gat.next()
                fw.op("act", lambda h, ga=ga, bk=bk: h.activation(out=ga[:, 0, :], in_=bk[0:64, 0:16], func=AF.Exp, scale=-1.0), reads=[bb], writes=[gab])
                fw.op("act", lambda h, ga=ga: h.activation(out=ga[:, 0, :], in_=ga[:, 0, :], func=AF.Ln, bias=1.0), reads=[gab], writes=[gab])
                fw.op("act", lambda h, ga=ga: h.activation(out=ga[:, 2, :], in_=ga[:, 0, :], func=AF.Exp, scale=-1.0), reads=[gab], writes=[gab])
                fw.op("dve", lambda h, ga=ga: h.tensor_scalar(out=ga[:, 1, :], in0=ga[:, 0, :], scalar1=-1.0, scalar2=None, op0=ALU.mult), reads=[gab], writes=[gab])
                fw.op("dve", lambda h, ga=ga, bk=bk: h.tensor_tensor(out=ga[:, 3, :], in0=bk[0:64, 16:32], in1=bc16[:, 1, :], op=ALU.add), reads=[bb, b_cst], writes=[gab])
                fw.op("act", lambda h, ga=ga: h.activation(out=ga[:, 3, :], in_=ga[:, 3, :], func=AF.Exp), reads=[gab], writes=[gab])
                fw.op("act", lambda h, ga=ga: h.activation(out=ga[:, 3, :], in_=ga[:, 3, :], func=AF.Ln, bias=1.0), reads=[gab], writes=[gab])
                fw.op("dve", lambda h, ga=ga: h.tensor_tensor(out=ga[:, 3, :], in0=ga[:, 3, :], in1=bc16[:, 2, :], op=ALU.mult), reads=[gab, b_cst], writes=[gab])
                b2, b2b = bank.next()
                mm(b2[0:64, 0:16], cst[:, 1, :], ga[:, 3, :], [b_cst, gab], b2b)
                mm(b2[0:64, 16:32], cst[:, 3, :], ga[:, 3, :], [b_cst, gab], b2b, acc=True)
                mm(b2[:, 32:48], ones32[:, :], ga[:, 3, :], [b_cst, gab], b2b, acc=True)
                gs, gsb = gst.next()
                et, etb = etot.next()
                fw.op("act", lambda h, gs=gs, b2=b2: h.activation(out=gs[:, 0, :], in_=b2[0:64, 0:16], func=AF.Copy), reads=[b2b], writes=[gsb])
                fw.op("act", lambda h, gs=gs, b2=b2: h.activation(out=gs[:, 1, :], in_=b2[0:64, 0:16], func=AF.Exp), reads=[b2b], writes=[gsb])
                fw.op("act", lambda h, gs=gs, b2=b2: h.activation(out=gs[:, 2, :], in_=b2[0:64, 16:32], func=AF.Exp), reads=[b2b], writes=[gsb])
                fw.op("act", lambda h, et=et, b2=b2: h.activation(out=et[:], in_=b2[:, 32:48], func=AF.Exp), reads=[b2b], writes=[etb])
                fw.op("dve", lambda h, ga=ga, gs=gs: h.tensor_tensor(out=ga[:, 4, :], in0=ga[:, 2, :], in1=gs[:, 1, :], op=ALU.mult), reads=[gab, gsb], writes=[gab])
                G.append((ga, gab, gs, gsb, et, etb))
            for hg in range(16 // NCH):
                if heads is not None and not any(hg * NCH + q in heads for q in range(NCH)):
                    continue
                CS = {}
                OB = {}
                for hl in range(NCH):
                    hd = hg * NCH + hl
                    if heads is not None and hd not in heads:
                        continue
                    cs_t = CS.setdefault(hd, {})
                    for i, wn in ((1, "w_gk"), (2, "w_gv"), (0, "w_gq")):
                        if i == 0 and not (own or blk == NBLK - NOWN - 1):
                            continue
                        bk, bb = proj_f(dr[wn], hd)
                        ci, cib = cin.next()
                        fw.op("act", lambda h, ci=ci, bk=bk: h.activation(out=ci[:, 3:3 + TB], in_=bk[:, :], func=AF.Copy), reads=[bb], writes=[cib])
                        fw.op("dve", lambda h, ci=ci, i=i, hd=hd: h.tensor_copy(out=ci[:, 0:3], in_=halo[:, hd, i, :]), reads=[b_halo[hd][i]], writes=[cib])
                        fw.op("dve", lambda h, ci=ci, i=i, hd=hd: h.tensor_copy(out=halo[:, hd, i, :], in_=ci[:, TB:TB + 3]), reads=[cib], writes=[b_halo[hd][i]])
                        if i == 0 and not own:
                            continue
                        ac, acb = cac.next()
                        fw.op("dve", lambda h, ci=ci, ac=ac, i=i, hd=hd: h.tensor_scalar(out=ac[:], in0=ci[:, 0:TB], scalar1=cw[:, hd, i, 0:1], scalar2=None, op0=ALU.mult), reads=[cib, b_cst], writes=[acb])
                        for j in (1, 2, 3):
                            fw.op("dve", lambda h, ci=ci, ac=ac, i=i, hd=hd, j=j: h.scalar_tensor_tensor(out=ac[:], in0=ci[:, j:j + TB], scalar=cw[:, hd, i, j:j + 1], in1=ac[:], op0=ALU.mult, op1=ALU.add),
                                  reads=[cib, b_cst, acb], writes=[acb])
                        st, stb = csl[i].next()
                        fw.op("act", lambda h, st=st, ac=ac: h.activation(out=st[:], in_=ac[:], func=AF.Silu), reads=[acb], writes=[stb])
                        cs_t[i] = (st, stb)
                    OB[hd] = (obr.next() if own else (None, None))
                    if own:
                        bk, bb = proj_f(dr["w_gz"], hd)
                        zs, zsb = zsr.next()
                        fw.op("act", lambda h, zs=zs, bk=bk: h.activation(out=zs[:], in_=bk[:, :], func=AF.Silu), reads=[bb], writes=[zsb])
                        CS[hd]["z"] = (zs, zsb)

                def chain(hd, hl, c, R):
                    cs_t = CS[hd]
                    ob, obb = OB[hd]
                    cs = slice(c * 64, c * 64 + 64)
                    ga, gab, gs, gsb, et, etb = G[c][:6]
                    kst, kstb = cs_t[1]
                    pb, pbb = bkb.next()
                    tr(pb[0:64, 0:128], kst[:, cs], idb[:, :], [kstb], pbb)
                    cl, clb = R["col"].next()
                    junk, jb = R["tb64"].next()
                    fw.op("act", lambda h, junk=junk, pb=pb, cl=cl: h.activation(out=junk[:], in_=pb[0:64, 0:128], func=AF.Square, accum_out=cl[:, 0:1]), reads=[pbb], writes=[jb, clb])
                    rsq(cl[:, 0:1], clb, 64, 1.0)
                    kn, knb = R["tb64"].next()
                    fw.op("dve", lambda h, kn=kn, pb=pb, cl=cl: h.tensor_scalar(out=kn[:], in0=pb[0:64, 0:128], scalar1=cl[:, 0:1], scalar2=None, op0=ALU.mult), reads=[pbb, clb], writes=[knb])
                    yield
                    pb2, pb2b = bkb.next()
                    tr(pb2[:, 0:64], kn[:], idb[0:64, 0:64], [knb], pb2b)
                    kT, kTb = R["t128"].next()
                    fw.op("act", lambda h, kT=kT, pb2=pb2: h.activation(out=kT[:], in_=pb2[:, 0:64], func=AF.Copy), reads=[pb2b], writes=[kTb])
                    yield
                    vst, vstb = cs_t[2]
                    pb3, pb3b = bkb.next()
                    tr(pb3[0:64, 0:128], vst[:, cs], idb[:, :], [vstb], pb3b)
                    vb, vbb = R["tb64"].next()
                    fw.op("dve", lambda h, vb=vb, pb3=pb3, ga=ga, hd=hd: h.tensor_scalar(out=vb[:], in0=pb3[0:64, 0:128], scalar1=ga[:, 2, hd:hd + 1], scalar2=None, op0=ALU.mult), reads=[pb3b, gab], writes=[vbb])
                    yield
                    r12, r12b = R["t64"].next()
                    fw.op("dve", lambda h, r12=r12, ga=ga, hd=hd: h.tensor_scalar(out=r12[:, 0:64], in0=cst[:, 1, :], scalar1=ga[:, 3, hd:hd + 1], scalar2=None, op0=ALU.mult), reads=[b_cst, gab], writes=[r12b])
                    fw.op("dve", lambda h, r12=r12, ga=ga, hd=hd: h.scalar_tensor_tensor(out=r12[:, 64:128], in0=cst[:, 0, :], scalar=ga[:, 1, hd:hd + 1], in1=r12[:, 0:64], op0=ALU.mult, op1=ALU.add), reads=[b_cst, gab, r12b], writes=[r12b])
                    pg, pgb = bank.next()
                    mm(pg[:, 0:128], ones32[:, :], r12[:, 0:128], [b_cst, r12b], pgb)
                    if own:
                        eg, egb = R["f128"].next()
                        fw.op("act", lambda h, eg=eg, pg=pg: h.activation(out=eg[:], in_=pg[:, 0:64], func=AF.Exp), reads=[pgb], writes=[egb])
                    dm, dmb = R["t64"].next()
                    fw.op("dve", lambda h, dm=dm, pg=pg, gs=gs, hd=hd: h.tensor_scalar(out=dm[:], in0=pg[0:64, 0:128], scalar1=gs[:, 0, hd:hd + 1], scalar2=0.0, op0=ALU.subtract, op1=ALU.min), reads=[pgb, gsb], writes=[dmb])
                    yield
                    fw.op("act", lambda h, dm=dm: h.activation(out=dm[:], in_=dm[:], func=AF.Exp), reads=[dmb], writes=[dmb])
                    fw.op("dve", lambda h, dm=dm: h.tensor_tensor(out=dm[:], in0=dm[:], in1=mk2[:], op=ALU.mult), reads=[dmb, b_cst], writes=[dmb])
                    pk, pkb = bank.next()
                    mm(pk[0:64, 0:64], kT[:, :], kT[:, :], [kTb], pkb)
                    X, Xb = R["tt"].next()
                    fw.op("dve", lambda h, X=X, pk=pk, dm=dm: h.tensor_tensor(out=X[:], in0=pk[0:64, 0:64], in1=dm[:, 64:128], op=ALU.mult), reads=[pkb, dmb], writes=[Xb])
                    yield
                    pa, pab = bkb.next()
                    tr(pa[0:64, 0:64], X[:], idb[0:64, 0:64], [Xb], pab)
                    Y, Yb = R["tt"].next()
                    fw.op("act", lambda h, Y=Y, pa=pa: h.activation(out=Y[:], in_=pa[0:64, 0:64], func=AF.Copy), reads=[pab], writes=[Yb])
                    P, Pb = R["tt"].next()
                    fw.op("dve", lambda h, P=P, X=X: h.tensor_tensor(out=P[:], in0=idt[:], in1=X[:], op=ALU.subtract), reads=[Xb, b_cst], writes=[Pb])
                    yield
                    for lvl in range(5):
                        py, pyb = bank.next()
                        mm(py[0:64, 0:64], X[:], Y[:], [Xb, Yb], pyb)
                        if lvl < 4:
                            mm(py[0:64, 64:128], Y[:], X[:], [Xb, Yb], pyb, acc=True)
                        Y2, Y2b = R["tt"].next()
                        fw.op("act", lambda h, Y2=Y2, py=py: h.activation(out=Y2[:], in_=py[0:64, 0:64], func=AF.Copy), reads=[pyb], writes=[Y2b])
                        if lvl < 4:
                            X2, X2b = R["tt"].next()
                            fw.op("act", lambda h, X2=X2, py=py: h.activation(out=X2[:], in_=py[0:64, 64:128], func=AF.Copy), reads=[pyb], writes=[X2b])
                            yield
                        pp, ppb = bank.next()
                        mm(pp[0:64, 0:64], Y2[:], P[:], [Y2b, Pb], ppb)
                        P2, P2b = R["tt"].next()
                        fw.op("dve", lambda h, P2=P2, P=P, pp=pp: h.tensor_tensor(out=P2[:], in0=P[:], in1=pp[0:64, 0:64], op=ALU.add), reads=[Pb, ppb], writes=[P2b])
                        yield
                        P, Pb = P2, P2b
                        Y, Yb = Y2, Y2b
                        if lvl < 4:
                            X, Xb = X2, X2b
                    if TDT != BF16:
                        Pm, Pmb = R["tb64"].next()
                        fw.op("act", lambda h, Pm=Pm, P=P: h.activation(out=Pm[:, 0:64], in_=P[:], func=AF.Copy), reads=[Pb], writes=[Pmb])
                        Pt = Pm[:, 0:64]; Ptb = Pmb
                    else:
                        Pt = P[:]; Ptb = Pb
                    pu, pub = bank.next()
                    mm(pu[0:64, 0:128], Pt, vb[:], [Ptb, vbb], pub)
                    u, ub = R["t64"].next()
                    fw.op("act", lambda h, u=u, pu=pu: h.activation(out=u[:], in_=pu[0:64, 0:128], func=AF.Copy), reads=[pub], writes=[ub])
                    yield
                    kbg, kbgb = R["tb64"].next()
                    fw.op("dve", lambda h, kbg=kbg, kn=kn, ga=ga, hd=hd: h.tensor_scalar(out=kbg[:], in0=kn[:], scalar1=ga[:, 4, hd:hd + 1], scalar2=None, op0=ALU.mult), reads=[knb, gab], writes=[kbgb])
                    pw, pwb = bank.next()
                    mm(pw[:, 0:64], kbg[:], Pt, [kbgb, Ptb], pwb)
                    wT, wTb = R["t128"].next()
                    fw.op("act", lambda h, wT=wT, pw=pw: h.activation(out=wT[:], in_=pw[:, 0:64], func=AF.Copy), reads=[pwb], writes=[wTb])
                    yield
                    kg, kgb = R["tb64"].next()
                    fw.op("dve", lambda h, kg=kg, kn=kn, gs=gs, hd=hd: h.tensor_scalar(out=kg[:], in0=kn[:], scalar1=gs[:, 2, hd:hd + 1], scalar2=None, op0=ALU.mult), reads=[knb, gsb], writes=[kgb])
                    pws, pwsb = bank.next()
                    mm(pws[0:64, 0:128], wT[:], Sb[:, hd, :], [wTb, b_Sb[hd]], pwsb)
                    vn, vnb = R["tb64"].next()
                    fw.op("dve", lambda h, vn=vn, u=u, pws=pws: h.tensor_tensor(out=vn[:], in0=u[:], in1=pws[0:64, 0:128], op=ALU.subtract), reads=[ub, pwsb], writes=[vnb])
                    yield
                    if own:
                        qst, qstb = cs_t[0]
                        pq, pqb = bkb.next()
                        tr(pq[0:64, 0:128], qst[:, cs], idb[:, :], [qstb], pqb)
                        cq, cqb = R["col"].next()
                        junk2, j2b = R["tb64"].next()
                        fw.op("act", lambda h, junk2=junk2, pq=pq, cq=cq: h.activation(out=junk2[:], in_=pq[0:64, 0:128], func=AF.Square, accum_out=cq[:, 0:1]), reads=[pqb], writes=[j2b, cqb])
                        rsq(cq[:, 0:1], cqb, 64, 1.0)
                        qn, qnb = R["tb64"].next()
                        fw.op("dve", lambda h, qn=qn, pq=pq, cq=cq: h.tensor_scalar(out=qn[:], in0=pq[0:64, 0:128], scalar1=cq[:, 0:1], scalar2=128.0 ** -0.5, op0=ALU.mult, op1=ALU.mult), reads=[pqb, cqb], writes=[qnb])
                        yield
                        pq2, pq2b = bkb.next()
                        tr(pq2[:, 0:64], qn[:], idb[0:64, 0:64], [qnb], pq2b)
                        qT, qTb = R["t128"].next()
                        fw.op("act", lambda h, qT=qT, pq2=pq2: h.activation(out=qT[:], in_=pq2[:, 0:64], func=AF.Copy), reads=[pq2b], writes=[qTb])
                        qg, qgb = R["t128"].next()
                        fw.op("dve", lambda h, qg=qg, qT=qT, eg=eg: h.tensor_tensor(out=qg[:], in0=qT[:], in1=eg[:], op=ALU.mult), reads=[qTb, egb], writes=[qgb])
                        yield
                        pat, patb = bank.next()
                        mm(pat[0:64, 0:64], kT[:, :], qT[:, :], [kTb, qTb], patb)
                        at, atb = R["tb64"].next()
                        fw.op("dve", lambda h, at=at, pat=pat, dm=dm: h.tensor_tensor(out=at[:, 0:64], in0=pat[0:64, 0:64], in1=dm[:, 0:64], op=ALU.mult), reads=[patb, dmb], writes=[atb])
                        yield
                        po, pob = bank.next()
                        mm(po[0:64, 0:128], qg[:], Sb[:, hd, :], [qgb, b_Sb[hd]], pob, start=True, stop=False)
                        mm(po[0:64, 0:128], at[:, 0:64], vn[:], [atb, vnb], pob, start=False, stop=True, acc=True)
                        co, cob = R["col"].next()
                        junk3, j3b = R["tb64"].next()
                        fw.op("act", lambda h, junk3=junk3, po=po, co=co: h.activation(out=junk3[:], in_=po[0:64, 0:128], func=AF.Square, accum_out=co[:, 0:1]), reads=[pob], writes=[j3b, cob])
                        rsq(co[:, 0:1], cob, 64, 1.0 / 128)
                        on, onb = R["t64"].next()
                        fw.op("dve", lambda h, on=on, po=po, co=co: h.scalar_tensor_tensor(out=on[:], in0=po[0:64, 0:128], scalar=co[:, 0:1], in1=ogain[:], op0=ALU.mult, op1=ALU.mult), reads=[pob, cob, b_cst], writes=[onb])
                        yield
                        zs, zsb = cs_t["z"]
                        pz, pzb = bkb.next()
                        tr(pz[0:64, 0:128], zs[:, cs], idb[:, :], [zsb], pzb)
                        obt, obtb = R["tb64"].next()
                        fw.op("dve", lambda h, obt=obt, on=on, pz=pz: h.tensor_tensor(out=obt[:], in0=on[:], in1=pz[0:64, 0:128], op=ALU.mult), reads=[onb, pzb], writes=[obtb])
                        pot, potb = bkb.next()
                        tr(pot[:, 0:64], obt[:], idb[0:64, 0:64], [obtb], potb)
                        fw.op("act", lambda h, ob=ob, pot=pot, cs=cs: h.activation(out=ob[:, cs], in_=pot[:, 0:64], func=AF.Copy), reads=[potb], writes=[obb])
                        yield
                    ps, psb = bank.next()
                    mm(ps[:, 0:128], kg[:], vn[:], [kgb, vnb], psb)
                    fw.op("dve", lambda h, hd=hd, et=et, ps=ps: h.scalar_tensor_tensor(out=Sf[:, hd, :], in0=Sf[:, hd, :], scalar=et[:, hd:hd + 1], in1=ps[:, 0:128], op0=ALU.mult, op1=ALU.add),
                          reads=[b_Sf[hd], etb, psb], writes=[b_Sf[hd]])
                    fw.op("act", lambda h, hd=hd: h.activation(out=Sb[:, hd, :], in_=Sf[:, hd, :], func=AF.Copy), reads=[b_Sf[hd]], writes=[b_Sb[hd]])

                hds = [hg * NCH + hl for hl in range(NCH) if heads is None or hg * NCH + hl in heads]
                for c in range(8):
                    cs = slice(c * 64, c * 64 + 64)
                    gens = [chain(hd, hd % 4, c, RES[i_]) for i_, hd in enumerate(hds)]
                    while gens:
                        nxt = []
                        for g_ in gens:
                            try:
                                next(g_)
                                nxt.append(g_)
                            except StopIteration:
                                pass
                        gens = nxt
                if own:
                    oblk = blk - (NBLK - NOWN)
                    for hd in hds:
                        ob, obb = OB[hd]
                        fw.dma("sp", "ob", lambda h, ob=ob, hd=hd, oblk=oblk: h.dma_start(out=dr["obT"][hd * 128:(hd + 1) * 128, oblk * TB:(oblk + 1) * TB], in_=ob[:]), reads=[obb], writes=[b_out])
        fw.flush(barrier=True)
    return b_out


def gdn_inputs(cfg, inp, c):
    D, S, TO, G = cfg.D, cfg.S, cfg.TO, cfg.G
    b, g = c // G, c % G
    n = (g + 1) * TO
    xp = np.zeros((D, S), np.float32)
    xp[:, S - n:] = inp["x"][b, :n].T
    w_in = inp["w_in"][0]
    o = 5168
    cwt = inp["gdn_conv_w"][0].reshape(4, 3, 16, 128)
    rep = lambda v, n_: np.ascontiguousarray(np.broadcast_to(v[None, :], (n_, v.shape[0])))
    return {
        "xTpad": xp, "gmix_col": col_vec(inp["g_mix"][0]), "gcst": gdn_consts(),
        "ident128": np.eye(128, dtype=np.float32),
        "cw": np.ascontiguousarray(cwt.transpose(3, 2, 1, 0)),
        "adt": np.ascontiguousarray(np.stack([rep(inp["gdn_a_log"][0], 64), rep(inp["gdn_dt_bias"][0], 64)], axis=1)),
        "ogain": rep(inp["gdn_o_gain"][0], 64),
        "w_gq": blk_w(w_in[:, o:o + 2048]), "w_gk": blk_w(w_in[:, o + 2048:o + 4096]), "w_gv": blk_w(w_in[:, o + 4096:o + 6144]),
        "w_gz": blk_w(w_in[:, 11312:11312 + 2048]),
        "w_bd": np.ascontiguousarray(w_in[:, 13360:13392].reshape(D // 128, 128, 32).transpose(1, 0, 2)),
    }


def emit_tail(nc, fw, cfg, dr):
    D, KC, TT, TO, HID, MW = cfg.D, cfg.KC, cfg.TT, cfg.TO, cfg.HID, cfg.MW
    MC, PC, HC = MW // 128, cfg.PLE // 128, HID // 128
    HH = min(HC, 32)
    with ExitStack() as es:
        sb = lambda n, s, d: es.enter_context(nc.sbuf_tensor(n, list(s), d))
        xT = sb("t_x", [128, KC, TT], F32); b_x = [Buf("x%d" % k) for k in range(KC)]
        hT = sb("t_h", [128, KC, TT], BF16); b_h = [Buf("h%d" % k) for k in range(KC)]
        big = sb("t_big", [128, max(2 * MC + KC, HH), TT], BF16)
        b_oa, b_ob = Buf("oa"), Buf("ob")
        b_mx = [Buf("mx%d" % k) for k in range(KC)]
        b_hd = [Buf("hd%d" % k) for k in range(HH)]
        pTb = sb("t_p", [128, PC, TT], BF16); b_p = Buf("p")
        gcol = sb("t_g", [128, 3, KC], F32); b_g = Buf("g")
        ones = sb("t_ones", [128, 128], BF16); b_ones = Buf("ones")
        epsc = sb("t_eps", [128, 1], F32)
        rstd = sb("t_rstd", [128, TT], F32); b_rstd = Buf("rstd")
        wring = Ring(es, nc, "t_w", [128, max(KC, HH), 128], BF16, 3)
        sq = Ring(es, nc, "t_sq", [128, TT], BF16, 2)
        tmp = Ring(es, nc, "t_tmp", [128, TT], F32, 4)
        yt = Ring(es, nc, "t_y", [128, TT], F32, 2)
        bank = Ring(es, nc, "t_bk", [128, 512], F32, 8, psum=True)
        b_y = Buf("y")

        fw.dma("sp", "cst", lambda h: h.dma_start(out=gcol[:], in_=dr["gcols"]), writes=[b_g])
        fw.op("dve", lambda h: h.memset(ones[:], 1.0), writes=[b_ones])
        fw.op("dve", lambda h: h.memset(epsc[:], EPS), writes=[b_g])

        def load_w(wd, cb, nk, k0=0):
            wt, wb = wring.next()
            fw.dma("pool", "w", lambda h: h.dma_start(out=wt[:, 0:nk, :], in_=wd[cb, :, k0:k0 + nk, :]), writes=[wb])
            return wt, wb

        def acc(wd, cb, nk, rhs, rbufs, k0=0):
            wt, wb = load_w(wd, cb, nk, k0)
            bk, bb = bank.next()
            for k in range(nk):
                fw.op("pe", lambda h, k=k: h.matmul(bk[:, 0:TT], lhsT=wt[:, k, :], rhs=rhs(k), start=(k == 0), stop=(k == nk - 1)),
                      reads=[wb, rbufs[k]], writes=[bb], pe_acc=(k > 0))
            return bk, bb

        def norm_to_h(gi):
            bk, bb = bank.next()
            for k in range(KC):
                st, sbuf_ = sq.next()
                fw.op("act", lambda h, k=k, st=st: h.activation(out=st[:], in_=xT[:, k, :], func=AF.Square), reads=[b_x[k]], writes=[sbuf_])
                fw.op("pe", lambda h, k=k, st=st: h.matmul(bk[:, 0:TT], lhsT=ones[:], rhs=st[:], start=(k == 0), stop=(k == KC - 1)),
                      reads=[b_ones, sbuf_], writes=[bb], pe_acc=(k > 0))
            fw.op("act", lambda h: h.activation(out=rstd[:], in_=bk[:, 0:TT], func=AF.Sqrt, bias=epsc[:], scale=1.0 / D), reads=[bb, b_g], writes=[b_rstd])
            fw.op("dve", lambda h: h.reciprocal(out=rstd[:], in_=rstd[:]), reads=[b_rstd], writes=[b_rstd])
            for k in range(KC):
                fw.op("dve", lambda h, k=k: h.scalar_tensor_tensor(out=hT[:, k, :], in0=xT[:, k, :], scalar=gcol[:, gi, k:k + 1], in1=rstd[:], op0=ALU.mult, op1=ALU.mult),
                      reads=[b_x[k], b_g, b_rstd], writes=[b_h[k]])

        oaT, obT, mxT = big[:, 0:MC, :], big[:, MC:2 * MC, :], big[:, 2 * MC:2 * MC + KC, :]
        for blk in range(TO // TT):
            ts = slice(blk * TT, (blk + 1) * TT)
            for k in range(KC):
                fw.dma("sp", "x", lambda h, k=k, blk=blk: h.dma_start(out=xT[:, k, :], in_=dr["xTpad"][k * 128:(k + 1) * 128, cfg.S - cfg.TO + blk * TT:cfg.S - cfg.TO + (blk + 1) * TT]), writes=[b_x[k]])
            fw.dma("pool", "mix", lambda h, ts=ts: h.dma_start(out=oaT, in_=dr["oaT"].rearrange("(c p) t -> p c t", p=128)[:, :, ts]), writes=[b_oa])
            fw.dma("pool", "mix", lambda h, ts=ts: h.dma_start(out=obT, in_=dr["obT"].rearrange("(c p) t -> p c t", p=128)[:, :, ts]), writes=[b_ob])
            fw.dma("pool", "mix", lambda h, ts=ts: h.dma_start(out=pTb[:], in_=dr["pT"].rearrange("(c p) t -> p c t", p=128)[:, :, ts]), writes=[b_p])
            norm_to_h(0)
            hr = lambda k: hT[:, k, :]
            for cb in range(KC):
                ga, gab = acc(dr["w_ga"], cb, KC, hr, b_h)
                ua, uab = acc(dr["w_upa"], cb, MC, lambda k: big[:, k, :], [b_oa] * MC)
                gb, gbb = acc(dr["w_gb"], cb, KC, hr, b_h)
                ub, ubb = acc(dr["w_upb"], cb, MC, lambda k: big[:, MC + k, :], [b_ob] * MC)
                s1, s1b = tmp.next(); s2, s2b = tmp.next()
                fw.op("act", lambda h, s1=s1, ga=ga: h.activation(out=s1[:], in_=ga[:, 0:TT], func=AF.Sigmoid), reads=[gab], writes=[s1b])
                fw.op("act", lambda h, s2=s2, gb=gb: h.activation(out=s2[:], in_=gb[:, 0:TT], func=AF.Sigmoid), reads=[gbb], writes=[s2b])
                fw.op("dve", lambda h, s1=s1, ua=ua: h.tensor_tensor(out=s1[:], in0=s1[:], in1=ua[:, 0:TT], op=ALU.mult), reads=[s1b, uab], writes=[s1b])
                fw.op("dve", lambda h, s2=s2, ub=ub: h.tensor_tensor(out=s2[:], in0=s2[:], in1=ub[:, 0:TT], op=ALU.mult), reads=[s2b, ubb], writes=[s2b])
                fw.op("dve", lambda h, s1=s1, s2=s2, cb=cb: h.tensor_tensor(out=big[:, 2 * MC + cb, :], in0=s1[:], in1=s2[:], op=ALU.add), reads=[s1b, s2b], writes=[b_mx[cb]])
            for cb in range(KC):
                bk, bb = acc(dr["w_out"], cb, KC, lambda k: big[:, 2 * MC + k, :], b_mx)
                fw.op("dve", lambda h, cb=cb, bk=bk: h.tensor_tensor(out=xT[:, cb, :], in0=xT[:, cb, :], in1=bk[:, 0:TT], op=ALU.add), reads=[bb, b_x[cb]], writes=[b_x[cb]])
            if "dbg1" in dr:
                for k in range(KC):
                    fw.dma("sp", "y", lambda h, k=k, ts=ts: h.dma_start(out=dr["dbg1"][k * 128:(k + 1) * 128, ts], in_=xT[:, k, :]), reads=[b_x[k]], writes=[b_y])
                    fw.dma("sp", "y", lambda h, k=k, ts=ts: h.dma_start(out=dr["dbg0"][k * 128:(k + 1) * 128, ts], in_=big[:, 2 * MC + k, :]), reads=[b_mx[k]], writes=[b_y])
            norm_to_h(1)
            for h0 in range(0, HC, HH):
                for j in range(HH):
                    bk, bb = acc(dr["w_mi"], h0 + j, KC, hr, b_h)
                    r, rb = tmp.next()
                    fw.op("act", lambda h, r=r, bk=bk: h.activation(out=r[:], in_=bk[:, 0:TT], func=AF.Relu), reads=[bb], writes=[rb])
                    fw.op("dve", lambda h, r=r, j=j: h.tensor_tensor(out=big[:, j, :], in0=r[:], in1=r[:], op=ALU.mult), reads=[rb], writes=[b_hd[j]])
                for cb in range(KC):
                    bk, bb = acc(dr["w_mo"], cb, HH, lambda k: big[:, k, :], b_hd, k0=h0)
                    fw.op("dve", lambda h, cb=cb, bk=bk: h.tensor_tensor(out=xT[:, cb, :], in0=xT[:, cb, :], in1=bk[:, 0:TT], op=ALU.add), reads=[bb, b_x[cb]], writes=[b_x[cb]])
            if "dbg2" in dr:
                for k in range(KC):
                    fw.dma("sp", "y", lambda h, k=k, ts=ts: h.dma_start(out=dr["dbg2"][k * 128:(k + 1) * 128, ts], in_=xT[:, k, :]), reads=[b_x[k]], writes=[b_y])
            norm_to_h(2)
            for cb in range(KC):
                g_, gb_ = acc(dr["w_pg"], cb, KC, hr, b_h)
                p_, pb_ = acc(dr["w_pp"], cb, PC, lambda k: pTb[:, k, :], [b_p] * PC)
                s1, s1b = tmp.next()
                y, yb = yt.next()
                fw.op("act", lambda h, s1=s1, g_=g_: h.activation(out=s1[:], in_=g_[:, 0:TT], func=AF.Sigmoid), reads=[gb_], writes=[s1b])
                fw.op("dve", lambda h, s1=s1, p_=p_: h.tensor_tensor(out=s1[:], in0=s1[:], in1=p_[:, 0:TT], op=ALU.mult), reads=[s1b, pb_], writes=[s1b])
                fw.op("dve", lambda h, s1=s1, y=y, cb=cb: h.tensor_tensor(out=y[:], in0=s1[:], in1=xT[:, cb, :], op=ALU.add), reads=[s1b, b_x[cb]], writes=[yb])
                fw.dma("sp", "y", lambda h, y=y, cb=cb, ts=ts: h.dma_start(out=dr["yT"][cb * 128:(cb + 1) * 128, ts], in_=y[:]), reads=[yb], writes=[b_y])
        fw.flush(barrier=True)
    return b_y


def dram_decls(nc, cfg, debug_feed_mix=False, debug_out=False):
    D, S, TO, KC, HID, MW, PLE = cfg.D, cfg.S, cfg.TO, cfg.KC, cfg.HID, cfg.MW, cfg.PLE
    NCP, NSB = S // 16, S // 64
    ext = lambda n, s, d=F32: nc.dram_tensor(n, list(s), d, kind="ExternalInput").ap()
    scr = lambda n, s, d=BF16: nc.dram_tensor(n, list(s), d, kind="Internal").ap()
    dr = {"xTpad": ext("xTpad", [D, S]), "gmix_col": ext("gmix_col", [128, KC]), "pT": ext("pT", [PLE, TO]), "gcols": ext("gcols", [128, 3, KC])}
    for n_ in ("w_kc", "w_vc", "w_ks", "w_kw"):
        dr[n_] = ext(n_, [4, 128, KC, 128])
    for n_ in ("w_q", "w_gq", "w_gk", "w_gv", "w_gz"):
        dr[n_] = ext(n_, [16, 128, KC, 128])
    dr["w_v_t"] = ext("w_v_t", [128, KC, 1024]); dr["w_gate_t"] = ext("w_gate_t", [128, KC, 48])
    dr["w_bd"] = ext("w_bd", [128, KC, 32])
    dr["kgain_cols"] = ext("kgain_cols", [128, 3]); dr["kcg_bc"] = ext("kcg_bc", [32, 128]); dr["qgain_col"] = ext("qgain_col", [128, 1])
    dr["w1k"] = ext("w1k", [128, 32, 512]); dr["w1v"] = ext("w1v", [128, 32, 512]); dr["w2k"] = ext("w2k", [128, 4, 128]); dr["w2v"] = ext("w2v", [128, 4, 128])
    dr["pek"] = ext("pek", [128, 32]); dr["pev"] = ext("pev", [128, 32])
    for n_, s_ in (("cosk", [128, S]), ("sink", [128, S]), ("cosc", [NCP, 64]), ("sinc", [NCP, 64]), ("cthr_row", [128, NCP]), ("cthr_col", [128, max(NCP // 128, 1)]),
                   ("tq_row", [128, TO]), ("tq_col", [128, TO // 128]), ("j_row", [128, NSB]), ("curp_col", [128, TO // 128]), ("validj_row", [128, NSB]), ("e0_row", [128, NSB]),
                   ("kvalid_col", [128, S // 128]), ("ebig", [NSB, S]), ("caus", [128, 4, 512]), ("band", [128, 8, 512]), ("prot", [128, 128]), ("ident128", [128, 128]),
                   ("gcst", [64, 5, 64]), ("cw", [128, 16, 3, 4]), ("adt", [64, 2, 16]), ("ogain", [64, 128])):
        dr[n_] = ext(n_, s_)
    dr["KS"] = scr("KS", [4, 128, S]); dr["KW"] = scr("KW", [4, 128, S])
    dr["VS"] = scr("VS", [4, 128, S // 128, 132]); dr["VW"] = scr("VW", [4, 128, S // 128, 132])
    dr["KC"] = scr("KC", [4, 128, NCP]); dr["VC"] = scr("VC", [4, 128, max(NCP // 128, 1), 132])
    for n_ in ("w_ga", "w_gb", "w_out", "w_pg"):
        dr[n_] = ext(n_, [KC, 128, KC, 128])
    dr["w_upa"] = ext("w_upa", [KC, 128, MW // 128, 128]); dr["w_upb"] = ext("w_upb", [KC, 128, MW // 128, 128])
    dr["w_mi"] = ext("w_mi", [HID // 128, 128, KC, 128]); dr["w_mo"] = ext("w_mo", [KC, 128, HID // 128, 128]); dr["w_pp"] = ext("w_pp", [KC, 128, PLE // 128, 128])
    dr["yT"] = nc.dram_tensor("yT", [D, TO], F32, kind="ExternalOutput").ap()
    if debug_feed_mix:
        dr["oaT"] = ext("oaT", [MW, TO], BF16); dr["obT"] = ext("obT", [MW, TO], BF16)
    else:
        mk = (lambda n, s_: nc.dram_tensor(n, list(s_), BF16, kind="ExternalOutput").ap()) if debug_out else scr
        dr["oaT"] = mk("oaT", [MW, TO]); dr["obT"] = mk("obT", [MW, TO])
        if debug_out:
            dr["dbg0"] = nc.dram_tensor("dbg0", [D, TO], BF16, kind="ExternalOutput").ap()
            dr["dbg1"] = nc.dram_tensor("dbg1", [D, TO], F32, kind="ExternalOutput").ap()
            dr["dbg2"] = nc.dram_tensor("dbg2", [D, TO], F32, kind="ExternalOutput").ap()
    return dr


def build_program(cfg, debug_feed_mix=False, debug_out=False):
    nc = bass.Bass("TRN2", target_bir_lowering=False)
    dr = dram_decls(nc, cfg, debug_feed_mix, debug_out)
    with ExitStack() as es:
        fw = FW(nc, es)
        if not debug_feed_mix:
            dr["_b_scr"] = emit_nsa_k(nc, fw, cfg, dr)
            emit_nsa_q(nc, fw, cfg, dr)
            emit_gdn(nc, fw, cfg, dr)
        emit_tail(nc, fw, cfg, dr)
    return nc


def tail_inputs(cfg, inp, c):
    D, TO, G = cfg.D, cfg.TO, cfg.G
    b, g = c // G, c % G
    t0 = g * TO
    off = 13392
    w_in = inp["w_in"][0]
    return {
        "pT": np.ascontiguousarray(inp["p"][0, b, t0:t0 + TO].T),
        "gcols": np.ascontiguousarray(np.stack([col_vec(inp["g_mix"][0]), col_vec(inp["g_mlp"][0]), col_vec(inp["g_ple"][0])], axis=1)),
        "w_ga": blk_w(w_in[:, off:off + D]), "w_gb": blk_w(w_in[:, off + D:off + 2 * D]),
        "w_upa": blk_w(inp["w_up_nsa"][0]), "w_upb": blk_w(inp["w_up_gdn"][0]), "w_out": blk_w(inp["w_out"][0]),
        "w_mi": blk_w(inp["w_mlp_in"][0]), "w_mo": blk_w(inp["w_mlp_out"][0]),
        "w_pg": blk_w(inp["w_ple_gate"][0]), "w_pp": blk_w(inp["w_ple_proj"][0]),
    }


def core_inputs(cfg, inp, c):
    m = {}
    m.update(nsa_inputs(cfg, inp, c))
    m.update(gdn_inputs(cfg, inp, c))
    m.update(tail_inputs(cfg, inp, c))
    return m


def percore_inputs(cfg, inp, c):
    b, g = c // cfg.G, c % cfg.G
    n = (g + 1) * cfg.TO
    xp = np.zeros((cfg.D, cfg.S), np.float32)
    xp[:, cfg.S - n:] = inp["x"][b, :n].T
    m = {"xTpad": xp, "pT": np.ascontiguousarray(inp["p"][0, b, g * cfg.TO:(g + 1) * cfg.TO].T)}
    m.update(nsa_tables(cfg, g))
    return m


_PROG = {}


def run_cfg(cfg, inputs):
    key = (cfg.D, cfg.S, cfg.NB, cfg.HID)
    if key not in _PROG:
        _PROG[key] = build_program(cfg)
    nc = _PROG[key]
    ncores = cfg.NB * cfg.G
    inp = {k_: np.asarray(v) for k_, v in inputs.items()}
    base = core_inputs(cfg, inp, 0)
    maps = [base]
    for c in range(1, ncores):
        m = dict(base)
        m.update(percore_inputs(cfg, inp, c))
        maps.append(m)
    res = run_bass_kernel_spmd(nc, maps, core_ids=list(range(ncores)))
    out = np.empty((cfg.NB, cfg.S, cfg.D), np.float32)
    for c in range(ncores):
        b, g_ = c // cfg.G, c % cfg.G
        out[b, g_ * cfg.TO:(g_ + 1) * cfg.TO] = np.asarray(res.results[c]["yT"]).T
    return out


def kernel(**inputs):
    return run_cfg(Cfg(), inputs)
```
